# Optimizing a Trainium2 kernel written in Bass

```python
import jax, jax.numpy as jnp
from jax import lax
import numpy as np

D_MODEL = 1024
BATCH = 4
SEQ = 4096
DEPTH = 4

GRID_W = 64
CTX_LEN = 256
N_MIXERS = 2
N_A = (DEPTH + 1) // 2
N_B = DEPTH // 2
MLA_HEADS = 8
QK_NOPE = 128
QK_ROPE = 64
V_DIM = 128
QK_DIM = QK_NOPE + QK_ROPE
Q_LORA = 384
KV_LORA = 256
ROPE_THETA = 10000.0
Q_BLOCK = 128
GM_CHUNK = 128
GM_GROUPS = 8
GM_HALF = 2 * D_MODEL
GM_GROUP_DIM = GM_HALF // GM_GROUPS
FFN_HIDDEN = 4 * D_MODEL
N_MOD = 6
EPS = 1e-6

kernel_name = "hybrid_mla_gmlp_prefix_dit"


def rmsnorm(x, g):
    xf = x.astype(jnp.float32)
    y = xf * lax.rsqrt(jnp.mean(xf * xf, axis=-1, keepdims=True) + EPS)
    return (y * g.astype(jnp.float32)).astype(x.dtype)


def layernorm(x, g, b):
    xf = x.astype(jnp.float32)
    mu = jnp.mean(xf, axis=-1, keepdims=True)
    var = jnp.mean(jnp.square(xf - mu), axis=-1, keepdims=True)
    y = (xf - mu) * lax.rsqrt(var + EPS)
    return (y * g.astype(jnp.float32) + b.astype(jnp.float32)).astype(x.dtype)


def modulate(x, g, shift, scale):
    return rmsnorm(x, g) * (1.0 + scale) + shift


def axial_rope_tables(n_tokens, dtype):
    rows = n_tokens // GRID_W
    row = jnp.repeat(jnp.arange(rows, dtype=jnp.float32), GRID_W)
    col = jnp.tile(jnp.arange(GRID_W, dtype=jnp.float32), rows)
    half = QK_ROPE // 2
    inv = ROPE_THETA ** (-jnp.arange(0, half, 2, dtype=jnp.float32) / half)
    ang_r = row[:, None] * inv[None, :]
    ang_c = col[:, None] * inv[None, :]
    ang = jnp.concatenate([ang_r, ang_r, ang_c, ang_c], axis=-1)
    return jnp.cos(ang).astype(dtype), jnp.sin(ang).astype(dtype)


def rotate_axial(x):
    a1, a2, b1, b2 = jnp.split(x, 4, axis=-1)
    return jnp.concatenate([-a2, a1, -b2, b1], axis=-1)


def rope_part(t, cos, sin):
    nope, pe = t[..., :QK_NOPE], t[..., QK_NOPE:]
    c = cos[None, :, None, :]
    s = sin[None, :, None, :]
    return jnp.concatenate([nope, pe * c + rotate_axial(pe) * s], axis=-1)


def block_attention(q, k, v, scale):
    B, Sq, H, Dk = q.shape
    nb = Sq // Q_BLOCK
    qb = q.reshape(B, nb, Q_BLOCK, H, Dk).transpose(1, 0, 2, 3, 4)

    def one(qi):
        s = jnp.einsum('bqhd,bkhd->bhqk', qi, k, preferred_element_type=jnp.float32) * scale
        p = jax.nn.softmax(s, axis=-1)
        return jnp.einsum('bhqk,bkhd->bqhd', p.astype(v.dtype), v)

    o = lax.map(one, qb)
    return o.transpose(1, 0, 2, 3, 4).reshape(B, Sq, H, v.shape[-1])


def mla_queries(h, wq_a, q_a_norm, wq_b, q_norm):
    B, S, _ = h.shape
    cq = rmsnorm(h @ wq_a, q_a_norm)
    q = (cq @ wq_b).reshape(B, S, MLA_HEADS, QK_DIM)
    return rmsnorm(q, q_norm)


def mla_keys_values(h, wkv_a, kv_a_norm, wkv_b, k_norm):
    B, S, _ = h.shape
    kv_a = h @ wkv_a
    ckv, k_pe = kv_a[..., :KV_LORA], kv_a[..., KV_LORA:]
    kv = (rmsnorm(ckv, kv_a_norm) @ wkv_b).reshape(B, S, MLA_HEADS, QK_NOPE + V_DIM)
    k_nope, v = kv[..., :QK_NOPE], kv[..., QK_NOPE:]
    k_pe = jnp.broadcast_to(k_pe[:, :, None, :], (B, S, MLA_HEADS, QK_ROPE))
    k = rmsnorm(jnp.concatenate([k_nope, k_pe], axis=-1), k_norm)
    return k, v


def mla_mixer(h_lat, h_ctx, cos, sin, wq_a, q_a_norm, wq_b, wkv_a, kv_a_norm, wkv_b,
              q_norm, k_norm, wo, with_ctx):
    B, S, _ = h_lat.shape
    scale = QK_DIM ** -0.5
    k_lat, v_lat = mla_keys_values(h_lat, wkv_a, kv_a_norm, wkv_b, k_norm)
    k_ctx, v_ctx = mla_keys_values(h_ctx, wkv_a, kv_a_norm, wkv_b, k_norm)
    k_lat = rope_part(k_lat, cos, sin)
    q_lat = rope_part(mla_queries(h_lat, wq_a, q_a_norm, wq_b, q_norm), cos, sin)
    k_all = jnp.concatenate([k_ctx, k_lat], axis=1)
    v_all = jnp.concatenate([v_ctx, v_lat], axis=1)
    o_lat = block_attention(q_lat, k_all, v_all, scale).reshape(B, S, MLA_HEADS * V_DIM) @ wo
    o_ctx = None
    if with_ctx:
        q_ctx = mla_queries(h_ctx, wq_a, q_a_norm, wq_b, q_norm)
        o_ctx = block_attention(q_ctx, k_ctx, v_ctx, scale).reshape(
            B, h_ctx.shape[1], MLA_HEADS * V_DIM) @ wo
    return o_lat, o_ctx


def gmlp_mixer(h, w_in, ln_g, ln_b, ws, bs, w_out):
    B, S, _ = h.shape
    z = jax.nn.gelu(h @ w_in, approximate=False)
    u, v = z[..., :GM_HALF], z[..., GM_HALF:]
    v = layernorm(v, ln_g, ln_b)
    vc = v.reshape(B, S // GM_CHUNK, GM_CHUNK, GM_GROUPS, GM_GROUP_DIM)
    mixed = jnp.einsum('gpq,bnqgc->bnpgc', ws, vc) + bs.T[None, None, :, :, None]
    return (u * mixed.reshape(B, S, GM_HALF)) @ w_out


def squared_relu_mlp(h, w1, w2):
    return jnp.square(jax.nn.relu(h @ w1)) @ w2


def setup_inputs(seed: int = 0) -> dict:
    key = jax.random.key(seed)
    ks = jax.random.split(key, 32)
    D = D_MODEL

    def nrm(k, shape, scale):
        return jax.random.normal(k, shape, jnp.float32) * scale

    def gain(k, shape):
        return 1.0 + 0.05 * jax.random.normal(k, shape, jnp.float32)

    return {
        "x": nrm(ks[0], (BATCH, SEQ, D), 1.0),
        "c": nrm(ks[1], (BATCH, D), 1.0),
        "ctx": nrm(ks[2], (BATCH, CTX_LEN, D), 1.0),
        "c_ctx": nrm(ks[3], (D,), 1.0),
        "ada_w": nrm(ks[4], (DEPTH, D, N_MOD * D), 0.5 * D ** -0.5),
        "ada_b": nrm(ks[5], (DEPTH, N_MOD * D), 0.01),
        "norm_mix_g": gain(ks[6], (DEPTH, D)),
        "norm_ffn_g": gain(ks[7], (DEPTH, D)),
        "mla_wq_a": nrm(ks[8], (N_A, D, Q_LORA), D ** -0.5),
        "mla_q_a_norm": gain(ks[9], (N_A, Q_LORA)),
        "mla_wq_b": nrm(ks[10], (N_A, Q_LORA, MLA_HEADS * QK_DIM), Q_LORA ** -0.5),
        "mla_wkv_a": nrm(ks[11], (N_A, D, KV_LORA + QK_ROPE), D ** -0.5),
        "mla_kv_a_norm": gain(ks[12], (N_A, KV_LORA)),
        "mla_wkv_b": nrm(ks[13], (N_A, KV_LORA, MLA_HEADS * (QK_NOPE + V_DIM)), KV_LORA ** -0.5),
        "mla_q_norm": gain(ks[14], (N_A, QK_DIM)),
        "mla_k_norm": gain(ks[15], (N_A, QK_DIM)),
        "mla_wo": nrm(ks[16], (N_A, MLA_HEADS * V_DIM, D), (MLA_HEADS * V_DIM) ** -0.5),
        "gm_w_in": nrm(ks[17], (N_B, D, 2 * GM_HALF), D ** -0.5),
        "gm_ln_g": gain(ks[18], (N_B, GM_HALF)),
        "gm_ln_b": nrm(ks[19], (N_B, GM_HALF), 0.02),
        "gm_ws": nrm(ks[20], (N_B, GM_GROUPS, GM_CHUNK, GM_CHUNK), GM_CHUNK ** -0.5),
        "gm_bs": gain(ks[21], (N_B, GM_GROUPS, GM_CHUNK)),
        "gm_w_out": nrm(ks[22], (N_B, GM_HALF, D), GM_HALF ** -0.5),
        "ffn_w1": nrm(ks[23], (DEPTH, D, FFN_HIDDEN), D ** -0.5),
        "ffn_w2": nrm(ks[24], (DEPTH, FFN_HIDDEN, D), FFN_HIDDEN ** -0.5),
    }


def reference(x, c, ctx, c_ctx, ada_w, ada_b, norm_mix_g, norm_ffn_g,
              mla_wq_a, mla_q_a_norm, mla_wq_b, mla_wkv_a, mla_kv_a_norm, mla_wkv_b,
              mla_q_norm, mla_k_norm, mla_wo,
              gm_w_in, gm_ln_g, gm_ln_b, gm_ws, gm_bs, gm_w_out,
              ffn_w1, ffn_w2):
    S = x.shape[1]
    cos, sin = axial_rope_tables(S, x.dtype)
    silu_c = jax.nn.silu(c)
    silu_cc = jax.nn.silu(c_ctx)
    y = ctx
    for i in range(DEPTH):
        with_ctx = i < DEPTH - 1
        use_mla = (i % N_MIXERS) == 0
        j = i // N_MIXERS
        m_lat = (silu_c @ ada_w[i] + ada_b[i])[:, None, :]
        m_ctx = silu_cc @ ada_w[i] + ada_b[i]
        sh_m, sc_m, g_m, sh_f, sc_f, g_f = jnp.split(m_lat, N_MOD, axis=-1)
        csh_m, csc_m, cg_m, csh_f, csc_f, cg_f = jnp.split(m_ctx, N_MOD, axis=-1)

        h_lat = modulate(x, norm_mix_g[i], sh_m, sc_m)
        if use_mla:
            h_ctx = modulate(y, norm_mix_g[i], csh_m, csc_m)
            o_lat, o_ctx = mla_mixer(h_lat, h_ctx, cos, sin, mla_wq_a[j], mla_q_a_norm[j],
                                     mla_wq_b[j], mla_wkv_a[j], mla_kv_a_norm[j], mla_wkv_b[j],
                                     mla_q_norm[j], mla_k_norm[j], mla_wo[j], with_ctx)
        else:
            o_lat = gmlp_mixer(h_lat, gm_w_in[j], gm_ln_g[j], gm_ln_b[j], gm_ws[j], gm_bs[j],
                               gm_w_out[j])
            o_ctx = None
            if with_ctx:
                h_ctx = modulate(y, norm_mix_g[i], csh_m, csc_m)
                o_ctx = gmlp_mixer(h_ctx, gm_w_in[j], gm_ln_g[j], gm_ln_b[j], gm_ws[j], gm_bs[j],
                                   gm_w_out[j])
        x = x + g_m * o_lat
        x = x + g_f * squared_relu_mlp(modulate(x, norm_ffn_g[i], sh_f, sc_f), ffn_w1[i], ffn_w2[i])
        if with_ctx:
            y = y + cg_m * o_ctx
            y = y + cg_f * squared_relu_mlp(modulate(y, norm_ffn_g[i], csh_f, csc_f),
                                            ffn_w1[i], ffn_w2[i])
    return x
```

```python
import numpy as np
from contextlib import ExitStack
import concourse.bass as bass
import concourse.mybir as mybir
from concourse.bass_utils import run_bass_kernel_spmd

F32 = mybir.dt.float32
BF16 = mybir.dt.bfloat16
AF = mybir.ActivationFunctionType
ALU = mybir.AluOpType

D = 1024
KC = 8
NT = 2048
NCTX = 256
NKEY = 2 * NT + NCTX
HEADS = 8
EPS = 1e-6
SCALE = 192 ** -0.5
ARENA_WORDS = 53000

ENGS = ("pe", "act", "dve", "pool", "sp")
import os as _os
F_LN = _os.environ.get("F_LN", "1") == "1"
F_PAD = _os.environ.get("F_PAD", "1") == "1"
F_SUMS = _os.environ.get("F_SUMS", "1") == "1"


class Res:
    __slots__ = ("name", "w", "r", "rd")

    def __init__(self, name):
        self.name = name
        self.w = None
        self.r = {}
        self.rd = []


class Op:
    __slots__ = ("eng", "fn", "deps", "lane", "ndma", "signal", "sigkey", "sigval", "waits", "known")

    def __init__(self, eng, fn, lane, ndma):
        self.eng = eng
        self.fn = fn
        self.lane = lane
        self.ndma = ndma
        self.signal = False
        self.sigkey = None
        self.sigval = 0
        self.waits = ()
        self.known = None
        self.deps = ()


class Sched:
    def __init__(self):
        self.ops = []
        self.res = []
        self.lanes = {}
        self.pending = {e: None for e in ENGS}
        self.last = {e: None for e in ENGS}
        self.lane_last = {}

    def R(self, name):
        r = Res(name)
        self.res.append(r)
        return r

    def add(self, eng, fn, reads=(), writes=(), lane=None, ndma=0):
        op = Op(eng, fn, lane, ndma)
        deps = set()
        for r in reads:
            if r.w is not None:
                deps.add(r.w)
        for w in writes:
            if w.w is not None:
                deps.add(w.w)
            deps.update(w.r.values())
            deps.update(w.rd)
        if self.pending[eng] is not None:
            deps.update(self.pending[eng])
            self.pending[eng] = None
        dl = []
        for d in deps:
            if d.eng == "pe" and eng == "pe" and d.lane is None and lane is None:
                continue
            d.signal = True
            dl.append(d)
        op.deps = dl
        for r in reads:
            if lane is not None:
                r.rd.append(op)
            else:
                r.r[eng] = op
        for w in writes:
            w.w = op
            w.r = {}
            w.rd = []
        self.ops.append(op)
        if lane is not None:
            if lane in self.lanes:
                assert self.lanes[lane] == eng, "one issuing engine per lane"
            self.lanes[lane] = eng
            self.lane_last[lane] = op
        else:
            self.last[eng] = op
        return op

    def barrier(self):
        outstanding = [o for o in self.last.values() if o is not None]
        outstanding += list(self.lane_last.values())
        for o in outstanding:
            o.signal = True
        for e in ENGS:
            cur = self.pending[e]
            self.pending[e] = list(outstanding) + (cur if cur else [])
        for r in self.res:
            r.w = None
            r.r = {}
            r.rd = []

    def fence(self, eng="sp"):
        self.barrier()
        self.add(eng, None)

    def finalize(self):
        ms = {e: 0 for e in ENGS}
        lc = {l: 0 for l in self.lanes}
        for op in self.ops:
            if op.lane is not None:
                lc[op.lane] += 16 * op.ndma
                op.sigkey = ("l", op.lane)
                op.sigval = lc[op.lane]
            elif op.signal and op.fn is not None:
                ms[op.eng] += 1
                op.sigkey = ("e", op.eng)
                op.sigval = ms[op.eng]
        seen = {e: {} for e in ENGS}
        for op in self.ops:
            s = seen[op.eng]
            waits = {}
            for d in op.deps:
                if d.sigkey is None:
                    continue
                if s.get(d.sigkey, 0) >= d.sigval:
                    continue
                if waits.get(d.sigkey, 0) < d.sigval:
                    waits[d.sigkey] = d.sigval
            for d in op.deps:
                if d.sigkey in waits and d.known is not None:
                    for k, v in d.known.items():
                        if s.get(k, 0) < v:
                            s[k] = v
            for k, v in waits.items():
                if s.get(k, 0) < v:
                    s[k] = v
            op.waits = tuple(waits.items())
            if op.sigkey is not None:
                op.known = dict(s)
        self.stats = (dict(ms), dict(lc))


class Prog:
    def __init__(self, layers, first, last, debug_stop=None):
        self.layers = layers
        self.debug_stop = debug_stop
        self.debug = debug_stop is not None
        self.dbg_names = []
        nc = bass.Bass("TRN2", target_bir_lowering=False)
        self.nc = nc
        self.S = Sched()
        self.es = ExitStack()
        self._declare_dram()
        with self.es:
            self._alloc()
            self._build()
            self.S.finalize()
            self._emit()

    def _declare_dram(self):
        nc = self.nc
        din = lambda n, s: nc.dram_tensor(n, s, F32, kind="ExternalInput").ap()
        self.d_xown = din("x_own", [NT, D])
        self.d_xpar = din("x_par", [NT, D])
        self.d_ctx = din("ctx", [NCTX, D])
        self.d_vecs = din("vecs", [128, NV])
        self.d_ident = din("ident", [128, 128])
        self.d_rope = din("rope", [64, 2, NKEY])
        self.d_bsb = din("bsb", [2, 128, 1024])
        self.d_wsT = din("wsT", [2, 128, 1024])
        self.d_adaw = din("ada_w", [4, D, 6 * D])
        self.d_wqa = din("wq_a", [2, D, 384])
        self.d_wkva = din("wkv_a", [2, D, 384])
        self.d_wqb = din("wq_b", [2, 384, 2048])
        self.d_wkvb = din("wkv_b", [2, 256, 2048])
        self.d_wo = din("wo", [2, D, D])
        self.d_win = din("gm_w_in", [2, D, 4 * D])
        self.d_wout = din("gm_w_out", [2, 2 * D, D])
        self.d_w1 = din("ffn_w1", [4, D, 4 * D])
        self.d_w2 = din("ffn_w2", [4, 4 * D, D])
        self.d_out = nc.dram_tensor("out_x", [NT, D], F32, kind="ExternalOutput").ap()
        self.d_outy = nc.dram_tensor("out_y", [NCTX, D], F32, kind="ExternalOutput").ap()

    def _alloc(self):
        nc, es, S = self.nc, self.es, self.S
        self.arena = es.enter_context(nc.sbuf_tensor("arena", [128, ARENA_WORDS], F32))
        self.arena_b = self.arena.bitcast(BF16)
        self.top = 0
        self.banks = []
        for i in range(8):
            t = es.enter_context(nc.psum_tensor(f"pb{i}", [128, 512], F32))
            self.banks.append((t, S.R(f"pb{i}")))
        self.XT = self.f32(KC * NT).rearrange("p (k n) -> p k n", k=KC)
        self.XTr = [S.R(f"XT{g}") for g in range(4)]
        self.YT = self.f32(KC * NCTX).rearrange("p (k n) -> p k n", k=KC)
        self.YTr = S.R("YT")
        self.vecs = self.f32(NV)
        self.vecs_r = S.R("vecs")
        self.ident = self.f32(128)
        self.ident_r = S.R("ident")
        self.ones = self.bf(128)
        self.ones_r = S.R("ones")
        self.mod = self.f32(96)
        self.am = self.f32(16)
        self.af = self.f32(16)
        self.mod_r = S.R("mod")
        self.scb = self.bf(16)
        self.scb_r = S.R("scb")
        self.persist_top = self.top

    def f32(self, n, parts=None):
        a = self.arena[:, self.top:self.top + n]
        self.top += n
        assert self.top <= ARENA_WORDS, f"SBUF overflow {self.top}"
        return a

    def bf(self, n):
        w = (n + 1) // 2
        a = self.arena_b[:, 2 * self.top:2 * self.top + n]
        self.top += w
        assert self.top <= ARENA_WORDS, f"SBUF overflow {self.top}"
        return a

    def phase(self):
        self.S.barrier()
        self.top = self.persist_top

    def mm(self, out, lhsT, rhs, start, stop, reads, writes):
        self.S.add("pe", lambda e: e.matmul(out, lhsT=lhsT, rhs=rhs, start=start, stop=stop), reads, writes)

    def tr(self, out, in_, reads, writes):
        ident = self.ident
        self.S.add("pe", lambda e: e.transpose(out, in_, ident), list(reads) + [self.ident_r], writes)

    def act(self, out, in_, func, reads, writes, scale=None, bias=None):
        kw = {}
        if scale is not None:
            kw["scale"] = scale
        if bias is not None:
            kw["bias"] = bias
        self.S.add("act", lambda e: e.activation(out=out, in_=in_, func=func, **kw), reads, writes)

    def stt(self, out, in0, scalar, in1, op0, op1, reads, writes, eng="dve"):
        self.S.add(eng, lambda e: e.scalar_tensor_tensor(out=out, in0=in0, scalar=scalar, in1=in1, op0=op0, op1=op1),
                   reads, writes)

    def tt(self, out, in0, in1, op, reads, writes, eng="dve"):
        self.S.add(eng, lambda e: e.tensor_tensor(out=out, in0=in0, in1=in1, op=op), reads, writes)

    def ts(self, out, in0, s1, s2, op0, op1, reads, writes, eng="dve"):
        self.S.add(eng, lambda e: e.tensor_scalar(out=out, in0=in0, scalar1=s1, scalar2=s2, op0=op0, op1=op1),
                   reads, writes)

    def recip(self, out, in_, reads, writes):
        self.S.add("dve", lambda e: e.reciprocal(out=out, in_=in_), reads, writes)

    def copy(self, eng, out, in_, reads, writes):
        if eng == "act":
            self.S.add("act", lambda e: e.copy(out=out, in_=in_), reads, writes)
        else:
            self.S.add(eng, lambda e: e.tensor_copy(out=out, in_=in_), reads, writes)

    def dma(self, eng, out, in_, reads, writes, lane):
        def fn(e, sem):
            e.dma_start(out=out, in_=in_).then_inc(sem, 16)
        self.S.add(eng, fn, reads, writes, lane=lane, ndma=1)

    def dump(self, name, ap, reads, dtype=F32):
        if not getattr(self, "debug", False):
            return
        shape = list(ap.shape)
        d = self.nc.dram_tensor("dbg_" + name, shape, dtype, kind="ExternalOutput").ap()
        self.dma("sp", d, ap, reads, [], lane="dbg_" + name)
        self.dbg_names.append("dbg_" + name)

    def vcol(self, name, j=0):
        o = VOFF[name] + j
        return self.vecs[:, o:o + 1]

    def vslice(self, name, n):
        o = VOFF[name]
        return self.vecs[:, o:o + n]

    def bank(self, i):
        return self.banks[i]

    class WStream:
        def __init__(self, prog, name, slots, specs):
            self.p = prog
            self.slots = slots
            self.specs = specs
            self.issued = 0
            self.name = name

        def get(self, n):
            ns = len(self.slots)
            while self.issued < len(self.specs) and self.issued <= n + ns - 1:
                m = self.issued
                ap, res = self.slots[m % ns]
                src, view = self.specs[m]
                self.p.dma("pool", view(ap), src, [], [res], lane=f"{self.name}{m % ns}")
                self.issued += 1
            ap, res = self.slots[n % ns]
            return self.specs[n][1](ap), res

    def rsqrt_b(self, ps, N, inv_n, rs, rs_r, ps_r, parts=128):
        if F_LN:
            self.act(rs[:parts, :N], ps[:parts, :N], AF.Ln, [ps_r, self.eps_r], [rs_r], scale=inv_n, bias=self.eps_col[:parts, :])
            self.act(rs[:parts, :N], rs[:parts, :N], AF.Exp, [rs_r], [rs_r], scale=-0.5)
        else:
            self.act(rs[:parts, :N], ps[:parts, :N], AF.Sqrt, [ps_r, self.eps_r], [rs_r], scale=inv_n, bias=self.eps_col[:parts, :])
            self.recip(rs[:parts, :N], rs[:parts, :N], [rs_r], [rs_r])

    def modulate(self, src, src_r, N, which, kind, hT, hT_r, coff):
        self._modulate_multi(src, [src_r], N, which, kind, hT, hT_r, coff)

    def linear(self, ps, ps_r, N, w, w_r, hT, hT_r, coff, nk=KC, parts=128):
        for k in range(nk):
            self.mm(ps[:parts, :N], w(k), hT[:, k, coff:coff + N], k == 0, k == nk - 1, [w_r, hT_r], [ps_r])

    def load_tokens(self, dram_rows, ntiles, dest, dest_r):
        for h in range(ntiles // 2):
            st, st_r = self.stage[h % 2]
            src = dram_rows[h * 256:(h + 1) * 256, :].rearrange("(t p) d -> p t d", p=128)
            self.dma("sp", st, src, [], [st_r], lane=f"stage{h % 2}")
            for k in range(KC):
                pb, pb_r = self.bank(self.TRB[k % 2])
                for t in range(2):
                    self.tr(pb[:, t * 128:(t + 1) * 128], st[:, t, k * 128:(k + 1) * 128], [st_r], [pb_r])
                self.copy("act" if k % 2 == 0 else "dve", dest(k, h * 256, 256), pb[:, 0:256], [pb_r], [dest_r(h)])

    def store_tokens(self, src, src_r, ntiles, dram_rows, lane):
        for h in range(ntiles // 2):
            st, st_r = self.stage[h % 2]
            for t in range(2):
                for half in range(2):
                    pb, pb_r = self.bank(self.TRB[(2 * t + half) % 2])
                    for kk in range(4):
                        k = half * 4 + kk
                        self.tr(pb[:, kk * 128:(kk + 1) * 128], src(k, h * 256 + t * 128, 128), [src_r(h)], [pb_r])
                    self.copy("act" if half == 0 else "dve", st[:, t, half * 512:(half + 1) * 512], pb[:, :], [pb_r], [st_r])
            dst = dram_rows[h * 256:(h + 1) * 256, :].rearrange("(t p) d -> p t d", p=128)
            self.dma("sp", dst, st, [st_r], [], lane=f"{lane}{h % 2}")

    def _build(self):
        S = self.S
        self.out_r = S.R("out")
        self.phase()
        self.eps_col = self.f32(1)
        self.eps_r = S.R("eps")
        S.add("dve", lambda e: e.memset(self.eps_col, EPS), [], [self.eps_r])
        self.persist_top = self.top
        S.add("dve", lambda e: e.memset(self.ones, 1.0), [], [self.ones_r])
        self.dma("sp", self.vecs, self.d_vecs, [], [self.vecs_r], lane="vecs")
        self.dma("sp", self.ident, self.d_ident, [], [self.ident_r], lane="ident")
        self.act(self.scb, self.vslice("cvec", 16), AF.Silu, [self.vecs_r], [self.scb_r])
        self.stage = []
        for i in range(2):
            self.stage.append((self.f32(2048).rearrange("p (t d) -> p t d", t=2), S.R(f"stage{i}")))
        self.TRB = [0, 1]
        XT, YT = self.XT, self.YT
        self.load_tokens(self.d_xown, 16, lambda k, off, n: XT[:, k, off:off + n], lambda h: self.XTr[h // 2])
        self.load_tokens(self.d_ctx, 2, lambda k, off, n: YT[:, k, off:off + n], lambda h: self.YTr)

        for li, L in enumerate(self.layers):
            self.build_mod(L)
            if self.debug_stop == ("mod", L):
                break
            if L % 2 == 0:
                self.build_mla(L)
            else:
                self.build_gmlp(L)
            self.dump(f"xt_mix{L}", self.XT, self.XTr)
            self.dump(f"yt_mix{L}", self.YT, [self.YTr])
            if self.debug_stop == ("mix", L):
                break
            self.build_ffn(L)
            self.dump(f"xt_ffn{L}", self.XT, self.XTr)
            self.dump(f"yt_ffn{L}", self.YT, [self.YTr])
            if self.debug_stop == ("ffn", L):
                break

        self.phase()
        self.stage = []
        for i in range(2):
            self.stage.append((self.f32(2048).rearrange("p (t d) -> p t d", t=2), S.R(f"ostage{i}")))
        self.TRB = [0, 1]
        self.store_tokens(lambda k, off, n: XT[:, k, off:off + n], lambda h: self.XTr[h // 2], 16, self.d_out, "ox")
        self.store_tokens(lambda k, off, n: YT[:, k, off:off + n], lambda h: self.YTr, 2, self.d_outy, "oy")
        S.fence("sp")

    def build_mod(self, L):
        S = self.S
        self.phase()
        slots = [(self.bf(4096), S.R(f"adaslot{i}")) for i in range(2)]
        wsrc = self.d_adaw[L].rearrange("(k p) n -> p k n", p=128)
        view = lambda ap: ap.rearrange("p (k n) -> p k n", k=KC)
        ws = Prog.WStream(self, "ada", slots, [(wsrc[:, :, b * 512:(b + 1) * 512], view) for b in range(12)])
        pb, pb_r = self.bank(2)
        for b in range(12):
            w, w_r = ws.get(b)
            for cc in range(4):
                ch = b * 4 + cc
                for k in range(KC):
                    self.mm(pb[:, 2 * ch:2 * ch + 2], w[:, k, cc * 128:(cc + 1) * 128], self.scb[:, 2 * k:2 * k + 2],
                            k == 0, k == KC - 1, [w_r, self.scb_r], [pb_r])
        o = VOFF[f"adab{L}"]
        self.tt(self.mod, pb[:, 0:96], self.vecs[:, o:o + 96], ALU.add, [pb_r, self.vecs_r], [self.mod_r])
        og = VOFF[f"gmix{L}"]
        self.stt(self.am, self.mod[:, 16:32], 1.0, self.vecs[:, og:og + 16], ALU.add, ALU.mult, [self.mod_r, self.vecs_r], [self.mod_r])
        og = VOFF[f"gffn{L}"]
        self.stt(self.af, self.mod[:, 64:80], 1.0, self.vecs[:, og:og + 16], ALU.add, ALU.mult, [self.mod_r, self.vecs_r], [self.mod_r])
        self.dump(f"mod{L}", self.mod, [self.mod_r])

    def gate(self, kind, k, which):
        base = 16 if kind == "m" else 40
        c = 2 * (base + k) + which
        return self.mod[:, c:c + 1]

    def with_ctx(self, L):
        return L < 2

    def build_ffn(self, L):
        S = self.S
        self.phase()
        XT, YT = self.XT, self.YT
        sgs = [[("x", 0), ("x", 1)], [("x", 2), ("x", 3)]]
        if self.with_ctx(L):
            sgs.append([("y", 0)])
        NSG = 1024
        hT = self.bf(KC * NSG).rearrange("p (k n) -> p k n", k=KC)
        hT_r = S.R("hT")
        aT = self.bf(32 * NSG).rearrange("p (j n) -> p j n", j=32)
        aT_r = [S.R(f"aT{j}") for j in range(32)]
        slots = [(self.bf(4096), S.R(f"wslot{i}")) for i in range(4)]
        self.sqs = [(self.bf(512), S.R(f"sq{i}")) for i in range(2)]
        self.tmps = [(self.f32(512), S.R(f"tmp{i}")) for i in range(2)]
        self.rs_buf = (self.f32(512), S.R("rs"))
        rl = [(self.f32(512), S.R(f"relu{i}")) for i in range(2)]
        self.SSB = 7
        MM = [0, 1, 2, 3, 4, 5]
        w1src = self.d_w1[L].rearrange("(k p) n -> p k n", p=128)
        w2src = self.d_w2[L].rearrange("(j p) n -> p j n", p=128)
        v1 = lambda ap: ap.rearrange("p (k n) -> p k n", k=KC)
        v2 = lambda ap: ap.rearrange("p (j n) -> p j n", j=32)
        specs = []
        for sg in sgs:
            specs += [(w1src[:, :, b * 512:(b + 1) * 512], v1) for b in range(8)]
            specs += [(w2src[:, :, dc * 128:(dc + 1) * 128], v2) for dc in range(8)]
        ws = Prog.WStream(self, "ffw", slots, specs)
        bi = 0
        mmi = 0
        for sg in sgs:
            subs = []
            off = 0
            for (kind, g) in sg:
                if kind == "x":
                    N = 512
                    src = (lambda g: (lambda k: XT[:, k, g * 512:(g + 1) * 512]))(g)
                    src_r = self.XTr[g]
                    which = 0
                else:
                    N = NCTX
                    src = lambda k: YT[:, k, :]
                    src_r = self.YTr
                    which = 1
                subs.append((src, src_r, N, which, off))
                off += N
            for (src, src_r, N, which, off) in subs:
                self.modulate(src, src_r, N, which, "f", hT, hT_r, off)
            for b in range(8):
                w, w_r = ws.get(bi)
                bi += 1
                for jj in range(4):
                    j = b * 4 + jj
                    for (src, src_r, N, which, off) in subs:
                        pb, pb_r = self.bank(MM[mmi % len(MM)])
                        mmi += 1
                        self.linear(pb, pb_r, N, (lambda k, w=w, jj=jj: w[:, k, jj * 128:(jj + 1) * 128]), w_r, hT, hT_r, off)
                        r, r_r = rl[mmi % 2]
                        self.act(r[:, :N], pb[:, :N], AF.Relu, [pb_r], [r_r])
                        self.tt(aT[:, j, off:off + N], r[:, :N], r[:, :N], ALU.mult, [r_r], [aT_r[j]])
            for dc in range(8):
                w, w_r = ws.get(bi)
                bi += 1
                for (src, src_r, N, which, off) in subs:
                    pb, pb_r = self.bank(MM[mmi % len(MM)])
                    mmi += 1
                    for j in range(32):
                        self.mm(pb[:, :N], w[:, j, :], aT[:, j, off:off + N], j == 0, j == 31, [w_r, aT_r[j]], [pb_r])
                    self.stt(src(dc), pb[:, :N], self.gate("f", dc, which), src(dc), ALU.mult, ALU.add,
                             [pb_r, self.mod_r, src_r], [src_r])

    def build_gmlp(self, L):
        S = self.S
        j_ = L // 2
        self.phase()
        XT, YT = self.XT, self.YT
        groups = [("x", g) for g in range(4)]
        if self.with_ctx(L):
            groups.append(("y", 0))
        hT = self.bf(KC * 512).rearrange("p (k n) -> p k n", k=KC)
        hT_r = S.R("hT")
        uT = self.bf(16 * 512).rearrange("p (j n) -> p j n", j=16)
        uT_r = [S.R(f"uT{j}") for j in range(16)]
        gT, gT_r = uT, uT_r
        vt = [(self.f32(2048), S.R(f"v{t}")) for t in range(4)]
        vn = [(self.bf(2048), S.R(f"vn{i}")) for i in range(4)]
        slots = [(self.bf(4096), S.R(f"wslot{i}")) for i in range(4)]
        wsT = self.bf(1024).rearrange("p (g n) -> p g n", g=8)
        wsT_r = S.R("wsT")
        bsb = self.f32(1024).rearrange("p (g n) -> p g n", g=8)
        bsb_r = S.R("bsb")
        BIAS = self.f32(2048).rearrange("p (c n) -> p c n", c=16)
        BIAS_r = S.R("BIAS")
        self.sqs = [(self.bf(512), S.R(f"sq{i}")) for i in range(2)]
        self.tmps = [(self.f32(512), S.R(f"tmp{i}")) for i in range(2)]
        self.rs_buf = (self.f32(512), S.R("rs"))
        stats = self.f32(112)
        stats_r = [S.R(f"stats{t}") for t in range(4)]
        sc_r = [S.R(f"sc{t}") for t in range(4)]
        mv = self.f32(8)
        mv_r = [S.R(f"mv{t}") for t in range(4)]
        self.SSB = 7
        MM = [0, 1, 2, 3, 4, 5]
        mmi = 0
        self.dma("pool", wsT, self.d_wsT[j_].rearrange("p (g n) -> p g n", g=8), [], [wsT_r], lane="wsT")
        self.dma("sp", bsb, self.d_bsb[j_].rearrange("p (g n) -> p g n", g=8), [], [bsb_r], lane="bsb")
        pb, pb_r = self.bank(6)
        for g in range(4):
            self.mm(pb[:, g * 128:(g + 1) * 128], self.ones, wsT[:, g, :], True, True, [self.ones_r, wsT_r], [pb_r])
        pb2, pb2_r = self.bank(5)
        for g in range(4):
            self.mm(pb2[:, g * 128:(g + 1) * 128], self.ones, wsT[:, 4 + g, :], True, True, [self.ones_r, wsT_r], [pb2_r])
        for c in range(16):
            g = c // 2
            src = (pb if g < 4 else pb2)[:, (g % 4) * 128:(g % 4 + 1) * 128]
            self.stt(BIAS[:, c, :], src, self.vcol(f"lnb{j_}", c), bsb[:, g, :], ALU.mult, ALU.add,
                     [pb_r, pb2_r, self.vecs_r, bsb_r], [BIAS_r])
        winsrc = self.d_win[j_].rearrange("(k p) n -> p k n", p=128)
        woutsrc = self.d_wout[j_].rearrange("(c p) n -> p c n", p=128)
        v1 = lambda ap: ap.rearrange("p (k n) -> p k n", k=KC)
        v3 = lambda ap: ap[:, 0:2048].rearrange("p (c n) -> p c n", c=16)
        specs = []
        for _ in groups:
            specs += [(winsrc[:, :, b * 512:(b + 1) * 512], v1) for b in range(8)]
            specs += [(woutsrc[:, :, dc * 128:(dc + 1) * 128], v3) for dc in range(8)]
        ws = Prog.WStream(self, "gmw", slots, specs)
        bi = 0
        for (kind, g) in groups:
            if kind == "x":
                N = 512
                src = (lambda g: (lambda k: XT[:, k, g * 512:(g + 1) * 512]))(g)
                src_r = self.XTr[g]
                which = 0
            else:
                N = NCTX
                src = lambda k: YT[:, k, :]
                src_r = self.YTr
                which = 1
            ntile = N // 128
            self.modulate(src, src_r, N, which, "m", hT, hT_r, 0)
            for b in range(4):
                w, w_r = ws.get(bi)
                bi += 1
                for jj in range(4):
                    j = b * 4 + jj
                    pb, pb_r = self.bank(MM[mmi % len(MM)])
                    mmi += 1
                    self.linear(pb, pb_r, N, (lambda k, w=w, jj=jj: w[:, k, jj * 128:(jj + 1) * 128]), w_r, hT, hT_r, 0)
                    self.act(uT[:, j, :N], pb[:, :N], AF.Gelu, [pb_r], [uT_r[j]])
            for cb in range(4):
                w, w_r = ws.get(bi)
                bi += 1
                for t in range(ntile):
                    pb, pb_r = self.bank(MM[mmi % len(MM)])
                    mmi += 1
                    for k in range(KC):
                        self.mm(pb[:, :], hT[:, k, t * 128:(t + 1) * 128], w[:, k, :], k == 0, k == KC - 1, [w_r, hT_r], [pb_r])
                    self.act(vt[t][0][:, cb * 512:(cb + 1) * 512], pb[:, :], AF.Gelu, [pb_r], [vt[t][1]])
            for t in range(ntile):
                v, v_r = vt[t]
                st = stats[:, t * 24:(t + 1) * 24].rearrange("p (a b) -> p a b", a=4)
                for q in range(4):
                    S.add("dve", (lambda e, st=st, v=v, q=q: e.bn_stats(out=st[:, q, :], in_=v[:, q * 512:(q + 1) * 512])),
                          [v_r], [stats_r[t]])
                m2 = mv[:, 2 * t:2 * t + 2]
                st2 = stats[:, t * 24:(t + 1) * 24]
                S.add("dve", (lambda e, m2=m2, st2=st2: e.bn_aggr(out=m2, in_=st2)), [stats_r[t]], [mv_r[t]])
            for t in range(ntile):
                sc = stats[:, 96 + 2 * t:96 + 2 * t + 1]
                self.act(sc, mv[:, 2 * t + 1:2 * t + 2], AF.Sqrt, [mv_r[t], self.eps_r], [sc_r[t]], scale=1.0, bias=self.eps_col)
            for t in range(ntile):
                sc = stats[:, 96 + 2 * t:96 + 2 * t + 1]
                nb = stats[:, 96 + 2 * t + 1:96 + 2 * t + 2]
                self.recip(sc, sc, [sc_r[t]], [sc_r[t]])
                self.stt(nb, mv[:, 2 * t:2 * t + 1], -1.0, sc, ALU.mult, ALU.mult, [mv_r[t], sc_r[t]], [sc_r[t]])
            for t in range(ntile):
                v, v_r = vt[t]
                sc = stats[:, 96 + 2 * t:96 + 2 * t + 1]
                nb = stats[:, 96 + 2 * t + 1:96 + 2 * t + 2]
                vnb, vnb_r = vn[t]
                self.act(vnb, v, AF.Identity, [v_r, sc_r[t]], [vnb_r], scale=sc, bias=nb)
            for t in range(ntile):
                vnb, vnb_r = vn[t]
                for cq in range(4):
                    pb, pb_r = self.bank(MM[mmi % len(MM)])
                    mmi += 1
                    for cc in range(4):
                        c = cq * 4 + cc
                        self.mm(pb[:, cc * 128:(cc + 1) * 128], vnb[:, c * 128:(c + 1) * 128], wsT[:, c // 2, :], True, True,
                                [vnb_r, wsT_r], [pb_r])
                    for cc in range(4):
                        c = cq * 4 + cc
                        tmp, tmp_r = self.tmps[c % 2]
                        self.stt(tmp[:, :128], pb[:, cc * 128:(cc + 1) * 128], self.vcol(f"lng{j_}", c), BIAS[:, c, :],
                                 ALU.mult, ALU.add, [pb_r, self.vecs_r, BIAS_r], [tmp_r])
                        self.tt(uT[:, c, t * 128:(t + 1) * 128], tmp[:, :128], uT[:, c, t * 128:(t + 1) * 128], ALU.mult,
                                [tmp_r, uT_r[c]], [uT_r[c]])
            for dc in range(8):
                w, w_r = ws.get(bi)
                bi += 1
                pb, pb_r = self.bank(MM[mmi % len(MM)])
                mmi += 1
                for c in range(16):
                    self.mm(pb[:, :N], w[:, c, :], gT[:, c, :N], c == 0, c == 15, [w_r, gT_r[c]], [pb_r])
                self.stt(src(dc), pb[:, :N], self.gate("m", dc, which), src(dc), ALU.mult, ALU.add,
                         [pb_r, self.mod_r, src_r], [src_r])

    def build_mla(self, L):
        S = self.S
        j_ = L // 2
        wctx = self.with_ctx(L)
        XT, YT = self.XT, self.YT
        self.phase()
        NQ = NT + NCTX
        cqn = self.bf(3 * NQ).rearrange("p (m n) -> p m n", m=3)
        cqn_r = [S.R(f"cqn{g}") for g in range(5)]
        ckvn = self.bf(2 * NKEY).rearrange("p (m n) -> p m n", m=2)
        ckvn_r = [S.R(f"ckvn{g}") for g in range(9)]
        Cb = self.bf(NKEY)
        Cb_r = [S.R(f"C{g}") for g in range(9)]
        sqpe = self.bf(NKEY)
        sqpe_r = [S.R(f"sqpe{g}") for g in range(9)]
        if F_PAD:
            S.add("pool", lambda e: e.memset(sqpe[64:128, :], 0.0), [], sqpe_r)
        keep_top = self.top
        hT = self.bf(KC * 512).rearrange("p (k n) -> p k n", k=KC)
        hT_r = S.R("hT")
        xTp = self.f32(KC * 512).rearrange("p (k n) -> p k n", k=KC)
        xTp_r = [S.R("xTp0"), S.R("xTp1")]
        self.stage = [(self.f32(2048).rearrange("p (t d) -> p t d", t=2), S.R(f"stage{i}")) for i in range(2)]
        wkva = self.bf(KC * 384).rearrange("p (k n) -> p k n", k=KC)
        wkva_r = S.R("wkva")
        wqa = self.bf(KC * 384).rearrange("p (k n) -> p k n", k=KC)
        wqa_r = S.R("wqa")
        self.sqs = [(self.bf(512), S.R(f"sq{i}")) for i in range(3)]
        self.tmps = [(self.f32(512), S.R(f"tmp{i}")) for i in range(2)]
        self.rs_buf = (self.f32(512), S.R("rs"))
        rs2 = (self.f32(512), S.R("rs2"))
        tabs = [(self.f32(1024).rearrange("p (a n) -> p a n", a=2), S.R(f"tab{i}")) for i in range(2)]
        self.TRB = [0, 1]
        self.SSB = 2
        GEN = [3, 4, 5, 6, 7]
        self.dma("pool", wkva, self.d_wkva[j_].rearrange("(k p) n -> p k n", p=128), [], [wkva_r], lane="wkva")
        self.dma("pool", wqa, self.d_wqa[j_].rearrange("(k p) n -> p k n", p=128), [], [wqa_r], lane="wqa")
        groups = [("own", g) for g in range(4)] + [("par", g) for g in range(4)] + [("ctx", 0)]
        ti = 0
        for gi, (kind, g) in enumerate(groups):
            N = 512 if kind != "ctx" else NCTX
            kcol = gi * 512
            if kind == "own":
                src = (lambda g: (lambda k: XT[:, k, g * 512:(g + 1) * 512]))(g)
                src_r, which = self.XTr[g], 0
            elif kind == "par":
                self.load_tokens(self.d_xpar[g * 512:(g + 1) * 512, :], 4,
                                 lambda k, off, n: xTp[:, k, off:off + n], lambda h: xTp_r[h])
                src = lambda k: xTp[:, k, :]
                src_r, which = None, 0
            else:
                src = lambda k: YT[:, k, :]
                src_r, which = self.YTr, 1
            src_rs = xTp_r if kind == "par" else [src_r]
            self._modulate_multi(src, src_rs, N, which, "m", hT, hT_r, 0)
            pk = [self.bank(GEN[0]), self.bank(GEN[1])]
            ssb, ssb_r = self.bank(GEN[2])
            for m in range(2):
                self.linear(pk[m][0], pk[m][1], N, (lambda k, m=m: wkva[:, k, m * 128:(m + 1) * 128]), wkva_r, hT, hT_r, 0)
            for m in range(2):
                sq, sq_r = self.sqs[m]
                self.act(sq[:, :N], pk[m][0][:, :N], AF.Square, [pk[m][1]], [sq_r])
                self.mm(ssb[:, :N], self.ones, sq[:, :N], m == 0, m == 1, [sq_r, self.ones_r], [ssb_r])
            rs, rs_r = rs2
            self.rsqrt_b(ssb, N, 1.0 / 256, rs, rs_r, ssb_r)
            for m in range(2):
                self.stt(ckvn[:, m, kcol:kcol + N], pk[m][0][:, :N], self.vcol(f"gkva{j_}", m), rs[:, :N], ALU.mult, ALU.mult,
                         [pk[m][1], self.vecs_r, rs_r], [ckvn_r[gi]])
            pp, pp_r = self.bank(GEN[3])
            pw, pw_r = self.bank(GEN[4])
            self.linear(pp, pp_r, N, lambda k: wkva[:, k, 256:320], wkva_r, hT, hT_r, 0, parts=64)
            self.linear(pw, pw_r, N, lambda k: wkva[:, k, 320:384], wkva_r, hT, hT_r, 0, parts=64)
            self.act(sqpe[0:64, kcol:kcol + N], pp[0:64, :N], AF.Square, [pp_r], [sqpe_r[gi]])
            tb, tab_r = tabs[ti % 2]
            self.dma("sp", tb[0:64, :, :N], self.d_rope[:, :, kcol:kcol + N], [], [tab_r], lane=f"tab{ti % 2}")
            ti += 1
            cosb, sinb = tb[:, 0, :], tb[:, 1, :]
            tA, tA_r = self.tmps[0]
            tB, tB_r = self.tmps[1]
            self.stt(tA[0:64, :N], pp[0:64, :N], self.vcol(f"gkp{j_}")[0:64, :], cosb[0:64, :N], ALU.mult, ALU.mult,
                     [pp_r, self.vecs_r, tab_r], [tA_r])
            self.stt(tB[0:64, :N], pw[0:64, :N], self.vcol(f"gks{j_}")[0:64, :], sinb[0:64, :N], ALU.mult, ALU.mult,
                     [pw_r, self.vecs_r, tab_r], [tB_r])
            self.tt(Cb[0:64, kcol:kcol + N], tA[0:64, :N], tB[0:64, :N], ALU.add, [tA_r, tB_r], [Cb_r[gi]], eng="pool")
            if kind == "own" or (kind == "ctx" and wctx):
                qcol = g * 512 if kind == "own" else NT
                qg = g if kind == "own" else 4
                pq = [self.bank(GEN[0]), self.bank(GEN[1]), self.bank(GEN[3])]
                ssq, ssq_r = self.bank(GEN[2])
                for m in range(3):
                    self.linear(pq[m][0], pq[m][1], N, (lambda k, m=m: wqa[:, k, m * 128:(m + 1) * 128]), wqa_r, hT, hT_r, 0)
                for m in range(3):
                    sq, sq_r = self.sqs[m]
                    self.act(sq[:, :N], pq[m][0][:, :N], AF.Square, [pq[m][1]], [sq_r])
                    self.mm(ssq[:, :N], self.ones, sq[:, :N], m == 0, m == 2, [sq_r, self.ones_r], [ssq_r])
                rs, rs_r = rs2
                self.rsqrt_b(ssq, N, 1.0 / 384, rs, rs_r, ssq_r)
                for m in range(3):
                    self.stt(cqn[:, m, qcol:qcol + N], pq[m][0][:, :N], self.vcol(f"gqa{j_}", m), rs[:, :N], ALU.mult, ALU.mult,
                             [pq[m][1], self.vecs_r, rs_r], [cqn_r[qg]])

        S.barrier()
        self.top = keep_top
        KTn = self.bf(NKEY)
        KTp = self.bf(NKEY)
        Vh = self.bf(34 * 128)
        KV_r = [S.R(f"KV{g}") for g in range(9)]
        KP_r = [S.R(f"KP{g}") for g in range(9)]
        VV_r = [S.R(f"VV{g}") for g in range(9)]
        QTn = self.bf(NQ)
        QTp = self.bf(NQ)
        Q_r = [S.R(f"Q{g}") for g in range(5)]
        oTh = self.bf(NQ)
        oT_r = [S.R(f"oT{g}") for g in range(5)]
        PT = [(self.bf(512), S.R(f"PT{i}")) for i in range(4)]
        rec = (self.f32(512), S.R("rec"))
        rs2 = (self.f32(512), S.R("rs2"))
        accD = (self.f32(512), S.R("accD"))
        accP = (self.f32(512), S.R("accP"))
        sumhi = (self.bf(512), S.R("sumhi"))
        sumlo = (self.bf(512), S.R("sumlo"))
        self.tmps = [(self.f32(512), S.R(f"tmp{i}")) for i in range(3)]
        sqk = [(self.bf(512), S.R(f"sqk{i}")) for i in range(2)]
        sqp = (self.bf(512), S.R("sqp"))
        tabs = [(self.f32(1024).rearrange("p (a n) -> p a n", a=2), S.R(f"tab{i}")) for i in range(2)]
        wh = [(self.bf(2304), S.R(f"wh{i}")) for i in range(2)]
        if F_PAD:
            S.add("pool", lambda e: e.memset(KTp[64:128, :], 0.0), [], KP_r)
            S.add("pool", lambda e: e.memset(QTp[64:128, :], 0.0), [], Q_r)
            S.add("pool", lambda e: e.memset(sqp[0][64:128, :], 0.0), [], [sqp[1]])
        SB = [0, 1, 2, 7]
        GB = [0, 1, 2, 7]
        OB = [3, 4]
        UB = [5, 6]
        qgroups = [(g, g * 512, 512, 0) for g in range(4)]
        if wctx:
            qgroups.append((4, NT, NCTX, 1))
        ti = 0
        oi = 0
        si = 0
        for h in range(HEADS):
            whb, wh_r = wh[h % 2]
            wkvb = whb[:, 0:512].rearrange("p (m n) -> p m n", m=2)
            wqb = whb[:, 512:1280].rearrange("p (m n) -> p m n", m=3)
            woh = whb[:, 1280:2304]
            lane = f"wh{h % 2}"
            self.dma("pool", wkvb, self.d_wkvb[j_].rearrange("(m p) n -> p m n", p=128)[:, :, h * 256:(h + 1) * 256], [], [wh_r], lane=lane)
            self.dma("pool", wqb, self.d_wqb[j_].rearrange("(m p) n -> p m n", p=128)[:, :, h * 256:(h + 1) * 256], [], [wh_r], lane=lane)
            self.dma("pool", woh, self.d_wo[j_][h * 128:(h + 1) * 128, :], [], [wh_r], lane=lane)
            gkn = self.vcol(f"gkn{j_}")
            gqn = self.vcol(f"gqn{j_}")
            for kg in range(9):
                N = 512 if kg < 8 else NCTX
                kcol = kg * 512
                pk, pk_r = self.bank([0, 3][kg % 2])
                ssb, ssb_r = self.bank([1, 4][kg % 2])
                pv, pv_r = self.bank([2, 5][kg % 2])
                for m in range(2):
                    self.mm(pk[:, :N], wkvb[:, m, 0:128], ckvn[:, m, kcol:kcol + N], m == 0, m == 1, [wh_r, ckvn_r[kg]], [pk_r])
                sq, sq_r = sqk[kg % 2]
                self.act(sq[:, :N], pk[:, :N], AF.Square, [pk_r], [sq_r])
                self.mm(ssb[:, :N], self.ones, sq[:, :N], True, False, [sq_r, self.ones_r], [ssb_r])
                if F_PAD:
                    self.mm(ssb[:, :N], self.ones, sqpe[:, kcol:kcol + N], False, True, [sqpe_r[kg], self.ones_r], [ssb_r])
                else:
                    self.mm(ssb[:, :N], self.ones[0:64, :], sqpe[0:64, kcol:kcol + N], False, True, [sqpe_r[kg], self.ones_r], [ssb_r])
                rs, rs_r = rs2
                self.rsqrt_b(ssb, N, 1.0 / 192, rs, rs_r, ssb_r)
                self.stt(KTn[:, kcol:kcol + N], pk[:, :N], gkn, rs[:, :N], ALU.mult, ALU.mult, [pk_r, self.vecs_r, rs_r], [KV_r[kg]])
                self.tt(KTp[0:64, kcol:kcol + N], Cb[0:64, kcol:kcol + N], rs[0:64, :N], ALU.mult, [Cb_r[kg], rs_r], [KP_r[kg]], eng="pool")
                for t in range(N // 128):
                    for m in range(2):
                        self.mm(pv[:, t * 128:(t + 1) * 128], ckvn[:, m, kcol + t * 128:kcol + (t + 1) * 128], wkvb[:, m, 128:256],
                                m == 0, m == 1, [wh_r, ckvn_r[kg]], [pv_r])
                self.copy("dve", Vh[:, kcol:kcol + N], pv[:, :N], [pv_r], [VV_r[kg]])
            for (qg, qcol, N, which) in qgroups:
                gset = [[0, 1, 2, 7], [3, 4, 5, 6]][qg % 2]
                pn, pn_r = self.bank(gset[0])
                pp, pp_r = self.bank(gset[1])
                pw, pw_r = self.bank(gset[2])
                ssb, ssb_r = self.bank(gset[3])
                for m in range(3):
                    self.mm(pn[:, :N], wqb[:, m, 0:128], cqn[:, m, qcol:qcol + N], m == 0, m == 2, [wh_r, cqn_r[qg]], [pn_r])
                for m in range(3):
                    self.mm(pp[0:64, :N], wqb[:, m, 128:192], cqn[:, m, qcol:qcol + N], m == 0, m == 2, [wh_r, cqn_r[qg]], [pp_r])
                for m in range(3):
                    self.mm(pw[0:64, :N], wqb[:, m, 192:256], cqn[:, m, qcol:qcol + N], m == 0, m == 2, [wh_r, cqn_r[qg]], [pw_r])
                sq, sq_r = sqk[qg % 2]
                self.act(sq[:, :N], pn[:, :N], AF.Square, [pn_r], [sq_r])
                sp_, sp_r = sqp
                self.act(sp_[0:64, :N], pp[0:64, :N], AF.Square, [pp_r], [sp_r])
                self.mm(ssb[:, :N], self.ones, sq[:, :N], True, False, [sq_r, self.ones_r], [ssb_r])
                if F_PAD:
                    self.mm(ssb[:, :N], self.ones, sp_[:, :N], False, True, [sp_r, self.ones_r], [ssb_r])
                else:
                    self.mm(ssb[:, :N], self.ones[0:64, :], sp_[0:64, :N], False, True, [sp_r, self.ones_r], [ssb_r])
                rs, rs_r = rs2
                self.rsqrt_b(ssb, N, 1.0 / 192, rs, rs_r, ssb_r)
                self.stt(QTn[:, qcol:qcol + N], pn[:, :N], gqn, rs[:, :N], ALU.mult, ALU.mult, [pn_r, self.vecs_r, rs_r], [Q_r[qg]])
                tb, tab_r = tabs[ti % 2]
                tcol = qcol if which == 0 else 2 * NT
                self.dma("sp", tb[0:64, :, :N], self.d_rope[:, :, tcol:tcol + N], [], [tab_r], lane=f"tab{ti % 2}")
                ti += 1
                cosb, sinb = tb[:, 0, :], tb[:, 1, :]
                tA, tA_r = self.tmps[0]
                tB, tB_r = self.tmps[1]
                tC, tC_r = self.tmps[2]
                self.stt(tA[0:64, :N], pp[0:64, :N], self.vcol(f"gqp{j_}")[0:64, :], cosb[0:64, :N], ALU.mult, ALU.mult,
                         [pp_r, self.vecs_r, tab_r], [tA_r])
                self.stt(tB[0:64, :N], pw[0:64, :N], self.vcol(f"gqs{j_}")[0:64, :], sinb[0:64, :N], ALU.mult, ALU.mult,
                         [pw_r, self.vecs_r, tab_r], [tB_r])
                self.tt(tC[0:64, :N], tA[0:64, :N], tB[0:64, :N], ALU.add, [tA_r, tB_r], [tC_r], eng="pool")
                self.tt(QTp[0:64, qcol:qcol + N], tC[0:64, :N], rs[0:64, :N], ALU.mult, [tC_r, rs_r], [Q_r[qg]], eng="pool")
            for (qg, qcol, N, which) in qgroups:
                tiles = list(range(34)) if which == 0 else [32, 33]
                po, po_r = self.bank(OB[oi % 2])
                pu, pu_r = self.bank(UB[oi % 2])
                oi += 1
                LOOK = 2
                nt_ = len(tiles)
                sbank = {}

                def emit_s(i):
                    nonlocal si
                    kt = tiles[i]
                    ps, ps_r = self.bank(SB[si % len(SB)])
                    si += 1
                    sbank[i] = (ps, ps_r)
                    kg = kt // 4
                    self.mm(ps[:, :N], KTn[:, kt * 128:(kt + 1) * 128], QTn[:, qcol:qcol + N], True, False, [KV_r[kg], Q_r[qg]], [ps_r])
                    if F_PAD:
                        self.mm(ps[:, :N], KTp[:, kt * 128:(kt + 1) * 128], QTp[:, qcol:qcol + N], False, True, [KP_r[kg], Q_r[qg]], [ps_r])
                    else:
                        self.mm(ps[:, :N], KTp[0:64, kt * 128:(kt + 1) * 128], QTp[0:64, qcol:qcol + N], False, True, [KP_r[kg], Q_r[qg]], [ps_r])

                for i in range(min(LOOK, nt_)):
                    emit_s(i)
                for i in range(nt_):
                    kt = tiles[i]
                    kg = kt // 4
                    ps, ps_r = sbank.pop(i)
                    pt, pt_r = PT[i % 4]
                    self.act(pt[:, :N], ps[:, :N], AF.Exp, [ps_r], [pt_r], scale=SCALE)
                    if i + LOOK < nt_:
                        emit_s(i + LOOK)
                    self.mm(po[:, :N], Vh[:, kt * 128:(kt + 1) * 128], pt[:, :N], i == 0, i == nt_ - 1, [VV_r[kg], pt_r], [po_r])
                    if not F_SUMS:
                        self.mm(pu[:, :N], self.ones, pt[:, :N], i == 0, i == nt_ - 1, [self.ones_r, pt_r], [pu_r])
                    else:
                        onpool = (i % 3 == 2)
                        acc, acc_r = accP if onpool else accD
                        eng_ = "pool" if onpool else "dve"
                        first = (i == 2) if onpool else (i == 0)
                        if first:
                            self.copy(eng_, acc[:, :N], pt[:, :N], [pt_r], [acc_r])
                        else:
                            self.tt(acc[:, :N], acc[:, :N], pt[:, :N], ALU.add, [acc_r, pt_r], [acc_r], eng=eng_)
                if F_SUMS:
                    aD, aD_r = accD
                    aP, aP_r = accP
                    if nt_ > 2:
                        self.tt(aD[:, :N], aD[:, :N], aP[:, :N], ALU.add, [aD_r, aP_r], [aD_r])
                    hi, hi_r = sumhi
                    lo, lo_r = sumlo
                    self.copy("dve", hi[:, :N], aD[:, :N], [aD_r], [hi_r])
                    self.tt(lo[:, :N], aD[:, :N], hi[:, :N], ALU.subtract, [aD_r, hi_r], [lo_r])
                    self.mm(pu[:, :N], self.ones, hi[:, :N], True, False, [self.ones_r, hi_r], [pu_r])
                    self.mm(pu[:, :N], self.ones, lo[:, :N], False, True, [self.ones_r, lo_r], [pu_r])
                rc, rc_r = rec
                if F_LN:
                    self.act(rc[:, :N], pu[:, :N], AF.Ln, [pu_r], [rc_r])
                    self.act(rc[:, :N], rc[:, :N], AF.Exp, [rc_r], [rc_r], scale=-1.0)
                else:
                    self.recip(rc[:, :N], pu[:, :N], [pu_r], [rc_r])
                self.tt(oTh[:, qcol:qcol + N], po[:, :N], rc[:, :N], ALU.mult, [po_r, rc_r], [oT_r[qg]])
            gi_ = 0
            for (qg, qcol, N, which) in qgroups:
                if which == 0:
                    dst = (lambda qg: (lambda k: XT[:, k, qg * 512:(qg + 1) * 512]))(qg)
                    dst_r = self.XTr[qg]
                else:
                    dst = lambda k: YT[:, k, :]
                    dst_r = self.YTr
                for dc in range(8):
                    pb, pb_r = self.bank(GB[gi_ % len(GB)])
                    gi_ += 1
                    self.mm(pb[:, :N], woh[:, dc * 128:(dc + 1) * 128], oTh[:, qcol:qcol + N], True, True, [wh_r, oT_r[qg]], [pb_r])
                    self.stt(dst(dc), pb[:, :N], self.gate("m", dc, which), dst(dc), ALU.mult, ALU.add,
                             [pb_r, self.mod_r, dst_r], [dst_r])

    def _modulate_multi(self, src, src_rs, N, which, kind, hT, hT_r, coff):
        A = self.am if kind == "m" else self.af
        shb = 0 if kind == "m" else 24
        ssb, ssb_r = self.bank(self.SSB)
        src_rs = list(src_rs)
        for k in range(KC):
            sq, sq_r = self.sqs[k % len(self.sqs)]
            self.act(sq[:, :N], src(k), AF.Square, src_rs, [sq_r])
            self.mm(ssb[:, :N], self.ones, sq[:, :N], k == 0, k == KC - 1, [sq_r, self.ones_r], [ssb_r])
        rs, rs_r = self.rs_buf
        self.rsqrt_b(ssb, N, 1.0 / D, rs, rs_r, ssb_r)
        for k in range(KC):
            tmp, tmp_r = self.tmps[k % len(self.tmps)]
            self.tt(tmp[:, :N], src(k), rs[:, :N], ALU.mult, src_rs + [rs_r], [tmp_r])
            self.act(hT[:, k, coff:coff + N], tmp[:, :N], AF.Identity, [tmp_r, self.mod_r], [hT_r],
                     scale=A[:, 2 * k + which:2 * k + which + 1],
                     bias=self.mod[:, 2 * (shb + k) + which:2 * (shb + k) + which + 1])

    def _emit(self):
        nc, S = self.nc, self.S
        sem = {}
        for e in ENGS:
            sem[("e", e)] = self.es.enter_context(nc.semaphore(f"s_{e}"))
        for l in S.lanes:
            sem[("l", l)] = self.es.enter_context(nc.semaphore(f"l_{l}"))
        by = {e: [op for op in S.ops if op.eng == e] for e in ENGS}
        block = self.es.enter_context(nc.Block())

        def body(ename):
            def f(eng):
                for op in by[ename]:
                    for key, val in op.waits:
                        eng.wait_ge(sem[key], val)
                    if op.fn is None:
                        continue
                    if op.lane is not None:
                        op.fn(eng, sem[("l", op.lane)])
                    else:
                        inst = op.fn(eng)
                        if op.sigkey is not None:
                            inst.then_inc(sem[("e", ename)], 1)
            return f

        block.tensor(body("pe"))
        block.scalar(body("act"))
        block.vector(body("dve"))
        block.gpsimd(body("pool"))
        block.sync(body("sp"))


def _cols(v, n):
    return np.ascontiguousarray(np.asarray(v, np.float32).reshape(n, 128).T)


def _dup(a):
    return np.repeat(a, 2, axis=1)


_SWAP = np.array([(((d // 16) ^ 1) * 16 + d % 16) for d in range(64)])

VOFF = {}
NV = 0


def _vec_layout():
    global NV
    off = 0

    def add(name, n):
        nonlocal off
        VOFF[name] = off
        off += n
    for i in range(4):
        add(f"adab{i}", 96)
        add(f"gmix{i}", 16)
        add(f"gffn{i}", 16)
    for j in range(2):
        add(f"gqa{j}", 3)
        add(f"gkva{j}", 2)
        for n in ("gqn", "gqp", "gqs", "gkn", "gkp", "gks"):
            add(f"{n}{j}", 1)
        add(f"lng{j}", 16)
        add(f"lnb{j}", 16)
    add("cvec", 16)
    NV = off


_vec_layout()


def _pad64(v):
    o = np.zeros((128, 1), np.float32)
    o[:64, 0] = v
    return o


def _build_vecs(inp, b):
    V = np.zeros((128, NV), np.float32)

    def put(name, a):
        V[:, VOFF[name]:VOFF[name] + a.shape[1]] = a
    for i in range(4):
        put(f"adab{i}", _dup(_cols(inp["ada_b"][i], 48)))
        put(f"gmix{i}", _dup(_cols(inp["norm_mix_g"][i], 8)))
        put(f"gffn{i}", _dup(_cols(inp["norm_ffn_g"][i], 8)))
    for j in range(2):
        put(f"gqa{j}", _cols(inp["mla_q_a_norm"][j], 3))
        put(f"gkva{j}", _cols(inp["mla_kv_a_norm"][j], 2))
        qn = np.asarray(inp["mla_q_norm"][j], np.float32)
        kn = np.asarray(inp["mla_k_norm"][j], np.float32)
        put(f"gqn{j}", qn[:128].reshape(128, 1))
        put(f"gqp{j}", _pad64(qn[128:]))
        put(f"gqs{j}", _pad64(qn[128:][_SWAP]))
        put(f"gkn{j}", kn[:128].reshape(128, 1))
        put(f"gkp{j}", _pad64(kn[128:]))
        put(f"gks{j}", _pad64(kn[128:][_SWAP]))
        put(f"lng{j}", _cols(inp["gm_ln_g"][j], 16))
        put(f"lnb{j}", _cols(inp["gm_ln_b"][j], 16))
    cv = np.stack([_cols(inp["c"][b], 8), _cols(inp["c_ctx"], 8)], axis=2).reshape(128, 16)
    put("cvec", cv)
    return V


def _rope_tables(half):
    pos_own = np.arange(half * NT, (half + 1) * NT)
    pos_par = np.arange((1 - half) * NT, (2 - half) * NT)
    pos = np.concatenate([pos_own, pos_par]).astype(np.float32)
    row = np.floor(pos / 64).astype(np.float32)
    col = (pos - row * 64).astype(np.float32)
    inv = (np.float32(10000.0) ** (-np.arange(0, 32, 2, dtype=np.float32) / np.float32(32))).astype(np.float32)
    ang_r = row[:, None] * inv[None, :]
    ang_c = col[:, None] * inv[None, :]
    ang = np.concatenate([ang_r, ang_r, ang_c, ang_c], axis=-1).astype(np.float32)
    cos = np.cos(ang).astype(np.float32)
    sin = np.sin(ang).astype(np.float32)
    sign = np.concatenate([-np.ones(16), np.ones(16), -np.ones(16), np.ones(16)]).astype(np.float32)
    sinS = sin * sign[None, :]
    cosT = np.concatenate([cos.T, np.ones((64, NCTX), np.float32)], axis=1)
    sinT = np.concatenate([sinS.T, np.zeros((64, NCTX), np.float32)], axis=1)
    return np.ascontiguousarray(np.stack([cosT, sinT], axis=1))


def _shared_weights(inp):
    f = lambda a: np.ascontiguousarray(np.asarray(a, np.float32))
    wkva = np.asarray(inp["mla_wkv_a"], np.float32)
    wkva2 = np.concatenate([wkva, wkva[:, :, 256:][:, :, _SWAP]], axis=2)
    wqb = np.asarray(inp["mla_wq_b"], np.float32).reshape(2, 384, 8, 192)
    wqb2 = np.concatenate([wqb, wqb[:, :, :, 128:][:, :, :, _SWAP]], axis=3).reshape(2, 384, 2048)
    bs = np.asarray(inp["gm_bs"], np.float32)
    bsb = np.broadcast_to(bs.reshape(2, 1, 1024), (2, 128, 1024))
    wsT = np.asarray(inp["gm_ws"], np.float32).transpose(0, 3, 1, 2).reshape(2, 128, 1024)
    return {
        "ident": np.eye(128, dtype=np.float32),
        "bsb": f(bsb), "wsT": f(wsT),
        "ada_w": f(inp["ada_w"]), "wq_a": f(inp["mla_wq_a"]), "wkv_a": f(wkva2), "wq_b": f(wqb2),
        "wkv_b": f(inp["mla_wkv_b"]), "wo": f(inp["mla_wo"]),
        "gm_w_in": f(inp["gm_w_in"]), "gm_w_out": f(inp["gm_w_out"]),
        "ffn_w1": f(inp["ffn_w1"]), "ffn_w2": f(inp["ffn_w2"]),
    }


_PROG_CACHE = {}


def _get_prog(layers, debug_stop=None):
    key = (tuple(layers), debug_stop)
    if key not in _PROG_CACHE:
        _PROG_CACHE[key] = Prog(list(layers), True, True, debug_stop)
    return _PROG_CACHE[key]


def run_layers(inp, x, y, layers, debug_stop=None, ncores=8):
    prog = _get_prog(layers, debug_stop)
    shared = _shared_weights(inp)
    in_maps = []
    for c in range(ncores):
        b, half = c // 2, c % 2
        rope = _rope_tables(half)
        m = dict(shared)
        m["x_own"] = np.ascontiguousarray(x[b, half * NT:(half + 1) * NT])
        m["x_par"] = np.ascontiguousarray(x[b, (1 - half) * NT:(2 - half) * NT])
        m["ctx"] = np.ascontiguousarray(y[b])
        m["vecs"] = _build_vecs(inp, b)
        m["rope"] = rope
        in_maps.append(m)
    res = run_bass_kernel_spmd(prog.nc, in_maps, core_ids=list(range(ncores)))
    xo = np.zeros_like(x)
    yo = np.zeros_like(y)
    if prog.dbg_names:
        run_layers.dbg = [{n: np.asarray(res.results[c][n]) for n in prog.dbg_names} for c in range(ncores)]
    for c in range(ncores):
        b, half = c // 2, c % 2
        xo[b, half * NT:(half + 1) * NT] = res.results[c]["out_x"]
        if half == 0:
            yo[b] = res.results[c]["out_y"]
    return xo, yo


def kernel(**inputs):
    inp = {k: np.asarray(v) for k, v in inputs.items()}
    x = np.ascontiguousarray(inp["x"], dtype=np.float32)
    y = np.ascontiguousarray(inp["ctx"], dtype=np.float32)
    x, y = run_layers(inp, x, y, (0, 1))
    x, y = run_layers(inp, x, y, (2, 3))
    return x.astype(np.float32)
```

```python
import numpy as np
from contextlib import ExitStack
import concourse.bass as bass
import concourse.mybir as mybir
from concourse.bass_utils import run_bass_kernel_spmd

F32 = mybir.dt.float32
BF16 = mybir.dt.bfloat16
AF = mybir.ActivationFunctionType
ALU = mybir.AluOpType

D = 1024
KC = 8
NT = 2048
NCTX = 256
NKEY = 2 * NT + NCTX
HEADS = 8
EPS = 1e-6
SCALE = 192 ** -0.5
ARENA_WORDS = 53000

ENGS = ("pe", "act", "dve", "pool", "sp")
import os as _os
F_LN = _os.environ.get("F_LN", "1") == "1"
F_PAD = _os.environ.get("F_PAD", "1") == "1"
F_SUMS = _os.environ.get("F_SUMS", "1") == "1"


class Res:
    __slots__ = ("name", "w", "r", "rd")

    def __init__(self, name):
        self.name = name
        self.w = None
        self.r = {}
        self.rd = []


class Op:
    __slots__ = ("eng", "fn", "deps", "lane", "ndma", "signal", "sigkey", "sigval", "waits", "known")

    def __init__(self, eng, fn, lane, ndma):
        self.eng = eng
        self.fn = fn
        self.lane = lane
        self.ndma = ndma
        self.signal = False
        self.sigkey = None
        self.sigval = 0
        self.waits = ()
        self.known = None
        self.deps = ()


class Sched:
    def __init__(self):
        self.ops = []
        self.res = []
        self.lanes = {}
        self.pending = {e: None for e in ENGS}
        self.last = {e: None for e in ENGS}
        self.lane_last = {}

    def R(self, name):
        r = Res(name)
        self.res.append(r)
        return r

    def add(self, eng, fn, reads=(), writes=(), lane=None, ndma=0):
        op = Op(eng, fn, lane, ndma)
        deps = set()
        for r in reads:
            if r.w is not None:
                deps.add(r.w)
        for w in writes:
            if w.w is not None:
                deps.add(w.w)
            deps.update(w.r.values())
            deps.update(w.rd)
        if self.pending[eng] is not None:
            deps.update(self.pending[eng])
            self.pending[eng] = None
        dl = []
        for d in deps:
            if d.eng == "pe" and eng == "pe" and d.lane is None and lane is None:
                continue
            d.signal = True
            dl.append(d)
        op.deps = dl
        for r in reads:
            if lane is not None:
                r.rd.append(op)
            else:
                r.r[eng] = op
        for w in writes:
            w.w = op
            w.r = {}
            w.rd = []
        self.ops.append(op)
        if lane is not None:
            if lane in self.lanes:
                assert self.lanes[lane] == eng, "one issuing engine per lane"
            self.lanes[lane] = eng
            self.lane_last[lane] = op
        else:
            self.last[eng] = op
        return op

    def barrier(self):
        outstanding = [o for o in self.last.values() if o is not None]
        outstanding += list(self.lane_last.values())
        for o in outstanding:
            o.signal = True
        for e in ENGS:
            cur = self.pending[e]
            self.pending[e] = list(outstanding) + (cur if cur else [])
        for r in self.res:
            r.w = None
            r.r = {}
            r.rd = []

    def fence(self, eng="sp"):
        self.barrier()
        self.add(eng, None)

    def finalize(self):
        ms = {e: 0 for e in ENGS}
        lc = {l: 0 for l in self.lanes}
        for op in self.ops:
            if op.lane is not None:
                lc[op.lane] += 16 * op.ndma
                op.sigkey = ("l", op.lane)
                op.sigval = lc[op.lane]
            elif op.signal and op.fn is not None:
                ms[op.eng] += 1
                op.sigkey = ("e", op.eng)
                op.sigval = ms[op.eng]
        seen = {e: {} for e in ENGS}
        for op in self.ops:
            s = seen[op.eng]
            waits = {}
            for d in op.deps:
                if d.sigkey is None:
                    continue
                if s.get(d.sigkey, 0) >= d.sigval:
                    continue
                if waits.get(d.sigkey, 0) < d.sigval:
                    waits[d.sigkey] = d.sigval
            for d in op.deps:
                if d.sigkey in waits and d.known is not None:
                    for k, v in d.known.items():
                        if s.get(k, 0) < v:
                            s[k] = v
            for k, v in waits.items():
                if s.get(k, 0) < v:
                    s[k] = v
            op.waits = tuple(waits.items())
            if op.sigkey is not None:
                op.known = dict(s)
        self.stats = (dict(ms), dict(lc))


class Prog:
    def __init__(self, layers, first, last, debug_stop=None):
        self.layers = layers
        self.debug_stop = debug_stop
        self.debug = debug_stop is not None
        self.dbg_names = []
        nc = bass.Bass("TRN2", target_bir_lowering=False)
        self.nc = nc
        self.S = Sched()
        self.es = ExitStack()
        self._declare_dram()
        with self.es:
            self._alloc()
            self._build()
            self.S.finalize()
            self._emit()

    def _declare_dram(self):
        nc = self.nc
        din = lambda n, s: nc.dram_tensor(n, s, F32, kind="ExternalInput").ap()
        self.d_xown = din("x_own", [NT, D])
        self.d_xpar = din("x_par", [NT, D])
        self.d_ctx = din("ctx", [NCTX, D])
        self.d_vecs = din("vecs", [128, NV])
        self.d_ident = din("ident", [128, 128])
        self.d_rope = din("rope", [64, 2, NKEY])
        self.d_bsb = din("bsb", [2, 128, 1024])
        self.d_wsT = din("wsT", [2, 128, 1024])
        self.d_adaw = din("ada_w", [4, D, 6 * D])
        self.d_wqa = din("wq_a", [2, D, 384])
        self.d_wkva = din("wkv_a", [2, D, 384])
        self.d_wqb = din("wq_b", [2, 384, 2048])
        self.d_wkvb = din("wkv_b", [2, 256, 2048])
        self.d_wo = din("wo", [2, D, D])
        self.d_win = din("gm_w_in", [2, D, 4 * D])
        self.d_wout = din("gm_w_out", [2, 2 * D, D])
        self.d_w1 = din("ffn_w1", [4, D, 4 * D])
        self.d_w2 = din("ffn_w2", [4, 4 * D, D])
        self.d_out = nc.dram_tensor("out_x", [NT, D], F32, kind="ExternalOutput").ap()
        self.d_outy = nc.dram_tensor("out_y", [NCTX, D], F32, kind="ExternalOutput").ap()

    def _alloc(self):
        nc, es, S = self.nc, self.es, self.S
        self.arena = es.enter_context(nc.sbuf_tensor("arena", [128, ARENA_WORDS], F32))
        self.arena_b = self.arena.bitcast(BF16)
        self.top = 0
        self.banks = []
        for i in range(8):
            t = es.enter_context(nc.psum_tensor(f"pb{i}", [128, 512], F32))
            self.banks.append((t, S.R(f"pb{i}")))
        self.XT = self.f32(KC * NT).rearrange("p (k n) -> p k n", k=KC)
        self.XTr = [S.R(f"XT{g}") for g in range(4)]
        self.YT = self.f32(KC * NCTX).rearrange("p (k n) -> p k n", k=KC)
        self.YTr = S.R("YT")
        self.vecs = self.f32(NV)
        self.vecs_r = S.R("vecs")
        self.ident = self.f32(128)
        self.ident_r = S.R("ident")
        self.ones = self.bf(128)
        self.ones_r = S.R("ones")
        self.mod = self.f32(96)
        self.am = self.f32(16)
        self.af = self.f32(16)
        self.mod_r = S.R("mod")
        self.scb = self.bf(16)
        self.scb_r = S.R("scb")
        self.persist_top = self.top

    def f32(self, n, parts=None):
        a = self.arena[:, self.top:self.top + n]
        self.top += n
        assert self.top <= ARENA_WORDS, f"SBUF overflow {self.top}"
        return a

    def bf(self, n):
        w = (n + 1) // 2
        a = self.arena_b[:, 2 * self.top:2 * self.top + n]
        self.top += w
        assert self.top <= ARENA_WORDS, f"SBUF overflow {self.top}"
        return a

    def phase(self):
        self.S.barrier()
        self.top = self.persist_top

    def mm(self, out, lhsT, rhs, start, stop, reads, writes):
        self.S.add("pe", lambda e: e.matmul(out, lhsT=lhsT, rhs=rhs, start=start, stop=stop), reads, writes)

    def tr(self, out, in_, reads, writes):
        ident = self.ident
        self.S.add("pe", lambda e: e.transpose(out, in_, ident), list(reads) + [self.ident_r], writes)

    def act(self, out, in_, func, reads, writes, scale=None, bias=None):
        kw = {}
        if scale is not None:
            kw["scale"] = scale
        if bias is not None:
            kw["bias"] = bias
        self.S.add("act", lambda e: e.activation(out=out, in_=in_, func=func, **kw), reads, writes)

    def stt(self, out, in0, scalar, in1, op0, op1, reads, writes, eng="dve"):
        self.S.add(eng, lambda e: e.scalar_tensor_tensor(out=out, in0=in0, scalar=scalar, in1=in1, op0=op0, op1=op1),
                   reads, writes)

    def tt(self, out, in0, in1, op, reads, writes, eng="dve"):
        self.S.add(eng, lambda e: e.tensor_tensor(out=out, in0=in0, in1=in1, op=op), reads, writes)

    def ts(self, out, in0, s1, s2, op0, op1, reads, writes, eng="dve"):
        self.S.add(eng, lambda e: e.tensor_scalar(out=out, in0=in0, scalar1=s1, scalar2=s2, op0=op0, op1=op1),
                   reads, writes)

    def recip(self, out, in_, reads, writes):
        self.S.add("dve", lambda e: e.reciprocal(out=out, in_=in_), reads, writes)

    def copy(self, eng, out, in_, reads, writes):
        if eng == "act":
            self.S.add("act", lambda e: e.copy(out=out, in_=in_), reads, writes)
        else:
            self.S.add(eng, lambda e: e.tensor_copy(out=out, in_=in_), reads, writes)

    def dma(self, eng, out, in_, reads, writes, lane):
        def fn(e, sem):
            e.dma_start(out=out, in_=in_).then_inc(sem, 16)
        self.S.add(eng, fn, reads, writes, lane=lane, ndma=1)

    def dump(self, name, ap, reads, dtype=F32):
        if not getattr(self, "debug", False):
            return
        shape = list(ap.shape)
        d = self.nc.dram_tensor("dbg_" + name, shape, dtype, kind="ExternalOutput").ap()
        self.dma("sp", d, ap, reads, [], lane="dbg_" + name)
        self.dbg_names.append("dbg_" + name)

    def vcol(self, name, j=0):
        o = VOFF[name] + j
        return self.vecs[:, o:o + 1]

    def vslice(self, name, n):
        o = VOFF[name]
        return self.vecs[:, o:o + n]

    def bank(self, i):
        return self.banks[i]

    class WStream:
        def __init__(self, prog, name, slots, specs):
            self.p = prog
            self.slots = slots
            self.specs = specs
            self.issued = 0
            self.name = name

        def get(self, n):
            ns = len(self.slots)
            while self.issued < len(self.specs) and self.issued <= n + ns - 1:
                m = self.issued
                ap, res = self.slots[m % ns]
                src, view = self.specs[m]
                self.p.dma("pool", view(ap), src, [], [res], lane=f"{self.name}{m % ns}")
                self.issued += 1
            ap, res = self.slots[n % ns]
            return self.specs[n][1](ap), res

    def rsqrt_b(self, ps, N, inv_n, rs, rs_r, ps_r, parts=128):
        if F_LN:
            self.act(rs[:parts, :N], ps[:parts, :N], AF.Ln, [ps_r, self.eps_r], [rs_r], scale=inv_n, bias=self.eps_col[:parts, :])
            self.act(rs[:parts, :N], rs[:parts, :N], AF.Exp, [rs_r], [rs_r], scale=-0.5)
        else:
            self.act(rs[:parts, :N], ps[:parts, :N], AF.Sqrt, [ps_r, self.eps_r], [rs_r], scale=inv_n, bias=self.eps_col[:parts, :])
            self.recip(rs[:parts, :N], rs[:parts, :N], [rs_r], [rs_r])

    def modulate(self, src, src_r, N, which, kind, hT, hT_r, coff):
        self._modulate_multi(src, [src_r], N, which, kind, hT, hT_r, coff)

    def linear(self, ps, ps_r, N, w, w_r, hT, hT_r, coff, nk=KC, parts=128):
        for k in range(nk):
            self.mm(ps[:parts, :N], w(k), hT[:, k, coff:coff + N], k == 0, k == nk - 1, [w_r, hT_r], [ps_r])

    def load_tokens(self, dram_rows, ntiles, dest, dest_r):
        for h in range(ntiles // 2):
            st, st_r = self.stage[h % 2]
            src = dram_rows[h * 256:(h + 1) * 256, :].rearrange("(t p) d -> p t d", p=128)
            self.dma("sp", st, src, [], [st_r], lane=f"stage{h % 2}")
            for k in range(KC):
                pb, pb_r = self.bank(self.TRB[k % 2])
                for t in range(2):
                    self.tr(pb[:, t * 128:(t + 1) * 128], st[:, t, k * 128:(k + 1) * 128], [st_r], [pb_r])
                self.copy("act" if k % 2 == 0 else "dve", dest(k, h * 256, 256), pb[:, 0:256], [pb_r], [dest_r(h)])

    def store_tokens(self, src, src_r, ntiles, dram_rows, lane):
        for h in range(ntiles // 2):
            st, st_r = self.stage[h % 2]
            for t in range(2):
                for half in range(2):
                    pb, pb_r = self.bank(self.TRB[(2 * t + half) % 2])
                    for kk in range(4):
                        k = half * 4 + kk
                        self.tr(pb[:, kk * 128:(kk + 1) * 128], src(k, h * 256 + t * 128, 128), [src_r(h)], [pb_r])
                    self.copy("act" if half == 0 else "dve", st[:, t, half * 512:(half + 1) * 512], pb[:, :], [pb_r], [st_r])
            dst = dram_rows[h * 256:(h + 1) * 256, :].rearrange("(t p) d -> p t d", p=128)
            self.dma("sp", dst, st, [st_r], [], lane=f"{lane}{h % 2}")

    def _build(self):
        S = self.S
        self.out_r = S.R("out")
        self.phase()
        self.eps_col = self.f32(1)
        self.eps_r = S.R("eps")
        S.add("dve", lambda e: e.memset(self.eps_col, EPS), [], [self.eps_r])
        self.persist_top = self.top
        S.add("dve", lambda e: e.memset(self.ones, 1.0), [], [self.ones_r])
        self.dma("sp", self.vecs, self.d_vecs, [], [self.vecs_r], lane="vecs")
        self.dma("sp", self.ident, self.d_ident, [], [self.ident_r], lane="ident")
        self.act(self.scb, self.vslice("cvec", 16), AF.Silu, [self.vecs_r], [self.scb_r])
        self.stage = []
        for i in range(2):
            self.stage.append((self.f32(2048).rearrange("p (t d) -> p t d", t=2), S.R(f"stage{i}")))
        self.TRB = [0, 1]
        XT, YT = self.XT, self.YT
        self.load_tokens(self.d_xown, 16, lambda k, off, n: XT[:, k, off:off + n], lambda h: self.XTr[h // 2])
        self.load_tokens(self.d_ctx, 2, lambda k, off, n: YT[:, k, off:off + n], lambda h: self.YTr)

        for li, L in enumerate(self.layers):
            self.build_mod(L)
            if self.debug_stop == ("mod", L):
                break
            if L % 2 == 0:
                self.build_mla(L)
            else:
                self.build_gmlp(L)
            self.dump(f"xt_mix{L}", self.XT, self.XTr)
            self.dump(f"yt_mix{L}", self.YT, [self.YTr])
            if self.debug_stop == ("mix", L):
                break
            self.build_ffn(L)
            self.dump(f"xt_ffn{L}", self.XT, self.XTr)
            self.dump(f"yt_ffn{L}", self.YT, [self.YTr])
            if self.debug_stop == ("ffn", L):
                break

        self.phase()
        self.stage = []
        for i in range(2):
            self.stage.append((self.f32(2048).rearrange("p (t d) -> p t d", t=2), S.R(f"ostage{i}")))
        self.TRB = [0, 1]
        self.store_tokens(lambda k, off, n: XT[:, k, off:off + n], lambda h: self.XTr[h // 2], 16, self.d_out, "ox")
        self.store_tokens(lambda k, off, n: YT[:, k, off:off + n], lambda h: self.YTr, 2, self.d_outy, "oy")
        S.fence("sp")

    def build_mod(self, L):
        S = self.S
        self.phase()
        slots = [(self.bf(4096), S.R(f"adaslot{i}")) for i in range(2)]
        wsrc = self.d_adaw[L].rearrange("(k p) n -> p k n", p=128)
        view = lambda ap: ap.rearrange("p (k n) -> p k n", k=KC)
        ws = Prog.WStream(self, "ada", slots, [(wsrc[:, :, b * 512:(b + 1) * 512], view) for b in range(12)])
        pb, pb_r = self.bank(2)
        for b in range(12):
            w, w_r = ws.get(b)
            for cc in range(4):
                ch = b * 4 + cc
                for k in range(KC):
                    self.mm(pb[:, 2 * ch:2 * ch + 2], w[:, k, cc * 128:(cc + 1) * 128], self.scb[:, 2 * k:2 * k + 2],
                            k == 0, k == KC - 1, [w_r, self.scb_r], [pb_r])
        o = VOFF[f"adab{L}"]
        self.tt(self.mod, pb[:, 0:96], self.vecs[:, o:o + 96], ALU.add, [pb_r, self.vecs_r], [self.mod_r])
        og = VOFF[f"gmix{L}"]
        self.stt(self.am, self.mod[:, 16:32], 1.0, self.vecs[:, og:og + 16], ALU.add, ALU.mult, [self.mod_r, self.vecs_r], [self.mod_r])
        og = VOFF[f"gffn{L}"]
        self.stt(self.af, self.mod[:, 64:80], 1.0, self.vecs[:, og:og + 16], ALU.add, ALU.mult, [self.mod_r, self.vecs_r], [self.mod_r])
        self.dump(f"mod{L}", self.mod, [self.mod_r])

    def gate(self, kind, k, which):
        base = 16 if kind == "m" else 40
        c = 2 * (base + k) + which
        return self.mod[:, c:c + 1]

    def with_ctx(self, L):
        return L < 2

    def build_ffn(self, L):
        S = self.S
        self.phase()
        XT, YT = self.XT, self.YT
        sgs = [[("x", 0), ("x", 1)], [("x", 2), ("x", 3)]]
        if self.with_ctx(L):
            sgs.append([("y", 0)])
        NSG = 1024
        hT = self.bf(KC * NSG).rearrange("p (k n) -> p k n", k=KC)
        hT_r = S.R("hT")
        aT = self.bf(32 * NSG).rearrange("p (j n) -> p j n", j=32)
        aT_r = [S.R(f"aT{j}") for j in range(32)]
        slots = [(self.bf(4096), S.R(f"wslot{i}")) for i in range(4)]
        self.sqs = [(self.bf(512), S.R(f"sq{i}")) for i in range(2)]
        self.tmps = [(self.f32(512), S.R(f"tmp{i}")) for i in range(2)]
        self.rs_buf = (self.f32(512), S.R("rs"))
        rl = [(self.f32(512), S.R(f"relu{i}")) for i in range(2)]
        self.SSB = 7
        MM = [0, 1, 2, 3, 4, 5]
        w1src = self.d_w1[L].rearrange("(k p) n -> p k n", p=128)
        w2src = self.d_w2[L].rearrange("(j p) n -> p j n", p=128)
        v1 = lambda ap: ap.rearrange("p (k n) -> p k n", k=KC)
        v2 = lambda ap: ap.rearrange("p (j n) -> p j n", j=32)
        specs = []
        for sg in sgs:
            specs += [(w1src[:, :, b * 512:(b + 1) * 512], v1) for b in range(8)]
            specs += [(w2src[:, :, dc * 128:(dc + 1) * 128], v2) for dc in range(8)]
        ws = Prog.WStream(self, "ffw", slots, specs)
        bi = 0
        mmi = 0
        for sg in sgs:
            subs = []
            off = 0
            for (kind, g) in sg:
                if kind == "x":
                    N = 512
                    src = (lambda g: (lambda k: XT[:, k, g * 512:(g + 1) * 512]))(g)
                    src_r = self.XTr[g]
                    which = 0
                else:
                    N = NCTX
                    src = lambda k: YT[:, k, :]
                    src_r = self.YTr
                    which = 1
                subs.append((src, src_r, N, which, off))
                off += N
            for (src, src_r, N, which, off) in subs:
                self.modulate(src, src_r, N, which, "f", hT, hT_r, off)
            for b in range(8):
                w, w_r = ws.get(bi)
                bi += 1
                for jj in range(4):
                    j = b * 4 + jj
                    for (src, src_r, N, which, off) in subs:
                        pb, pb_r = self.bank(MM[mmi % len(MM)])
                        mmi += 1
                        self.linear(pb, pb_r, N, (lambda k, w=w, jj=jj: w[:, k, jj * 128:(jj + 1) * 128]), w_r, hT, hT_r, off)
                        r, r_r = rl[mmi % 2]
                        self.act(r[:, :N], pb[:, :N], AF.Relu, [pb_r], [r_r])
                        self.tt(aT[:, j, off:off + N], r[:, :N], r[:, :N], ALU.mult, [r_r], [aT_r[j]])
            for dc in range(8):
                w, w_r = ws.get(bi)
                bi += 1
                for (src, src_r, N, which, off) in subs:
                    pb, pb_r = self.bank(MM[mmi % len(MM)])
                    mmi += 1
                    for j in range(32):
                        self.mm(pb[:, :N], w[:, j, :], aT[:, j, off:off + N], j == 0, j == 31, [w_r, aT_r[j]], [pb_r])
                    self.stt(src(dc), pb[:, :N], self.gate("f", dc, which), src(dc), ALU.mult, ALU.add,
                             [pb_r, self.mod_r, src_r], [src_r])

    def build_gmlp(self, L):
        S = self.S
        j_ = L // 2
        self.phase()
        XT, YT = self.XT, self.YT
        groups = [("x", g) for g in range(4)]
        if self.with_ctx(L):
            groups.append(("y", 0))
        hT = self.bf(KC * 512).rearrange("p (k n) -> p k n", k=KC)
        hT_r = S.R("hT")
        uT = self.bf(16 * 512).rearrange("p (j n) -> p j n", j=16)
        uT_r = [S.R(f"uT{j}") for j in range(16)]
        gT, gT_r = uT, uT_r
        vt = [(self.f32(2048), S.R(f"v{t}")) for t in range(4)]
        vn = [(self.bf(2048), S.R(f"vn{i}")) for i in range(4)]
        slots = [(self.bf(4096), S.R(f"wslot{i}")) for i in range(4)]
        wsT = self.bf(1024).rearrange("p (g n) -> p g n", g=8)
        wsT_r = S.R("wsT")
        bsb = self.f32(1024).rearrange("p (g n) -> p g n", g=8)
        bsb_r = S.R("bsb")
        BIAS = self.f32(2048).rearrange("p (c n) -> p c n", c=16)
        BIAS_r = S.R("BIAS")
        self.sqs = [(self.bf(512), S.R(f"sq{i}")) for i in range(2)]
        self.tmps = [(self.f32(512), S.R(f"tmp{i}")) for i in range(2)]
        self.rs_buf = (self.f32(512), S.R("rs"))
        stats = self.f32(112)
        stats_r = [S.R(f"stats{t}") for t in range(4)]
        sc_r = [S.R(f"sc{t}") for t in range(4)]
        mv = self.f32(8)
        mv_r = [S.R(f"mv{t}") for t in range(4)]
        self.SSB = 7
        MM = [0, 1, 2, 3, 4, 5]
        mmi = 0
        self.dma("pool", wsT, self.d_wsT[j_].rearrange("p (g n) -> p g n", g=8), [], [wsT_r], lane="wsT")
        self.dma("sp", bsb, self.d_bsb[j_].rearrange("p (g n) -> p g n", g=8), [], [bsb_r], lane="bsb")
        pb, pb_r = self.bank(6)
        for g in range(4):
            self.mm(pb[:, g * 128:(g + 1) * 128], self.ones, wsT[:, g, :], True, True, [self.ones_r, wsT_r], [pb_r])
        pb2, pb2_r = self.bank(5)
        for g in range(4):
            self.mm(pb2[:, g * 128:(g + 1) * 128], self.ones, wsT[:, 4 + g, :], True, True, [self.ones_r, wsT_r], [pb2_r])
        for c in range(16):
            g = c // 2
            src = (pb if g < 4 else pb2)[:, (g % 4) * 128:(g % 4 + 1) * 128]
            self.stt(BIAS[:, c, :], src, self.vcol(f"lnb{j_}", c), bsb[:, g, :], ALU.mult, ALU.add,
                     [pb_r, pb2_r, self.vecs_r, bsb_r], [BIAS_r])
        winsrc = self.d_win[j_].rearrange("(k p) n -> p k n", p=128)
        woutsrc = self.d_wout[j_].rearrange("(c p) n -> p c n", p=128)
        v1 = lambda ap: ap.rearrange("p (k n) -> p k n", k=KC)
        v3 = lambda ap: ap[:, 0:2048].rearrange("p (c n) -> p c n", c=16)
        specs = []
        for _ in groups:
            specs += [(winsrc[:, :, b * 512:(b + 1) * 512], v1) for b in range(8)]
            specs += [(woutsrc[:, :, dc * 128:(dc + 1) * 128], v3) for dc in range(8)]
        ws = Prog.WStream(self, "gmw", slots, specs)
        bi = 0
        for (kind, g) in groups:
            if kind == "x":
                N = 512
                src = (lambda g: (lambda k: XT[:, k, g * 512:(g + 1) * 512]))(g)
                src_r = self.XTr[g]
                which = 0
            else:
                N = NCTX
                src = lambda k: YT[:, k, :]
                src_r = self.YTr
                which = 1
            ntile = N // 128
            self.modulate(src, src_r, N, which, "m", hT, hT_r, 0)
            for b in range(4):
                w, w_r = ws.get(bi)
                bi += 1
                for jj in range(4):
                    j = b * 4 + jj
                    pb, pb_r = self.bank(MM[mmi % len(MM)])
                    mmi += 1
                    self.linear(pb, pb_r, N, (lambda k, w=w, jj=jj: w[:, k, jj * 128:(jj + 1) * 128]), w_r, hT, hT_r, 0)
                    self.act(uT[:, j, :N], pb[:, :N], AF.Gelu, [pb_r], [uT_r[j]])
            for cb in range(4):
                w, w_r = ws.get(bi)
                bi += 1
                for t in range(ntile):
                    pb, pb_r = self.bank(MM[mmi % len(MM)])
                    mmi += 1
                    for k in range(KC):
                        self.mm(pb[:, :], hT[:, k, t * 128:(t + 1) * 128], w[:, k, :], k == 0, k == KC - 1, [w_r, hT_r], [pb_r])
                    self.act(vt[t][0][:, cb * 512:(cb + 1) * 512], pb[:, :], AF.Gelu, [pb_r], [vt[t][1]])
            for t in range(ntile):
                v, v_r = vt[t]
                st = stats[:, t * 24:(t + 1) * 24].rearrange("p (a b) -> p a b", a=4)
                for q in range(4):
                    S.add("dve", (lambda e, st=st, v=v, q=q: e.bn_stats(out=st[:, q, :], in_=v[:, q * 512:(q + 1) * 512])),
                          [v_r], [stats_r[t]])
                m2 = mv[:, 2 * t:2 * t + 2]
                st2 = stats[:, t * 24:(t + 1) * 24]
                S.add("dve", (lambda e, m2=m2, st2=st2: e.bn_aggr(out=m2, in_=st2)), [stats_r[t]], [mv_r[t]])
            for t in range(ntile):
                sc = stats[:, 96 + 2 * t:96 + 2 * t + 1]
                self.act(sc, mv[:, 2 * t + 1:2 * t + 2], AF.Sqrt, [mv_r[t], self.eps_r], [sc_r[t]], scale=1.0, bias=self.eps_col)
            for t in range(ntile):
                sc = stats[:, 96 + 2 * t:96 + 2 * t + 1]
                nb = stats[:, 96 + 2 * t + 1:96 + 2 * t + 2]
                self.recip(sc, sc, [sc_r[t]], [sc_r[t]])
                self.stt(nb, mv[:, 2 * t:2 * t + 1], -1.0, sc, ALU.mult, ALU.mult, [mv_r[t], sc_r[t]], [sc_r[t]])
            for t in range(ntile):
                v, v_r = vt[t]
                sc = stats[:, 96 + 2 * t:96 + 2 * t + 1]
                nb = stats[:, 96 + 2 * t + 1:96 + 2 * t + 2]
                vnb, vnb_r = vn[t]
                self.act(vnb, v, AF.Identity, [v_r, sc_r[t]], [vnb_r], scale=sc, bias=nb)
            for t in range(ntile):
                vnb, vnb_r = vn[t]
                for cq in range(4):
                    pb, pb_r = self.bank(MM[mmi % len(MM)])
                    mmi += 1
                    for cc in range(4):
                        c = cq * 4 + cc
                        self.mm(pb[:, cc * 128:(cc + 1) * 128], vnb[:, c * 128:(c + 1) * 128], wsT[:, c // 2, :], True, True,
                                [vnb_r, wsT_r], [pb_r])
                    for cc in range(4):
                        c = cq * 4 + cc
                        tmp, tmp_r = self.tmps[c % 2]
                        self.stt(tmp[:, :128], pb[:, cc * 128:(cc + 1) * 128], self.vcol(f"lng{j_}", c), BIAS[:, c, :],
                                 ALU.mult, ALU.add, [pb_r, self.vecs_r, BIAS_r], [tmp_r])
                        self.tt(uT[:, c, t * 128:(t + 1) * 128], tmp[:, :128], uT[:, c, t * 128:(t + 1) * 128], ALU.mult,
                                [tmp_r, uT_r[c]], [uT_r[c]])
            for dc in range(8):
                w, w_r = ws.get(bi)
                bi += 1
                pb, pb_r = self.bank(MM[mmi % len(MM)])
                mmi += 1
                for c in range(16):
                    self.mm(pb[:, :N], w[:, c, :], gT[:, c, :N], c == 0, c == 15, [w_r, gT_r[c]], [pb_r])
                self.stt(src(dc), pb[:, :N], self.gate("m", dc, which), src(dc), ALU.mult, ALU.add,
                         [pb_r, self.mod_r, src_r], [src_r])

    def build_mla(self, L):
        S = self.S
        j_ = L // 2
        wctx = self.with_ctx(L)
        XT, YT = self.XT, self.YT
        self.phase()
        NQ = NT + NCTX
        cqn = self.bf(3 * NQ).rearrange("p (m n) -> p m n", m=3)
        cqn_r = [S.R(f"cqn{g}") for g in range(5)]
        ckvn = self.bf(2 * NKEY).rearrange("p (m n) -> p m n", m=2)
        ckvn_r = [S.R(f"ckvn{g}") for g in range(9)]
        Cb = self.bf(NKEY)
        Cb_r = [S.R(f"C{g}") for g in range(9)]
        sqpe = self.bf(NKEY)
        sqpe_r = [S.R(f"sqpe{g}") for g in range(9)]
        if F_PAD:
            S.add("pool", lambda e: e.memset(sqpe[64:128, :], 0.0), [], sqpe_r)
        keep_top = self.top
        hT = self.bf(KC * 512).rearrange("p (k n) -> p k n", k=KC)
        hT_r = S.R("hT")
        xTp = self.f32(KC * 512).rearrange("p (k n) -> p k n", k=KC)
        xTp_r = [S.R("xTp0"), S.R("xTp1")]
        self.stage = [(self.f32(2048).rearrange("p (t d) -> p t d", t=2), S.R(f"stage{i}")) for i in range(2)]
        wkva = self.bf(KC * 384).rearrange("p (k n) -> p k n", k=KC)
        wkva_r = S.R("wkva")
        wqa = self.bf(KC * 384).rearrange("p (k n) -> p k n", k=KC)
        wqa_r = S.R("wqa")
        self.sqs = [(self.bf(512), S.R(f"sq{i}")) for i in range(3)]
        self.tmps = [(self.f32(512), S.R(f"tmp{i}")) for i in range(2)]
        self.rs_buf = (self.f32(512), S.R("rs"))
        rs2 = (self.f32(512), S.R("rs2"))
        tabs = [(self.f32(1024).rearrange("p (a n) -> p a n", a=2), S.R(f"tab{i}")) for i in range(2)]
        self.TRB = [0, 1]
        self.SSB = 2
        GEN = [3, 4, 5, 6, 7]
        self.dma("pool", wkva, self.d_wkva[j_].rearrange("(k p) n -> p k n", p=128), [], [wkva_r], lane="wkva")
        self.dma("pool", wqa, self.d_wqa[j_].rearrange("(k p) n -> p k n", p=128), [], [wqa_r], lane="wqa")
        groups = [("own", g) for g in range(4)] + [("par", g) for g in range(4)] + [("ctx", 0)]
        ti = 0
        for gi, (kind, g) in enumerate(groups):
            N = 512 if kind != "ctx" else NCTX
            kcol = gi * 512
            if kind == "own":
                src = (lambda g: (lambda k: XT[:, k, g * 512:(g + 1) * 512]))(g)
                src_r, which = self.XTr[g], 0
            elif kind == "par":
                self.load_tokens(self.d_xpar[g * 512:(g + 1) * 512, :], 4,
                                 lambda k, off, n: xTp[:, k, off:off + n], lambda h: xTp_r[h])
                src = lambda k: xTp[:, k, :]
                src_r, which = None, 0
            else:
                src = lambda k: YT[:, k, :]
                src_r, which = self.YTr, 1
            src_rs = xTp_r if kind == "par" else [src_r]
            self._modulate_multi(src, src_rs, N, which, "m", hT, hT_r, 0)
            pk = [self.bank(GEN[0]), self.bank(GEN[1])]
            ssb, ssb_r = self.bank(GEN[2])
            for m in range(2):
                self.linear(pk[m][0], pk[m][1], N, (lambda k, m=m: wkva[:, k, m * 128:(m + 1) * 128]), wkva_r, hT, hT_r, 0)
            for m in range(2):
                sq, sq_r = self.sqs[m]
                self.act(sq[:, :N], pk[m][0][:, :N], AF.Square, [pk[m][1]], [sq_r])
                self.mm(ssb[:, :N], self.ones, sq[:, :N], m == 0, m == 1, [sq_r, self.ones_r], [ssb_r])
            rs, rs_r = rs2
            self.rsqrt_b(ssb, N, 1.0 / 256, rs, rs_r, ssb_r)
            for m in range(2):
                self.stt(ckvn[:, m, kcol:kcol + N], pk[m][0][:, :N], self.vcol(f"gkva{j_}", m), rs[:, :N], ALU.mult, ALU.mult,
                         [pk[m][1], self.vecs_r, rs_r], [ckvn_r[gi]])
            pp, pp_r = self.bank(GEN[3])
            pw, pw_r = self.bank(GEN[4])
            self.linear(pp, pp_r, N, lambda k: wkva[:, k, 256:320], wkva_r, hT, hT_r, 0, parts=64)
            self.linear(pw, pw_r, N, lambda k: wkva[:, k, 320:384], wkva_r, hT, hT_r, 0, parts=64)
            self.act(sqpe[0:64, kcol:kcol + N], pp[0:64, :N], AF.Square, [pp_r], [sqpe_r[gi]])
            tb, tab_r = tabs[ti % 2]
            self.dma("sp", tb[0:64, :, :N], self.d_rope[:, :, kcol:kcol + N], [], [tab_r], lane=f"tab{ti % 2}")
            ti += 1
            cosb, sinb = tb[:, 0, :], tb[:, 1, :]
            tA, tA_r = self.tmps[0]
            tB, tB_r = self.tmps[1]
            self.stt(tA[0:64, :N], pp[0:64, :N], self.vcol(f"gkp{j_}")[0:64, :], cosb[0:64, :N], ALU.mult, ALU.mult,
                     [pp_r, self.vecs_r, tab_r], [tA_r])
            self.stt(tB[0:64, :N], pw[0:64, :N], self.vcol(f"gks{j_}")[0:64, :], sinb[0:64, :N], ALU.mult, ALU.mult,
                     [pw_r, self.vecs_r, tab_r], [tB_r])
            self.tt(Cb[0:64, kcol:kcol + N], tA[0:64, :N], tB[0:64, :N], ALU.add, [tA_r, tB_r], [Cb_r[gi]], eng="pool")
            if kind == "own" or (kind == "ctx" and wctx):
                qcol = g * 512 if kind == "own" else NT
                qg = g if kind == "own" else 4
                pq = [self.bank(GEN[0]), self.bank(GEN[1]), self.bank(GEN[3])]
                ssq, ssq_r = self.bank(GEN[2])
                for m in range(3):
                    self.linear(pq[m][0], pq[m][1], N, (lambda k, m=m: wqa[:, k, m * 128:(m + 1) * 128]), wqa_r, hT, hT_r, 0)
                for m in range(3):
                    sq, sq_r = self.sqs[m]
                    self.act(sq[:, :N], pq[m][0][:, :N], AF.Square, [pq[m][1]], [sq_r])
                    self.mm(ssq[:, :N], self.ones, sq[:, :N], m == 0, m == 2, [sq_r, self.ones_r], [ssq_r])
                rs, rs_r = rs2
                self.rsqrt_b(ssq, N, 1.0 / 384, rs, rs_r, ssq_r)
                for m in range(3):
                    self.stt(cqn[:, m, qcol:qcol + N], pq[m][0][:, :N], self.vcol(f"gqa{j_}", m), rs[:, :N], ALU.mult, ALU.mult,
                             [pq[m][1], self.vecs_r, rs_r], [cqn_r[qg]])

        S.barrier()
        self.top = keep_top
        KTn = self.bf(NKEY)
        KTp = self.bf(NKEY)
        Vh = self.bf(34 * 128)
        KV_r = [S.R(f"KV{g}") for g in range(9)]
        KP_r = [S.R(f"KP{g}") for g in range(9)]
        VV_r = [S.R(f"VV{g}") for g in range(9)]
        QTn = self.bf(NQ)
        QTp = self.bf(NQ)
        Q_r = [S.R(f"Q{g}") for g in range(5)]
        oTh = self.bf(NQ)
        oT_r = [S.R(f"oT{g}") for g in range(5)]
        PT = [(self.bf(512), S.R(f"PT{i}")) for i in range(4)]
        rec = (self.f32(512), S.R("rec"))
        rs2 = (self.f32(512), S.R("rs2"))
        accD = (self.f32(512), S.R("accD"))
        accP = (self.f32(512), S.R("accP"))
        sumhi = (self.bf(512), S.R("sumhi"))
        sumlo = (self.bf(512), S.R("sumlo"))
        self.tmps = [(self.f32(512), S.R(f"tmp{i}")) for i in range(3)]
        sqk = [(self.bf(512), S.R(f"sqk{i}")) for i in range(2)]
        sqp = (self.bf(512), S.R("sqp"))
        tabs = [(self.f32(1024).rearrange("p (a n) -> p a n", a=2), S.R(f"tab{i}")) for i in range(2)]
        wh = [(self.bf(2304), S.R(f"wh{i}")) for i in range(2)]
        if F_PAD:
            S.add("pool", lambda e: e.memset(KTp[64:128, :], 0.0), [], KP_r)
            S.add("pool", lambda e: e.memset(QTp[64:128, :], 0.0), [], Q_r)
            S.add("pool", lambda e: e.memset(sqp[0][64:128, :], 0.0), [], [sqp[1]])
        SB = [0, 1, 2, 7]
        GB = [0, 1, 2, 7]
        OB = [3, 4]
        UB = [5, 6]
        qgroups = [(g, g * 512, 512, 0) for g in range(4)]
        if wctx:
            qgroups.append((4, NT, NCTX, 1))
        ti = 0
        oi = 0
        si = 0
        for h in range(HEADS):
            whb, wh_r = wh[h % 2]
            wkvb = whb[:, 0:512].rearrange("p (m n) -> p m n", m=2)
            wqb = whb[:, 512:1280].rearrange("p (m n) -> p m n", m=3)
            woh = whb[:, 1280:2304]
            lane = f"wh{h % 2}"
            self.dma("pool", wkvb, self.d_wkvb[j_].rearrange("(m p) n -> p m n", p=128)[:, :, h * 256:(h + 1) * 256], [], [wh_r], lane=lane)
            self.dma("pool", wqb, self.d_wqb[j_].rearrange("(m p) n -> p m n", p=128)[:, :, h * 256:(h + 1) * 256], [], [wh_r], lane=lane)
            self.dma("pool", woh, self.d_wo[j_][h * 128:(h + 1) * 128, :], [], [wh_r], lane=lane)
            gkn = self.vcol(f"gkn{j_}")
            gqn = self.vcol(f"gqn{j_}")
            for kg in range(9):
                N = 512 if kg < 8 else NCTX
                kcol = kg * 512
                pk, pk_r = self.bank([0, 3][kg % 2])
                ssb, ssb_r = self.bank([1, 4][kg % 2])
                pv, pv_r = self.bank([2, 5][kg % 2])
                for m in range(2):
                    self.mm(pk[:, :N], wkvb[:, m, 0:128], ckvn[:, m, kcol:kcol + N], m == 0, m == 1, [wh_r, ckvn_r[kg]], [pk_r])
                sq, sq_r = sqk[kg % 2]
                self.act(sq[:, :N], pk[:, :N], AF.Square, [pk_r], [sq_r])
                self.mm(ssb[:, :N], self.ones, sq[:, :N], True, False, [sq_r, self.ones_r], [ssb_r])
                if F_PAD:
                    self.mm(ssb[:, :N], self.ones, sqpe[:, kcol:kcol + N], False, True, [sqpe_r[kg], self.ones_r], [ssb_r])
                else:
                    self.mm(ssb[:, :N], self.ones[0:64, :], sqpe[0:64, kcol:kcol + N], False, True, [sqpe_r[kg], self.ones_r], [ssb_r])
                rs, rs_r = rs2
                self.rsqrt_b(ssb, N, 1.0 / 192, rs, rs_r, ssb_r)
                self.stt(KTn[:, kcol:kcol + N], pk[:, :N], gkn, rs[:, :N], ALU.mult, ALU.mult, [pk_r, self.vecs_r, rs_r], [KV_r[kg]])
                self.tt(KTp[0:64, kcol:kcol + N], Cb[0:64, kcol:kcol + N], rs[0:64, :N], ALU.mult, [Cb_r[kg], rs_r], [KP_r[kg]], eng="pool")
                for t in range(N // 128):
                    for m in range(2):
                        self.mm(pv[:, t * 128:(t + 1) * 128], ckvn[:, m, kcol + t * 128:kcol + (t + 1) * 128], wkvb[:, m, 128:256],
                                m == 0, m == 1, [wh_r, ckvn_r[kg]], [pv_r])
                self.copy("dve", Vh[:, kcol:kcol + N], pv[:, :N], [pv_r], [VV_r[kg]])
            for (qg, qcol, N, which) in qgroups:
                gset = [[0, 1, 2, 7], [3, 4, 5, 6]][qg % 2]
                pn, pn_r = self.bank(gset[0])
                pp, pp_r = self.bank(gset[1])
                pw, pw_r = self.bank(gset[2])
                ssb, ssb_r = self.bank(gset[3])
                for m in range(3):
                    self.mm(pn[:, :N], wqb[:, m, 0:128], cqn[:, m, qcol:qcol + N], m == 0, m == 2, [wh_r, cqn_r[qg]], [pn_r])
                for m in range(3):
                    self.mm(pp[0:64, :N], wqb[:, m, 128:192], cqn[:, m, qcol:qcol + N], m == 0, m == 2, [wh_r, cqn_r[qg]], [pp_r])
                for m in range(3):
                    self.mm(pw[0:64, :N], wqb[:, m, 192:256], cqn[:, m, qcol:qcol + N], m == 0, m == 2, [wh_r, cqn_r[qg]], [pw_r])
                sq, sq_r = sqk[qg % 2]
                self.act(sq[:, :N], pn[:, :N], AF.Square, [pn_r], [sq_r])
                sp_, sp_r = sqp
                self.act(sp_[0:64, :N], pp[0:64, :N], AF.Square, [pp_r], [sp_r])
                self.mm(ssb[:, :N], self.ones, sq[:, :N], True, False, [sq_r, self.ones_r], [ssb_r])
                if F_PAD:
                    self.mm(ssb[:, :N], self.ones, sp_[:, :N], False, True, [sp_r, self.ones_r], [ssb_r])
                else:
                    self.mm(ssb[:, :N], self.ones[0:64, :], sp_[0:64, :N], False, True, [sp_r, self.ones_r], [ssb_r])
                rs, rs_r = rs2
                self.rsqrt_b(ssb, N, 1.0 / 192, rs, rs_r, ssb_r)
                self.stt(QTn[:, qcol:qcol + N], pn[:, :N], gqn, rs[:, :N], ALU.mult, ALU.mult, [pn_r, self.vecs_r, rs_r], [Q_r[qg]])
                tb, tab_r = tabs[ti % 2]
                tcol = qcol if which == 0 else 2 * NT
                self.dma("sp", tb[0:64, :, :N], self.d_rope[:, :, tcol:tcol + N], [], [tab_r], lane=f"tab{ti % 2}")
                ti += 1
                cosb, sinb = tb[:, 0, :], tb[:, 1, :]
                tA, tA_r = self.tmps[0]
                tB, tB_r = self.tmps[1]
                tC, tC_r = self.tmps[2]
                self.stt(tA[0:64, :N], pp[0:64, :N], self.vcol(f"gqp{j_}")[0:64, :], cosb[0:64, :N], ALU.mult, ALU.mult,
                         [pp_r, self.vecs_r, tab_r], [tA_r])
                self.stt(tB[0:64, :N], pw[0:64, :N], self.vcol(f"gqs{j_}")[0:64, :], sinb[0:64, :N], ALU.mult, ALU.mult,
                         [pw_r, self.vecs_r, tab_r], [tB_r])
                self.tt(tC[0:64, :N], tA[0:64, :N], tB[0:64, :N], ALU.add, [tA_r, tB_r], [tC_r], eng="pool")
                self.tt(QTp[0:64, qcol:qcol + N], tC[0:64, :N], rs[0:64, :N], ALU.mult, [tC_r, rs_r], [Q_r[qg]], eng="pool")
            for (qg, qcol, N, which) in qgroups:
                tiles = list(range(34)) if which == 0 else [32, 33]
                po, po_r = self.bank(OB[oi % 2])
                pu, pu_r = self.bank(UB[oi % 2])
                oi += 1
                LOOK = 2
                nt_ = len(tiles)
                sbank = {}

                def emit_s(i):
                    nonlocal si
                    kt = tiles[i]
                    ps, ps_r = self.bank(SB[si % len(SB)])
                    si += 1
                    sbank[i] = (ps, ps_r)
                    kg = kt // 4
                    self.mm(ps[:, :N], KTn[:, kt * 128:(kt + 1) * 128], QTn[:, qcol:qcol + N], True, False, [KV_r[kg], Q_r[qg]], [ps_r])
                    if F_PAD:
                        self.mm(ps[:, :N], KTp[:, kt * 128:(kt + 1) * 128], QTp[:, qcol:qcol + N], False, True, [KP_r[kg], Q_r[qg]], [ps_r])
                    else:
                        self.mm(ps[:, :N], KTp[0:64, kt * 128:(kt + 1) * 128], QTp[0:64, qcol:qcol + N], False, True, [KP_r[kg], Q_r[qg]], [ps_r])

                for i in range(min(LOOK, nt_)):
                    emit_s(i)
                for i in range(nt_):
                    kt = tiles[i]
                    kg = kt // 4
                    ps, ps_r = sbank.pop(i)
                    pt, pt_r = PT[i % 4]
                    self.act(pt[:, :N], ps[:, :N], AF.Exp, [ps_r], [pt_r], scale=SCALE)
                    if i + LOOK < nt_:
                        emit_s(i + LOOK)
                    self.mm(po[:, :N], Vh[:, kt * 128:(kt + 1) * 128], pt[:, :N], i == 0, i == nt_ - 1, [VV_r[kg], pt_r], [po_r])
                    if not F_SUMS:
                        self.mm(pu[:, :N], self.ones, pt[:, :N], i == 0, i == nt_ - 1, [self.ones_r, pt_r], [pu_r])
                    else:
                        if i % 3 == 2:
                            self.mm(pu[:, :N], self.ones, pt[:, :N], i == 2, False, [self.ones_r, pt_r], [pu_r])
                        else:
                            acc, acc_r = accD
                            if i == 0:
                                self.copy("dve", acc[:, :N], pt[:, :N], [pt_r], [acc_r])
                            else:
                                self.tt(acc[:, :N], acc[:, :N], pt[:, :N], ALU.add, [acc_r, pt_r], [acc_r])
                if F_SUMS:
                    aD, aD_r = accD
                    hi, hi_r = sumhi
                    lo, lo_r = sumlo
                    self.copy("dve", hi[:, :N], aD[:, :N], [aD_r], [hi_r])
                    self.tt(lo[:, :N], aD[:, :N], hi[:, :N], ALU.subtract, [aD_r, hi_r], [lo_r])
                    self.mm(pu[:, :N], self.ones, hi[:, :N], nt_ <= 2, False, [self.ones_r, hi_r], [pu_r])
                    self.mm(pu[:, :N], self.ones, lo[:, :N], False, True, [self.ones_r, lo_r], [pu_r])
                rc, rc_r = rec
                if F_LN:
                    self.act(rc[:, :N], pu[:, :N], AF.Ln, [pu_r], [rc_r])
                    self.act(rc[:, :N], rc[:, :N], AF.Exp, [rc_r], [rc_r], scale=-1.0)
                else:
                    self.recip(rc[:, :N], pu[:, :N], [pu_r], [rc_r])
                self.tt(oTh[:, qcol:qcol + N], po[:, :N], rc[:, :N], ALU.mult, [po_r, rc_r], [oT_r[qg]])
            gi_ = 0
            for (qg, qcol, N, which) in qgroups:
                if which == 0:
                    dst = (lambda qg: (lambda k: XT[:, k, qg * 512:(qg + 1) * 512]))(qg)
                    dst_r = self.XTr[qg]
                else:
                    dst = lambda k: YT[:, k, :]
                    dst_r = self.YTr
                for dc in range(8):
                    pb, pb_r = self.bank(GB[gi_ % len(GB)])
                    gi_ += 1
                    self.mm(pb[:, :N], woh[:, dc * 128:(dc + 1) * 128], oTh[:, qcol:qcol + N], True, True, [wh_r, oT_r[qg]], [pb_r])
                    self.stt(dst(dc), pb[:, :N], self.gate("m", dc, which), dst(dc), ALU.mult, ALU.add,
                             [pb_r, self.mod_r, dst_r], [dst_r])

    def _modulate_multi(self, src, src_rs, N, which, kind, hT, hT_r, coff):
        A = self.am if kind == "m" else self.af
        shb = 0 if kind == "m" else 24
        ssb, ssb_r = self.bank(self.SSB)
        src_rs = list(src_rs)
        for k in range(KC):
            sq, sq_r = self.sqs[k % len(self.sqs)]
            self.act(sq[:, :N], src(k), AF.Square, src_rs, [sq_r])
            self.mm(ssb[:, :N], self.ones, sq[:, :N], k == 0, k == KC - 1, [sq_r, self.ones_r], [ssb_r])
        rs, rs_r = self.rs_buf
        self.rsqrt_b(ssb, N, 1.0 / D, rs, rs_r, ssb_r)
        for k in range(KC):
            tmp, tmp_r = self.tmps[k % len(self.tmps)]
            self.tt(tmp[:, :N], src(k), rs[:, :N], ALU.mult, src_rs + [rs_r], [tmp_r])
            self.act(hT[:, k, coff:coff + N], tmp[:, :N], AF.Identity, [tmp_r, self.mod_r], [hT_r],
                     scale=A[:, 2 * k + which:2 * k + which + 1],
                     bias=self.mod[:, 2 * (shb + k) + which:2 * (shb + k) + which + 1])

    def _emit(self):
        nc, S = self.nc, self.S
        sem = {}
        for e in ENGS:
            sem[("e", e)] = self.es.enter_context(nc.semaphore(f"s_{e}"))
        for l in S.lanes:
            sem[("l", l)] = self.es.enter_context(nc.semaphore(f"l_{l}"))
        by = {e: [op for op in S.ops if op.eng == e] for e in ENGS}
        block = self.es.enter_context(nc.Block())

        def body(ename):
            def f(eng):
                for op in by[ename]:
                    for key, val in op.waits:
                        eng.wait_ge(sem[key], val)
                    if op.fn is None:
                        continue
                    if op.lane is not None:
                        op.fn(eng, sem[("l", op.lane)])
                    else:
                        inst = op.fn(eng)
                        if op.sigkey is not None:
                            inst.then_inc(sem[("e", ename)], 1)
            return f

        block.tensor(body("pe"))
        block.scalar(body("act"))
        block.vector(body("dve"))
        block.gpsimd(body("pool"))
        block.sync(body("sp"))


def _cols(v, n):
    return np.ascontiguousarray(np.asarray(v, np.float32).reshape(n, 128).T)


def _dup(a):
    return np.repeat(a, 2, axis=1)


_SWAP = np.array([(((d // 16) ^ 1) * 16 + d % 16) for d in range(64)])

VOFF = {}
NV = 0


def _vec_layout():
    global NV
    off = 0

    def add(name, n):
        nonlocal off
        VOFF[name] = off
        off += n
    for i in range(4):
        add(f"adab{i}", 96)
        add(f"gmix{i}", 16)
        add(f"gffn{i}", 16)
    for j in range(2):
        add(f"gqa{j}", 3)
        add(f"gkva{j}", 2)
        for n in ("gqn", "gqp", "gqs", "gkn", "gkp", "gks"):
            add(f"{n}{j}", 1)
        add(f"lng{j}", 16)
        add(f"lnb{j}", 16)
    add("cvec", 16)
    NV = off


_vec_layout()


def _pad64(v):
    o = np.zeros((128, 1), np.float32)
    o[:64, 0] = v
    return o


def _build_vecs(inp, b):
    V = np.zeros((128, NV), np.float32)

    def put(name, a):
        V[:, VOFF[name]:VOFF[name] + a.shape[1]] = a
    for i in range(4):
        put(f"adab{i}", _dup(_cols(inp["ada_b"][i], 48)))
        put(f"gmix{i}", _dup(_cols(inp["norm_mix_g"][i], 8)))
        put(f"gffn{i}", _dup(_cols(inp["norm_ffn_g"][i], 8)))
    for j in range(2):
        put(f"gqa{j}", _cols(inp["mla_q_a_norm"][j], 3))
        put(f"gkva{j}", _cols(inp["mla_kv_a_norm"][j], 2))
        qn = np.asarray(inp["mla_q_norm"][j], np.float32)
        kn = np.asarray(inp["mla_k_norm"][j], np.float32)
        put(f"gqn{j}", qn[:128].reshape(128, 1))
        put(f"gqp{j}", _pad64(qn[128:]))
        put(f"gqs{j}", _pad64(qn[128:][_SWAP]))
        put(f"gkn{j}", kn[:128].reshape(128, 1))
        put(f"gkp{j}", _pad64(kn[128:]))
        put(f"gks{j}", _pad64(kn[128:][_SWAP]))
        put(f"lng{j}", _cols(inp["gm_ln_g"][j], 16))
        put(f"lnb{j}", _cols(inp["gm_ln_b"][j], 16))
    cv = np.stack([_cols(inp["c"][b], 8), _cols(inp["c_ctx"], 8)], axis=2).reshape(128, 16)
    put("cvec", cv)
    return V


def _rope_tables(half):
    pos_own = np.arange(half * NT, (half + 1) * NT)
    pos_par = np.arange((1 - half) * NT, (2 - half) * NT)
    pos = np.concatenate([pos_own, pos_par]).astype(np.float32)
    row = np.floor(pos / 64).astype(np.float32)
    col = (pos - row * 64).astype(np.float32)
    inv = (np.float32(10000.0) ** (-np.arange(0, 32, 2, dtype=np.float32) / np.float32(32))).astype(np.float32)
    ang_r = row[:, None] * inv[None, :]
    ang_c = col[:, None] * inv[None, :]
    ang = np.concatenate([ang_r, ang_r, ang_c, ang_c], axis=-1).astype(np.float32)
    cos = np.cos(ang).astype(np.float32)
    sin = np.sin(ang).astype(np.float32)
    sign = np.concatenate([-np.ones(16), np.ones(16), -np.ones(16), np.ones(16)]).astype(np.float32)
    sinS = sin * sign[None, :]
    cosT = np.concatenate([cos.T, np.ones((64, NCTX), np.float32)], axis=1)
    sinT = np.concatenate([sinS.T, np.zeros((64, NCTX), np.float32)], axis=1)
    return np.ascontiguousarray(np.stack([cosT, sinT], axis=1))


def _shared_weights(inp):
    f = lambda a: np.ascontiguousarray(np.asarray(a, np.float32))
    wkva = np.asarray(inp["mla_wkv_a"], np.float32)
    wkva2 = np.concatenate([wkva, wkva[:, :, 256:][:, :, _SWAP]], axis=2)
    wqb = np.asarray(inp["mla_wq_b"], np.float32).reshape(2, 384, 8, 192)
    wqb2 = np.concatenate([wqb, wqb[:, :, :, 128:][:, :, :, _SWAP]], axis=3).reshape(2, 384, 2048)
    bs = np.asarray(inp["gm_bs"], np.float32)
    bsb = np.broadcast_to(bs.reshape(2, 1, 1024), (2, 128, 1024))
    wsT = np.asarray(inp["gm_ws"], np.float32).transpose(0, 3, 1, 2).reshape(2, 128, 1024)
    return {
        "ident": np.eye(128, dtype=np.float32),
        "bsb": f(bsb), "wsT": f(wsT),
        "ada_w": f(inp["ada_w"]), "wq_a": f(inp["mla_wq_a"]), "wkv_a": f(wkva2), "wq_b": f(wqb2),
        "wkv_b": f(inp["mla_wkv_b"]), "wo": f(inp["mla_wo"]),
        "gm_w_in": f(inp["gm_w_in"]), "gm_w_out": f(inp["gm_w_out"]),
        "ffn_w1": f(inp["ffn_w1"]), "ffn_w2": f(inp["ffn_w2"]),
    }


_PROG_CACHE = {}


def _get_prog(layers, debug_stop=None):
    key = (tuple(layers), debug_stop)
    if key not in _PROG_CACHE:
        _PROG_CACHE[key] = Prog(list(layers), True, True, debug_stop)
    return _PROG_CACHE[key]


def run_layers(inp, x, y, layers, debug_stop=None, ncores=8):
    prog = _get_prog(layers, debug_stop)
    shared = _shared_weights(inp)
    in_maps = []
    for c in range(ncores):
        b, half = c // 2, c % 2
        rope = _rope_tables(half)
        m = dict(shared)
        m["x_own"] = np.ascontiguousarray(x[b, half * NT:(half + 1) * NT])
        m["x_par"] = np.ascontiguousarray(x[b, (1 - half) * NT:(2 - half) * NT])
        m["ctx"] = np.ascontiguousarray(y[b])
        m["vecs"] = _build_vecs(inp, b)
        m["rope"] = rope
        in_maps.append(m)
    res = run_bass_kernel_spmd(prog.nc, in_maps, core_ids=list(range(ncores)))
    xo = np.zeros_like(x)
    yo = np.zeros_like(y)
    if prog.dbg_names:
        run_layers.dbg = [{n: np.asarray(res.results[c][n]) for n in prog.dbg_names} for c in range(ncores)]
    for c in range(ncores):
        b, half = c // 2, c % 2
        xo[b, half * NT:(half + 1) * NT] = res.results[c]["out_x"]
        if half == 0:
            yo[b] = res.results[c]["out_y"]
    return xo, yo


def kernel(**inputs):
    inp = {k: np.asarray(v) for k, v in inputs.items()}
    x = np.ascontiguousarray(inp["x"], dtype=np.float32)
    y = np.ascontiguousarray(inp["ctx"], dtype=np.float32)
    x, y = run_layers(inp, x, y, (0, 1))
    x, y = run_layers(inp, x, y, (2, 3))
    return x.astype(np.float32)
```

```python
import numpy as np
from contextlib import ExitStack
import concourse.bass as bass
import concourse.mybir as mybir
from concourse.bass_utils import run_bass_kernel_spmd

F32 = mybir.dt.float32
BF16 = mybir.dt.bfloat16
AF = mybir.ActivationFunctionType
ALU = mybir.AluOpType

D = 1024
KC = 8
NT = 2048
NCTX = 256
NKEY = 2 * NT + NCTX
HEADS = 8
EPS = 1e-6
SCALE = 192 ** -0.5
ARENA_WORDS = 53000

ENGS = ("pe", "act", "dve", "pool", "sp")
import os as _os
F_LN = _os.environ.get("F_LN", "1") == "1"
F_PAD = _os.environ.get("F_PAD", "1") == "1"
F_SUMS = _os.environ.get("F_SUMS", "1") == "1"


class Res:
    __slots__ = ("name", "w", "r", "rd")

    def __init__(self, name):
        self.name = name
        self.w = None
        self.r = {}
        self.rd = []


class Op:
    __slots__ = ("eng", "fn", "deps", "lane", "ndma", "signal", "sigkey", "sigval", "waits", "known")

    def __init__(self, eng, fn, lane, ndma):
        self.eng = eng
        self.fn = fn
        self.lane = lane
        self.ndma = ndma
        self.signal = False
        self.sigkey = None
        self.sigval = 0
        self.waits = ()
        self.known = None
        self.deps = ()


class Sched:
    def __init__(self):
        self.ops = []
        self.res = []
        self.lanes = {}
        self.pending = {e: None for e in ENGS}
        self.last = {e: None for e in ENGS}
        self.lane_last = {}

    def R(self, name):
        r = Res(name)
        self.res.append(r)
        return r

    def add(self, eng, fn, reads=(), writes=(), lane=None, ndma=0):
        op = Op(eng, fn, lane, ndma)
        deps = set()
        for r in reads:
            if r.w is not None:
                deps.add(r.w)
        for w in writes:
            if w.w is not None:
                deps.add(w.w)
            deps.update(w.r.values())
            deps.update(w.rd)
        if self.pending[eng] is not None:
            deps.update(self.pending[eng])
            self.pending[eng] = None
        dl = []
        for d in deps:
            if d.eng == "pe" and eng == "pe" and d.lane is None and lane is None:
                continue
            d.signal = True
            dl.append(d)
        op.deps = dl
        for r in reads:
            if lane is not None:
                r.rd.append(op)
            else:
                r.r[eng] = op
        for w in writes:
            w.w = op
            w.r = {}
            w.rd = []
        self.ops.append(op)
        if lane is not None:
            if lane in self.lanes:
                assert self.lanes[lane] == eng, "one issuing engine per lane"
            self.lanes[lane] = eng
            self.lane_last[lane] = op
        else:
            self.last[eng] = op
        return op

    def barrier(self):
        outstanding = [o for o in self.last.values() if o is not None]
        outstanding += list(self.lane_last.values())
        for o in outstanding:
            o.signal = True
        for e in ENGS:
            cur = self.pending[e]
            self.pending[e] = list(outstanding) + (cur if cur else [])
        for r in self.res:
            r.w = None
            r.r = {}
            r.rd = []

    def fence(self, eng="sp"):
        self.barrier()
        self.add(eng, None)

    def finalize(self):
        ms = {e: 0 for e in ENGS}
        lc = {l: 0 for l in self.lanes}
        for op in self.ops:
            if op.lane is not None:
                lc[op.lane] += 16 * op.ndma
                op.sigkey = ("l", op.lane)
                op.sigval = lc[op.lane]
            elif op.signal and op.fn is not None:
                ms[op.eng] += 1
                op.sigkey = ("e", op.eng)
                op.sigval = ms[op.eng]
        seen = {e: {} for e in ENGS}
        for op in self.ops:
            s = seen[op.eng]
            waits = {}
            for d in op.deps:
                if d.sigkey is None:
                    continue
                if s.get(d.sigkey, 0) >= d.sigval:
                    continue
                if waits.get(d.sigkey, 0) < d.sigval:
                    waits[d.sigkey] = d.sigval
            for d in op.deps:
                if d.sigkey in waits and d.known is not None:
                    for k, v in d.known.items():
                        if s.get(k, 0) < v:
                            s[k] = v
            for k, v in waits.items():
                if s.get(k, 0) < v:
                    s[k] = v
            op.waits = tuple(waits.items())
            if op.sigkey is not None:
                op.known = dict(s)
        self.stats = (dict(ms), dict(lc))


class Prog:
    def __init__(self, layers, first, last, debug_stop=None):
        self.layers = layers
        self.debug_stop = debug_stop
        self.debug = debug_stop is not None
        self.dbg_names = []
        nc = bass.Bass("TRN2", target_bir_lowering=False)
        self.nc = nc
        self.S = Sched()
        self.es = ExitStack()
        self._declare_dram()
        with self.es:
            self._alloc()
            self._build()
            self.S.finalize()
            self._emit()

    def _declare_dram(self):
        nc = self.nc
        din = lambda n, s: nc.dram_tensor(n, s, F32, kind="ExternalInput").ap()
        self.d_xown = din("x_own", [NT, D])
        self.d_xpar = din("x_par", [NT, D])
        self.d_ctx = din("ctx", [NCTX, D])
        self.d_vecs = din("vecs", [128, NV])
        self.d_ident = din("ident", [128, 128])
        self.d_rope = din("rope", [64, 2, NKEY])
        self.d_bsb = din("bsb", [2, 128, 1024])
        self.d_wsT = din("wsT", [2, 128, 1024])
        self.d_adaw = din("ada_w", [4, D, 6 * D])
        self.d_wqa = din("wq_a", [2, D, 384])
        self.d_wkva = din("wkv_a", [2, D, 384])
        self.d_wqb = din("wq_b", [2, 384, 2048])
        self.d_wkvb = din("wkv_b", [2, 256, 2048])
        self.d_wo = din("wo", [2, D, D])
        self.d_win = din("gm_w_in", [2, D, 4 * D])
        self.d_wout = din("gm_w_out", [2, 2 * D, D])
        self.d_w1 = din("ffn_w1", [4, D, 4 * D])
        self.d_w2 = din("ffn_w2", [4, 4 * D, D])
        self.d_out = nc.dram_tensor("out_x", [NT, D], F32, kind="ExternalOutput").ap()
        self.d_outy = nc.dram_tensor("out_y", [NCTX, D], F32, kind="ExternalOutput").ap()

    def _alloc(self):
        nc, es, S = self.nc, self.es, self.S
        self.arena = es.enter_context(nc.sbuf_tensor("arena", [128, ARENA_WORDS], F32))
        self.arena_b = self.arena.bitcast(BF16)
        self.top = 0
        self.banks = []
        for i in range(8):
            t = es.enter_context(nc.psum_tensor(f"pb{i}", [128, 512], F32))
            self.banks.append((t, S.R(f"pb{i}")))
        self.XT = self.f32(KC * NT).rearrange("p (k n) -> p k n", k=KC)
        self.XTr = [S.R(f"XT{g}") for g in range(4)]
        self.YT = self.f32(KC * NCTX).rearrange("p (k n) -> p k n", k=KC)
        self.YTr = S.R("YT")
        self.vecs = self.f32(NV)
        self.vecs_r = S.R("vecs")
        self.ident = self.f32(128)
        self.ident_r = S.R("ident")
        self.ones = self.bf(128)
        self.ones_r = S.R("ones")
        self.mod = self.f32(96)
        self.am = self.f32(16)
        self.af = self.f32(16)
        self.mod_r = S.R("mod")
        self.scb = self.bf(16)
        self.scb_r = S.R("scb")
        self.persist_top = self.top

    def f32(self, n, parts=None):
        a = self.arena[:, self.top:self.top + n]
        self.top += n
        assert self.top <= ARENA_WORDS, f"SBUF overflow {self.top}"
        return a

    def bf(self, n):
        w = (n + 1) // 2
        a = self.arena_b[:, 2 * self.top:2 * self.top + n]
        self.top += w
        assert self.top <= ARENA_WORDS, f"SBUF overflow {self.top}"
        return a

    def phase(self):
        self.S.barrier()
        self.top = self.persist_top

    def mm(self, out, lhsT, rhs, start, stop, reads, writes):
        self.S.add("pe", lambda e: e.matmul(out, lhsT=lhsT, rhs=rhs, start=start, stop=stop), reads, writes)

    def tr(self, out, in_, reads, writes):
        ident = self.ident
        self.S.add("pe", lambda e: e.transpose(out, in_, ident), list(reads) + [self.ident_r], writes)

    def act(self, out, in_, func, reads, writes, scale=None, bias=None):
        kw = {}
        if scale is not None:
            kw["scale"] = scale
        if bias is not None:
            kw["bias"] = bias
        self.S.add("act", lambda e: e.activation(out=out, in_=in_, func=func, **kw), reads, writes)

    def stt(self, out, in0, scalar, in1, op0, op1, reads, writes, eng="dve"):
        self.S.add(eng, lambda e: e.scalar_tensor_tensor(out=out, in0=in0, scalar=scalar, in1=in1, op0=op0, op1=op1),
                   reads, writes)

    def tt(self, out, in0, in1, op, reads, writes, eng="dve"):
        self.S.add(eng, lambda e: e.tensor_tensor(out=out, in0=in0, in1=in1, op=op), reads, writes)

    def ts(self, out, in0, s1, s2, op0, op1, reads, writes, eng="dve"):
        self.S.add(eng, lambda e: e.tensor_scalar(out=out, in0=in0, scalar1=s1, scalar2=s2, op0=op0, op1=op1),
                   reads, writes)

    def recip(self, out, in_, reads, writes):
        self.S.add("dve", lambda e: e.reciprocal(out=out, in_=in_), reads, writes)

    def copy(self, eng, out, in_, reads, writes):
        if eng == "act":
            self.S.add("act", lambda e: e.copy(out=out, in_=in_), reads, writes)
        else:
            self.S.add(eng, lambda e: e.tensor_copy(out=out, in_=in_), reads, writes)

    def dma(self, eng, out, in_, reads, writes, lane):
        def fn(e, sem):
            e.dma_start(out=out, in_=in_).then_inc(sem, 16)
        self.S.add(eng, fn, reads, writes, lane=lane, ndma=1)

    def dump(self, name, ap, reads, dtype=F32):
        if not getattr(self, "debug", False):
            return
        shape = list(ap.shape)
        d = self.nc.dram_tensor("dbg_" + name, shape, dtype, kind="ExternalOutput").ap()
        self.dma("sp", d, ap, reads, [], lane="dbg_" + name)
        self.dbg_names.append("dbg_" + name)

    def vcol(self, name, j=0):
        o = VOFF[name] + j
        return self.vecs[:, o:o + 1]

    def vslice(self, name, n):
        o = VOFF[name]
        return self.vecs[:, o:o + n]

    def bank(self, i):
        return self.banks[i]

    class WStream:
        def __init__(self, prog, name, slots, specs):
            self.p = prog
            self.slots = slots
            self.specs = specs
            self.issued = 0
            self.name = name

        def get(self, n):
            ns = len(self.slots)
            while self.issued < len(self.specs) and self.issued <= n + ns - 1:
                m = self.issued
                ap, res = self.slots[m % ns]
                src, view = self.specs[m]
                self.p.dma("pool", view(ap), src, [], [res], lane=f"{self.name}{m % ns}")
                self.issued += 1
            ap, res = self.slots[n % ns]
            return self.specs[n][1](ap), res

    def rsqrt_b(self, ps, N, inv_n, rs, rs_r, ps_r, parts=128):
        if F_LN:
            self.act(rs[:parts, :N], ps[:parts, :N], AF.Ln, [ps_r, self.eps_r], [rs_r], scale=inv_n, bias=self.eps_col[:parts, :])
            self.act(rs[:parts, :N], rs[:parts, :N], AF.Exp, [rs_r], [rs_r], scale=-0.5)
        else:
            self.act(rs[:parts, :N], ps[:parts, :N], AF.Sqrt, [ps_r, self.eps_r], [rs_r], scale=inv_n, bias=self.eps_col[:parts, :])
            self.recip(rs[:parts, :N], rs[:parts, :N], [rs_r], [rs_r])

    def modulate(self, src, src_r, N, which, kind, hT, hT_r, coff):
        self._modulate_multi(src, [src_r], N, which, kind, hT, hT_r, coff)

    def linear(self, ps, ps_r, N, w, w_r, hT, hT_r, coff, nk=KC, parts=128):
        for k in range(nk):
            self.mm(ps[:parts, :N], w(k), hT[:, k, coff:coff + N], k == 0, k == nk - 1, [w_r, hT_r], [ps_r])

    def load_tokens(self, dram_rows, ntiles, dest, dest_r):
        for h in range(ntiles // 2):
            st, st_r = self.stage[h % 2]
            src = dram_rows[h * 256:(h + 1) * 256, :].rearrange("(t p) d -> p t d", p=128)
            self.dma("sp", st, src, [], [st_r], lane=f"stage{h % 2}")
            for k in range(KC):
                pb, pb_r = self.bank(self.TRB[k % 2])
                for t in range(2):
                    self.tr(pb[:, t * 128:(t + 1) * 128], st[:, t, k * 128:(k + 1) * 128], [st_r], [pb_r])
                self.copy("act" if k % 2 == 0 else "dve", dest(k, h * 256, 256), pb[:, 0:256], [pb_r], [dest_r(h)])

    def store_tokens(self, src, src_r, ntiles, dram_rows, lane):
        for h in range(ntiles // 2):
            st, st_r = self.stage[h % 2]
            for t in range(2):
                for half in range(2):
                    pb, pb_r = self.bank(self.TRB[(2 * t + half) % 2])
                    for kk in range(4):
                        k = half * 4 + kk
                        self.tr(pb[:, kk * 128:(kk + 1) * 128], src(k, h * 256 + t * 128, 128), [src_r(h)], [pb_r])
                    self.copy("act" if half == 0 else "dve", st[:, t, half * 512:(half + 1) * 512], pb[:, :], [pb_r], [st_r])
            dst = dram_rows[h * 256:(h + 1) * 256, :].rearrange("(t p) d -> p t d", p=128)
            self.dma("sp", dst, st, [st_r], [], lane=f"{lane}{h % 2}")

    def _build(self):
        S = self.S
        self.out_r = S.R("out")
        self.phase()
        self.eps_col = self.f32(1)
        self.eps_r = S.R("eps")
        S.add("dve", lambda e: e.memset(self.eps_col, EPS), [], [self.eps_r])
        self.persist_top = self.top
        S.add("dve", lambda e: e.memset(self.ones, 1.0), [], [self.ones_r])
        self.dma("sp", self.vecs, self.d_vecs, [], [self.vecs_r], lane="vecs")
        self.dma("sp", self.ident, self.d_ident, [], [self.ident_r], lane="ident")
        self.act(self.scb, self.vslice("cvec", 16), AF.Silu, [self.vecs_r], [self.scb_r])
        self.stage = []
        for i in range(2):
            self.stage.append((self.f32(2048).rearrange("p (t d) -> p t d", t=2), S.R(f"stage{i}")))
        self.TRB = [0, 1]
        XT, YT = self.XT, self.YT
        self.load_tokens(self.d_xown, 16, lambda k, off, n: XT[:, k, off:off + n], lambda h: self.XTr[h // 2])
        self.load_tokens(self.d_ctx, 2, lambda k, off, n: YT[:, k, off:off + n], lambda h: self.YTr)

        for li, L in enumerate(self.layers):
            self.build_mod(L)
            if self.debug_stop == ("mod", L):
                break
            if L % 2 == 0:
                self.build_mla(L)
            else:
                self.build_gmlp(L)
            self.dump(f"xt_mix{L}", self.XT, self.XTr)
            self.dump(f"yt_mix{L}", self.YT, [self.YTr])
            if self.debug_stop == ("mix", L):
                break
            self.build_ffn(L)
            self.dump(f"xt_ffn{L}", self.XT, self.XTr)
            self.dump(f"yt_ffn{L}", self.YT, [self.YTr])
            if self.debug_stop == ("ffn", L):
                break

        self.phase()
        self.stage = []
        for i in range(2):
            self.stage.append((self.f32(2048).rearrange("p (t d) -> p t d", t=2), S.R(f"ostage{i}")))
        self.TRB = [0, 1]
        self.store_tokens(lambda k, off, n: XT[:, k, off:off + n], lambda h: self.XTr[h // 2], 16, self.d_out, "ox")
        self.store_tokens(lambda k, off, n: YT[:, k, off:off + n], lambda h: self.YTr, 2, self.d_outy, "oy")
        S.fence("sp")

    def build_mod(self, L):
        S = self.S
        self.phase()
        slots = [(self.bf(4096), S.R(f"adaslot{i}")) for i in range(2)]
        wsrc = self.d_adaw[L].rearrange("(k p) n -> p k n", p=128)
        view = lambda ap: ap.rearrange("p (k n) -> p k n", k=KC)
        ws = Prog.WStream(self, "ada", slots, [(wsrc[:, :, b * 512:(b + 1) * 512], view) for b in range(12)])
        pb, pb_r = self.bank(2)
        for b in range(12):
            w, w_r = ws.get(b)
            for cc in range(4):
                ch = b * 4 + cc
                for k in range(KC):
                    self.mm(pb[:, 2 * ch:2 * ch + 2], w[:, k, cc * 128:(cc + 1) * 128], self.scb[:, 2 * k:2 * k + 2],
                            k == 0, k == KC - 1, [w_r, self.scb_r], [pb_r])
        o = VOFF[f"adab{L}"]
        self.tt(self.mod, pb[:, 0:96], self.vecs[:, o:o + 96], ALU.add, [pb_r, self.vecs_r], [self.mod_r])
        og = VOFF[f"gmix{L}"]
        self.stt(self.am, self.mod[:, 16:32], 1.0, self.vecs[:, og:og + 16], ALU.add, ALU.mult, [self.mod_r, self.vecs_r], [self.mod_r])
        og = VOFF[f"gffn{L}"]
        self.stt(self.af, self.mod[:, 64:80], 1.0, self.vecs[:, og:og + 16], ALU.add, ALU.mult, [self.mod_r, self.vecs_r], [self.mod_r])
        self.dump(f"mod{L}", self.mod, [self.mod_r])

    def gate(self, kind, k, which):
        base = 16 if kind == "m" else 40
        c = 2 * (base + k) + which
        return self.mod[:, c:c + 1]

    def with_ctx(self, L):
        return L < 2

    def build_ffn(self, L):
        S = self.S
        self.phase()
        XT, YT = self.XT, self.YT
        sgs = [[("x", 0), ("x", 1)], [("x", 2), ("x", 3)]]
        if self.with_ctx(L):
            sgs.append([("y", 0)])
        NSG = 1024
        hT = self.bf(KC * NSG).rearrange("p (k n) -> p k n", k=KC)
        hT_r = S.R("hT")
        aT = self.bf(32 * NSG).rearrange("p (j n) -> p j n", j=32)
        aT_r = [S.R(f"aT{j}") for j in range(32)]
        slots = [(self.bf(4096), S.R(f"wslot{i}")) for i in range(4)]
        self.sqs = [(self.bf(512), S.R(f"sq{i}")) for i in range(2)]
        self.tmps = [(self.f32(512), S.R(f"tmp{i}")) for i in range(2)]
        self.rs_buf = (self.f32(512), S.R("rs"))
        rl = [(self.f32(512), S.R(f"relu{i}")) for i in range(2)]
        self.SSB = 7
        MM = [0, 1, 2, 3, 4, 5]
        w1src = self.d_w1[L].rearrange("(k p) n -> p k n", p=128)
        w2src = self.d_w2[L].rearrange("(j p) n -> p j n", p=128)
        v1 = lambda ap: ap.rearrange("p (k n) -> p k n", k=KC)
        v2 = lambda ap: ap.rearrange("p (j n) -> p j n", j=32)
        specs = []
        for sg in sgs:
            specs += [(w1src[:, :, b * 512:(b + 1) * 512], v1) for b in range(8)]
            specs += [(w2src[:, :, dc * 128:(dc + 1) * 128], v2) for dc in range(8)]
        ws = Prog.WStream(self, "ffw", slots, specs)
        bi = 0
        mmi = 0
        for sg in sgs:
            subs = []
            off = 0
            for (kind, g) in sg:
                if kind == "x":
                    N = 512
                    src = (lambda g: (lambda k: XT[:, k, g * 512:(g + 1) * 512]))(g)
                    src_r = self.XTr[g]
                    which = 0
                else:
                    N = NCTX
                    src = lambda k: YT[:, k, :]
                    src_r = self.YTr
                    which = 1
                subs.append((src, src_r, N, which, off))
                off += N
            for (src, src_r, N, which, off) in subs:
                self.modulate(src, src_r, N, which, "f", hT, hT_r, off)
            for b in range(8):
                w, w_r = ws.get(bi)
                bi += 1
                for jj in range(4):
                    j = b * 4 + jj
                    for (src, src_r, N, which, off) in subs:
                        pb, pb_r = self.bank(MM[mmi % len(MM)])
                        mmi += 1
                        self.linear(pb, pb_r, N, (lambda k, w=w, jj=jj: w[:, k, jj * 128:(jj + 1) * 128]), w_r, hT, hT_r, off)
                        r, r_r = rl[mmi % 2]
                        self.act(r[:, :N], pb[:, :N], AF.Relu, [pb_r], [r_r])
                        self.tt(aT[:, j, off:off + N], r[:, :N], r[:, :N], ALU.mult, [r_r], [aT_r[j]])
            for dc in range(8):
                w, w_r = ws.get(bi)
                bi += 1
                for (src, src_r, N, which, off) in subs:
                    pb, pb_r = self.bank(MM[mmi % len(MM)])
                    mmi += 1
                    for j in range(32):
                        self.mm(pb[:, :N], w[:, j, :], aT[:, j, off:off + N], j == 0, j == 31, [w_r, aT_r[j]], [pb_r])
                    self.stt(src(dc), pb[:, :N], self.gate("f", dc, which), src(dc), ALU.mult, ALU.add,
                             [pb_r, self.mod_r, src_r], [src_r])

    def build_gmlp(self, L):
        S = self.S
        j_ = L // 2
        self.phase()
        XT, YT = self.XT, self.YT
        groups = [("x", g) for g in range(4)]
        if self.with_ctx(L):
            groups.append(("y", 0))
        hT = self.bf(KC * 512).rearrange("p (k n) -> p k n", k=KC)
        hT_r = S.R("hT")
        uT = self.bf(16 * 512).rearrange("p (j n) -> p j n", j=16)
        uT_r = [S.R(f"uT{j}") for j in range(16)]
        gT, gT_r = uT, uT_r
        vt = [(self.f32(2048), S.R(f"v{t}")) for t in range(4)]
        vn = [(self.bf(2048), S.R(f"vn{i}")) for i in range(4)]
        slots = [(self.bf(4096), S.R(f"wslot{i}")) for i in range(4)]
        wsT = self.bf(1024).rearrange("p (g n) -> p g n", g=8)
        wsT_r = S.R("wsT")
        bsb = self.f32(1024).rearrange("p (g n) -> p g n", g=8)
        bsb_r = S.R("bsb")
        BIAS = self.f32(2048).rearrange("p (c n) -> p c n", c=16)
        BIAS_r = S.R("BIAS")
        self.sqs = [(self.bf(512), S.R(f"sq{i}")) for i in range(2)]
        self.tmps = [(self.f32(512), S.R(f"tmp{i}")) for i in range(2)]
        self.rs_buf = (self.f32(512), S.R("rs"))
        stats = self.f32(112)
        stats_r = [S.R(f"stats{t}") for t in range(4)]
        sc_r = [S.R(f"sc{t}") for t in range(4)]
        mv = self.f32(8)
        mv_r = [S.R(f"mv{t}") for t in range(4)]
        self.SSB = 7
        MM = [0, 1, 2, 3, 4, 5]
        mmi = 0
        self.dma("pool", wsT, self.d_wsT[j_].rearrange("p (g n) -> p g n", g=8), [], [wsT_r], lane="wsT")
        self.dma("sp", bsb, self.d_bsb[j_].rearrange("p (g n) -> p g n", g=8), [], [bsb_r], lane="bsb")
        pb, pb_r = self.bank(6)
        for g in range(4):
            self.mm(pb[:, g * 128:(g + 1) * 128], self.ones, wsT[:, g, :], True, True, [self.ones_r, wsT_r], [pb_r])
        pb2, pb2_r = self.bank(5)
        for g in range(4):
            self.mm(pb2[:, g * 128:(g + 1) * 128], self.ones, wsT[:, 4 + g, :], True, True, [self.ones_r, wsT_r], [pb2_r])
        for c in range(16):
            g = c // 2
            src = (pb if g < 4 else pb2)[:, (g % 4) * 128:(g % 4 + 1) * 128]
            self.stt(BIAS[:, c, :], src, self.vcol(f"lnb{j_}", c), bsb[:, g, :], ALU.mult, ALU.add,
                     [pb_r, pb2_r, self.vecs_r, bsb_r], [BIAS_r])
        winsrc = self.d_win[j_].rearrange("(k p) n -> p k n", p=128)
        woutsrc = self.d_wout[j_].rearrange("(c p) n -> p c n", p=128)
        v1 = lambda ap: ap.rearrange("p (k n) -> p k n", k=KC)
        v3 = lambda ap: ap[:, 0:2048].rearrange("p (c n) -> p c n", c=16)
        specs = []
        for _ in groups:
            specs += [(winsrc[:, :, b * 512:(b + 1) * 512], v1) for b in range(8)]
            specs += [(woutsrc[:, :, dc * 128:(dc + 1) * 128], v3) for dc in range(8)]
        ws = Prog.WStream(self, "gmw", slots, specs)
        bi = 0
        for (kind, g) in groups:
            if kind == "x":
                N = 512
                src = (lambda g: (lambda k: XT[:, k, g * 512:(g + 1) * 512]))(g)
                src_r = self.XTr[g]
                which = 0
            else:
                N = NCTX
                src = lambda k: YT[:, k, :]
                src_r = self.YTr
                which = 1
            ntile = N // 128
            self.modulate(src, src_r, N, which, "m", hT, hT_r, 0)
            for b in range(4):
                w, w_r = ws.get(bi)
                bi += 1
                for jj in range(4):
                    j = b * 4 + jj
                    pb, pb_r = self.bank(MM[mmi % len(MM)])
                    mmi += 1
                    self.linear(pb, pb_r, N, (lambda k, w=w, jj=jj: w[:, k, jj * 128:(jj + 1) * 128]), w_r, hT, hT_r, 0)
                    self.act(uT[:, j, :N], pb[:, :N], AF.Gelu, [pb_r], [uT_r[j]])
            for cb in range(4):
                w, w_r = ws.get(bi)
                bi += 1
                for t in range(ntile):
                    pb, pb_r = self.bank(MM[mmi % len(MM)])
                    mmi += 1
                    for k in range(KC):
                        self.mm(pb[:, :], hT[:, k, t * 128:(t + 1) * 128], w[:, k, :], k == 0, k == KC - 1, [w_r, hT_r], [pb_r])
                    self.act(vt[t][0][:, cb * 512:(cb + 1) * 512], pb[:, :], AF.Gelu, [pb_r], [vt[t][1]])
            for t in range(ntile):
                v, v_r = vt[t]
                st = stats[:, t * 24:(t + 1) * 24].rearrange("p (a b) -> p a b", a=4)
                for q in range(4):
                    S.add("dve", (lambda e, st=st, v=v, q=q: e.bn_stats(out=st[:, q, :], in_=v[:, q * 512:(q + 1) * 512])),
                          [v_r], [stats_r[t]])
                m2 = mv[:, 2 * t:2 * t + 2]
                st2 = stats[:, t * 24:(t + 1) * 24]
                S.add("dve", (lambda e, m2=m2, st2=st2: e.bn_aggr(out=m2, in_=st2)), [stats_r[t]], [mv_r[t]])
            for t in range(ntile):
                sc = stats[:, 96 + 2 * t:96 + 2 * t + 1]
                self.act(sc, mv[:, 2 * t + 1:2 * t + 2], AF.Sqrt, [mv_r[t], self.eps_r], [sc_r[t]], scale=1.0, bias=self.eps_col)
            for t in range(ntile):
                sc = stats[:, 96 + 2 * t:96 + 2 * t + 1]
                nb = stats[:, 96 + 2 * t + 1:96 + 2 * t + 2]
                self.recip(sc, sc, [sc_r[t]], [sc_r[t]])
                self.stt(nb, mv[:, 2 * t:2 * t + 1], -1.0, sc, ALU.mult, ALU.mult, [mv_r[t], sc_r[t]], [sc_r[t]])
            for t in range(ntile):
                v, v_r = vt[t]
                sc = stats[:, 96 + 2 * t:96 + 2 * t + 1]
                nb = stats[:, 96 + 2 * t + 1:96 + 2 * t + 2]
                vnb, vnb_r = vn[t]
                self.act(vnb, v, AF.Identity, [v_r, sc_r[t]], [vnb_r], scale=sc, bias=nb)
            for t in range(ntile):
                vnb, vnb_r = vn[t]
                for cq in range(4):
                    pb, pb_r = self.bank(MM[mmi % len(MM)])
                    mmi += 1
                    for cc in range(4):
                        c = cq * 4 + cc
                        self.mm(pb[:, cc * 128:(cc + 1) * 128], vnb[:, c * 128:(c + 1) * 128], wsT[:, c // 2, :], True, True,
                                [vnb_r, wsT_r], [pb_r])
                    for cc in range(4):
                        c = cq * 4 + cc
                        tmp, tmp_r = self.tmps[c % 2]
                        self.stt(tmp[:, :128], pb[:, cc * 128:(cc + 1) * 128], self.vcol(f"lng{j_}", c), BIAS[:, c, :],
                                 ALU.mult, ALU.add, [pb_r, self.vecs_r, BIAS_r], [tmp_r])
                        self.tt(uT[:, c, t * 128:(t + 1) * 128], tmp[:, :128], uT[:, c, t * 128:(t + 1) * 128], ALU.mult,
                                [tmp_r, uT_r[c]], [uT_r[c]])
            for dc in range(8):
                w, w_r = ws.get(bi)
                bi += 1
                pb, pb_r = self.bank(MM[mmi % len(MM)])
                mmi += 1
                for c in range(16):
                    self.mm(pb[:, :N], w[:, c, :], gT[:, c, :N], c == 0, c == 15, [w_r, gT_r[c]], [pb_r])
                self.stt(src(dc), pb[:, :N], self.gate("m", dc, which), src(dc), ALU.mult, ALU.add,
                         [pb_r, self.mod_r, src_r], [src_r])

    def build_mla(self, L):
        S = self.S
        j_ = L // 2
        wctx = self.with_ctx(L)
        XT, YT = self.XT, self.YT
        self.phase()
        NQ = NT + NCTX
        cqn = self.bf(3 * NQ).rearrange("p (m n) -> p m n", m=3)
        cqn_r = [S.R(f"cqn{g}") for g in range(5)]
        ckvn = self.bf(2 * NKEY).rearrange("p (m n) -> p m n", m=2)
        ckvn_r = [S.R(f"ckvn{g}") for g in range(9)]
        Cb = self.bf(NKEY)
        Cb_r = [S.R(f"C{g}") for g in range(9)]
        sqpe = self.bf(NKEY)
        sqpe_r = [S.R(f"sqpe{g}") for g in range(9)]
        if F_PAD:
            S.add("pool", lambda e: e.memset(sqpe[64:128, :], 0.0), [], sqpe_r)
        keep_top = self.top
        hT = self.bf(KC * 512).rearrange("p (k n) -> p k n", k=KC)
        hT_r = S.R("hT")
        xTp = self.f32(KC * 512).rearrange("p (k n) -> p k n", k=KC)
        xTp_r = [S.R("xTp0"), S.R("xTp1")]
        self.stage = [(self.f32(2048).rearrange("p (t d) -> p t d", t=2), S.R(f"stage{i}")) for i in range(2)]
        wkva = self.bf(KC * 384).rearrange("p (k n) -> p k n", k=KC)
        wkva_r = S.R("wkva")
        wqa = self.bf(KC * 384).rearrange("p (k n) -> p k n", k=KC)
        wqa_r = S.R("wqa")
        self.sqs = [(self.bf(512), S.R(f"sq{i}")) for i in range(3)]
        self.tmps = [(self.f32(512), S.R(f"tmp{i}")) for i in range(2)]
        self.rs_buf = (self.f32(512), S.R("rs"))
        rs2 = (self.f32(512), S.R("rs2"))
        tabs = [(self.f32(1024).rearrange("p (a n) -> p a n", a=2), S.R(f"tab{i}")) for i in range(2)]
        self.TRB = [0, 1]
        self.SSB = 2
        GEN = [3, 4, 5, 6, 7]
        self.dma("pool", wkva, self.d_wkva[j_].rearrange("(k p) n -> p k n", p=128), [], [wkva_r], lane="wkva")
        self.dma("pool", wqa, self.d_wqa[j_].rearrange("(k p) n -> p k n", p=128), [], [wqa_r], lane="wqa")
        groups = [("own", g) for g in range(4)] + [("par", g) for g in range(4)] + [("ctx", 0)]
        ti = 0
        for gi, (kind, g) in enumerate(groups):
            N = 512 if kind != "ctx" else NCTX
            kcol = gi * 512
            if kind == "own":
                src = (lambda g: (lambda k: XT[:, k, g * 512:(g + 1) * 512]))(g)
                src_r, which = self.XTr[g], 0
            elif kind == "par":
                self.load_tokens(self.d_xpar[g * 512:(g + 1) * 512, :], 4,
                                 lambda k, off, n: xTp[:, k, off:off + n], lambda h: xTp_r[h])
                src = lambda k: xTp[:, k, :]
                src_r, which = None, 0
            else:
                src = lambda k: YT[:, k, :]
                src_r, which = self.YTr, 1
            src_rs = xTp_r if kind == "par" else [src_r]
            self._modulate_multi(src, src_rs, N, which, "m", hT, hT_r, 0)
            pk = [self.bank(GEN[0]), self.bank(GEN[1])]
            ssb, ssb_r = self.bank(GEN[2])
            for m in range(2):
                self.linear(pk[m][0], pk[m][1], N, (lambda k, m=m: wkva[:, k, m * 128:(m + 1) * 128]), wkva_r, hT, hT_r, 0)
            for m in range(2):
                sq, sq_r = self.sqs[m]
                self.act(sq[:, :N], pk[m][0][:, :N], AF.Square, [pk[m][1]], [sq_r])
                self.mm(ssb[:, :N], self.ones, sq[:, :N], m == 0, m == 1, [sq_r, self.ones_r], [ssb_r])
            rs, rs_r = rs2
            self.rsqrt_b(ssb, N, 1.0 / 256, rs, rs_r, ssb_r)
            for m in range(2):
                self.stt(ckvn[:, m, kcol:kcol + N], pk[m][0][:, :N], self.vcol(f"gkva{j_}", m), rs[:, :N], ALU.mult, ALU.mult,
                         [pk[m][1], self.vecs_r, rs_r], [ckvn_r[gi]])
            pp, pp_r = self.bank(GEN[3])
            pw, pw_r = self.bank(GEN[4])
            self.linear(pp, pp_r, N, lambda k: wkva[:, k, 256:320], wkva_r, hT, hT_r, 0, parts=64)
            self.linear(pw, pw_r, N, lambda k: wkva[:, k, 320:384], wkva_r, hT, hT_r, 0, parts=64)
            self.act(sqpe[0:64, kcol:kcol + N], pp[0:64, :N], AF.Square, [pp_r], [sqpe_r[gi]])
            tb, tab_r = tabs[ti % 2]
            self.dma("sp", tb[0:64, :, :N], self.d_rope[:, :, kcol:kcol + N], [], [tab_r], lane=f"tab{ti % 2}")
            ti += 1
            cosb, sinb = tb[:, 0, :], tb[:, 1, :]
            tA, tA_r = self.tmps[0]
            tB, tB_r = self.tmps[1]
            self.stt(tA[0:64, :N], pp[0:64, :N], self.vcol(f"gkp{j_}")[0:64, :], cosb[0:64, :N], ALU.mult, ALU.mult,
                     [pp_r, self.vecs_r, tab_r], [tA_r])
            self.stt(tB[0:64, :N], pw[0:64, :N], self.vcol(f"gks{j_}")[0:64, :], sinb[0:64, :N], ALU.mult, ALU.mult,
                     [pw_r, self.vecs_r, tab_r], [tB_r])
            self.tt(Cb[0:64, kcol:kcol + N], tA[0:64, :N], tB[0:64, :N], ALU.add, [tA_r, tB_r], [Cb_r[gi]], eng="pool")
            if kind == "own" or (kind == "ctx" and wctx):
                qcol = g * 512 if kind == "own" else NT
                qg = g if kind == "own" else 4
                pq = [self.bank(GEN[0]), self.bank(GEN[1]), self.bank(GEN[3])]
                ssq, ssq_r = self.bank(GEN[2])
                for m in range(3):
                    self.linear(pq[m][0], pq[m][1], N, (lambda k, m=m: wqa[:, k, m * 128:(m + 1) * 128]), wqa_r, hT, hT_r, 0)
                for m in range(3):
                    sq, sq_r = self.sqs[m]
                    self.act(sq[:, :N], pq[m][0][:, :N], AF.Square, [pq[m][1]], [sq_r])
                    self.mm(ssq[:, :N], self.ones, sq[:, :N], m == 0, m == 2, [sq_r, self.ones_r], [ssq_r])
                rs, rs_r = rs2
                self.rsqrt_b(ssq, N, 1.0 / 384, rs, rs_r, ssq_r)
                for m in range(3):
                    self.stt(cqn[:, m, qcol:qcol + N], pq[m][0][:, :N], self.vcol(f"gqa{j_}", m), rs[:, :N], ALU.mult, ALU.mult,
                             [pq[m][1], self.vecs_r, rs_r], [cqn_r[qg]])

        S.barrier()
        self.top = keep_top
        KTn = self.bf(NKEY)
        KTp = self.bf(NKEY)
        Vh = self.bf(34 * 128)
        KV_r = [S.R(f"KV{g}") for g in range(9)]
        KP_r = [S.R(f"KP{g}") for g in range(9)]
        VV_r = [S.R(f"VV{g}") for g in range(9)]
        QTn = self.bf(NQ)
        QTp = self.bf(NQ)
        Q_r = [S.R(f"Q{g}") for g in range(5)]
        oTh = self.bf(NQ)
        oT_r = [S.R(f"oT{g}") for g in range(5)]
        PT = [(self.bf(512), S.R(f"PT{i}")) for i in range(4)]
        rec = (self.f32(512), S.R("rec"))
        rs2 = (self.f32(512), S.R("rs2"))
        accD = (self.f32(512), S.R("accD"))
        accP = (self.f32(512), S.R("accP"))
        sumhi = (self.bf(512), S.R("sumhi"))
        sumlo = (self.bf(512), S.R("sumlo"))
        self.tmps = [(self.f32(512), S.R(f"tmp{i}")) for i in range(3)]
        sqk = [(self.bf(512), S.R(f"sqk{i}")) for i in range(2)]
        sqp = (self.bf(512), S.R("sqp"))
        sqp2 = (self.bf(512), S.R("sqp2"))
        sqps = [sqp, sqp2]
        tabs = [(self.f32(1024).rearrange("p (a n) -> p a n", a=2), S.R(f"tab{i}")) for i in range(2)]
        wh = [(self.bf(2304), S.R(f"wh{i}")) for i in range(2)]
        if F_PAD:
            S.add("pool", lambda e: e.memset(KTp[64:128, :], 0.0), [], KP_r)
            S.add("pool", lambda e: e.memset(QTp[64:128, :], 0.0), [], Q_r)
            S.add("pool", lambda e: e.memset(sqp[0][64:128, :], 0.0), [], [sqp[1]])
            S.add("pool", lambda e: e.memset(sqp2[0][64:128, :], 0.0), [], [sqp2[1]])
        SB = [0, 1, 2, 7]
        GB = [0, 1, 2, 7]
        OB = [3, 4]
        UB = [5, 6]
        qgroups = [(g, g * 512, 512, 0) for g in range(4)]
        if wctx:
            qgroups.append((4, NT, NCTX, 1))
        ti = 0
        oi = 0
        si = 0
        for h in range(HEADS):
            whb, wh_r = wh[h % 2]
            wkvb = whb[:, 0:512].rearrange("p (m n) -> p m n", m=2)
            wqb = whb[:, 512:1280].rearrange("p (m n) -> p m n", m=3)
            woh = whb[:, 1280:2304]
            lane = f"wh{h % 2}"
            self.dma("pool", wkvb, self.d_wkvb[j_].rearrange("(m p) n -> p m n", p=128)[:, :, h * 256:(h + 1) * 256], [], [wh_r], lane=lane)
            self.dma("pool", wqb, self.d_wqb[j_].rearrange("(m p) n -> p m n", p=128)[:, :, h * 256:(h + 1) * 256], [], [wh_r], lane=lane)
            self.dma("pool", woh, self.d_wo[j_][h * 128:(h + 1) * 128, :], [], [wh_r], lane=lane)
            gkn = self.vcol(f"gkn{j_}")
            gqn = self.vcol(f"gqn{j_}")
            rsb = [rs2, accP]

            def k_front(kg):
                N = 512 if kg < 8 else NCTX
                kcol = kg * 512
                pk, pk_r = self.bank([0, 3, 6][kg % 3])
                for m in range(2):
                    self.mm(pk[:, :N], wkvb[:, m, 0:128], ckvn[:, m, kcol:kcol + N], m == 0, m == 1, [wh_r, ckvn_r[kg]], [pk_r])
                sq, sq_r = sqk[kg % 2]
                self.act(sq[:, :N], pk[:, :N], AF.Square, [pk_r], [sq_r])

            k_front(0)
            for kg in range(9):
                N = 512 if kg < 8 else NCTX
                kcol = kg * 512
                pk, pk_r = self.bank([0, 3, 6][kg % 3])
                ssb, ssb_r = self.bank([1, 4, 7][kg % 3])
                pv, pv_r = self.bank([2, 5][kg % 2])
                sq, sq_r = sqk[kg % 2]
                if kg + 1 < 9:
                    k_front(kg + 1)
                for t in range(N // 128):
                    for m in range(2):
                        self.mm(pv[:, t * 128:(t + 1) * 128], ckvn[:, m, kcol + t * 128:kcol + (t + 1) * 128], wkvb[:, m, 128:256],
                                m == 0, m == 1, [wh_r, ckvn_r[kg]], [pv_r])
                self.mm(ssb[:, :N], self.ones, sq[:, :N], True, False, [sq_r, self.ones_r], [ssb_r])
                self.mm(ssb[:, :N], self.ones, sqpe[:, kcol:kcol + N], False, True, [sqpe_r[kg], self.ones_r], [ssb_r])
                rs, rs_r = rsb[kg % 2]
                self.rsqrt_b(ssb, N, 1.0 / 192, rs, rs_r, ssb_r)
                self.stt(KTn[:, kcol:kcol + N], pk[:, :N], gkn, rs[:, :N], ALU.mult, ALU.mult, [pk_r, self.vecs_r, rs_r], [KV_r[kg]])
                self.tt(KTp[0:64, kcol:kcol + N], Cb[0:64, kcol:kcol + N], rs[0:64, :N], ALU.mult, [Cb_r[kg], rs_r], [KP_r[kg]], eng="pool")
                self.copy("dve", Vh[:, kcol:kcol + N], pv[:, :N], [pv_r], [VV_r[kg]])
            qtab = {}

            def q_front(idx):
                nonlocal ti
                (qg, qcol, N, which) = qgroups[idx]
                gset = [[0, 1, 2, 7], [3, 4, 5, 6]][idx % 2]
                pn, pn_r = self.bank(gset[0])
                pp, pp_r = self.bank(gset[1])
                pw, pw_r = self.bank(gset[2])
                for m in range(3):
                    self.mm(pn[:, :N], wqb[:, m, 0:128], cqn[:, m, qcol:qcol + N], m == 0, m == 2, [wh_r, cqn_r[qg]], [pn_r])
                for m in range(3):
                    self.mm(pp[0:64, :N], wqb[:, m, 128:192], cqn[:, m, qcol:qcol + N], m == 0, m == 2, [wh_r, cqn_r[qg]], [pp_r])
                for m in range(3):
                    self.mm(pw[0:64, :N], wqb[:, m, 192:256], cqn[:, m, qcol:qcol + N], m == 0, m == 2, [wh_r, cqn_r[qg]], [pw_r])
                sq, sq_r = sqk[idx % 2]
                self.act(sq[:, :N], pn[:, :N], AF.Square, [pn_r], [sq_r])
                sp_, sp_r = sqps[idx % 2]
                self.act(sp_[0:64, :N], pp[0:64, :N], AF.Square, [pp_r], [sp_r])
                tb, tab_r = tabs[ti % 2]
                tcol = qcol if which == 0 else 2 * NT
                self.dma("sp", tb[0:64, :, :N], self.d_rope[:, :, tcol:tcol + N], [], [tab_r], lane=f"tab{ti % 2}")
                ti += 1
                qtab[idx] = (tb, tab_r)

            q_front(0)
            for idx, (qg, qcol, N, which) in enumerate(qgroups):
                gset = [[0, 1, 2, 7], [3, 4, 5, 6]][idx % 2]
                pn, pn_r = self.bank(gset[0])
                pp, pp_r = self.bank(gset[1])
                pw, pw_r = self.bank(gset[2])
                ssb, ssb_r = self.bank(gset[3])
                sq, sq_r = sqk[idx % 2]
                sp_, sp_r = sqps[idx % 2]
                if idx + 1 < len(qgroups):
                    q_front(idx + 1)
                self.mm(ssb[:, :N], self.ones, sq[:, :N], True, False, [sq_r, self.ones_r], [ssb_r])
                self.mm(ssb[:, :N], self.ones, sp_[:, :N], False, True, [sp_r, self.ones_r], [ssb_r])
                rs, rs_r = rsb[idx % 2]
                self.rsqrt_b(ssb, N, 1.0 / 192, rs, rs_r, ssb_r)
                self.stt(QTn[:, qcol:qcol + N], pn[:, :N], gqn, rs[:, :N], ALU.mult, ALU.mult, [pn_r, self.vecs_r, rs_r], [Q_r[qg]])
                tb, tab_r = qtab.pop(idx)
                cosb, sinb = tb[:, 0, :], tb[:, 1, :]
                tA, tA_r = self.tmps[0]
                tB, tB_r = self.tmps[1]
                tC, tC_r = self.tmps[2]
                self.stt(tA[0:64, :N], pp[0:64, :N], self.vcol(f"gqp{j_}")[0:64, :], cosb[0:64, :N], ALU.mult, ALU.mult,
                         [pp_r, self.vecs_r, tab_r], [tA_r])
                self.stt(tB[0:64, :N], pw[0:64, :N], self.vcol(f"gqs{j_}")[0:64, :], sinb[0:64, :N], ALU.mult, ALU.mult,
                         [pw_r, self.vecs_r, tab_r], [tB_r])
                self.tt(tC[0:64, :N], tA[0:64, :N], tB[0:64, :N], ALU.add, [tA_r, tB_r], [tC_r], eng="pool")
                self.tt(QTp[0:64, qcol:qcol + N], tC[0:64, :N], rs[0:64, :N], ALU.mult, [tC_r, rs_r], [Q_r[qg]], eng="pool")
            for (qg, qcol, N, which) in qgroups:
                tiles = list(range(34)) if which == 0 else [32, 33]
                po, po_r = self.bank(OB[oi % 2])
                pu, pu_r = self.bank(UB[oi % 2])
                oi += 1
                LOOK = 2
                nt_ = len(tiles)
                sbank = {}

                def emit_s(i):
                    nonlocal si
                    kt = tiles[i]
                    ps, ps_r = self.bank(SB[si % len(SB)])
                    si += 1
                    sbank[i] = (ps, ps_r)
                    kg = kt // 4
                    self.mm(ps[:, :N], KTn[:, kt * 128:(kt + 1) * 128], QTn[:, qcol:qcol + N], True, False, [KV_r[kg], Q_r[qg]], [ps_r])
                    if F_PAD:
                        self.mm(ps[:, :N], KTp[:, kt * 128:(kt + 1) * 128], QTp[:, qcol:qcol + N], False, True, [KP_r[kg], Q_r[qg]], [ps_r])
                    else:
                        self.mm(ps[:, :N], KTp[0:64, kt * 128:(kt + 1) * 128], QTp[0:64, qcol:qcol + N], False, True, [KP_r[kg], Q_r[qg]], [ps_r])

                for i in range(min(LOOK, nt_)):
                    emit_s(i)
                for i in range(nt_):
                    kt = tiles[i]
                    kg = kt // 4
                    ps, ps_r = sbank.pop(i)
                    pt, pt_r = PT[i % 4]
                    self.act(pt[:, :N], ps[:, :N], AF.Exp, [ps_r], [pt_r], scale=SCALE)
                    if i + LOOK < nt_:
                        emit_s(i + LOOK)
                    self.mm(po[:, :N], Vh[:, kt * 128:(kt + 1) * 128], pt[:, :N], i == 0, i == nt_ - 1, [VV_r[kg], pt_r], [po_r])
                    if not F_SUMS:
                        self.mm(pu[:, :N], self.ones, pt[:, :N], i == 0, i == nt_ - 1, [self.ones_r, pt_r], [pu_r])
                    else:
                        if i % 3 == 2:
                            self.mm(pu[:, :N], self.ones, pt[:, :N], i == 2, False, [self.ones_r, pt_r], [pu_r])
                        else:
                            acc, acc_r = accD
                            if i == 0:
                                self.copy("dve", acc[:, :N], pt[:, :N], [pt_r], [acc_r])
                            else:
                                self.tt(acc[:, :N], acc[:, :N], pt[:, :N], ALU.add, [acc_r, pt_r], [acc_r])
                if F_SUMS:
                    aD, aD_r = accD
                    hi, hi_r = sumhi
                    lo, lo_r = sumlo
                    self.copy("dve", hi[:, :N], aD[:, :N], [aD_r], [hi_r])
                    self.tt(lo[:, :N], aD[:, :N], hi[:, :N], ALU.subtract, [aD_r, hi_r], [lo_r])
                    self.mm(pu[:, :N], self.ones, hi[:, :N], nt_ <= 2, False, [self.ones_r, hi_r], [pu_r])
                    self.mm(pu[:, :N], self.ones, lo[:, :N], False, True, [self.ones_r, lo_r], [pu_r])
                rc, rc_r = rec
                if F_LN:
                    self.act(rc[:, :N], pu[:, :N], AF.Ln, [pu_r], [rc_r])
                    self.act(rc[:, :N], rc[:, :N], AF.Exp, [rc_r], [rc_r], scale=-1.0)
                else:
                    self.recip(rc[:, :N], pu[:, :N], [pu_r], [rc_r])
                self.tt(oTh[:, qcol:qcol + N], po[:, :N], rc[:, :N], ALU.mult, [po_r, rc_r], [oT_r[qg]])
            gi_ = 0
            for (qg, qcol, N, which) in qgroups:
                if which == 0:
                    dst = (lambda qg: (lambda k: XT[:, k, qg * 512:(qg + 1) * 512]))(qg)
                    dst_r = self.XTr[qg]
                else:
                    dst = lambda k: YT[:, k, :]
                    dst_r = self.YTr
                for dc in range(8):
                    pb, pb_r = self.bank(GB[gi_ % len(GB)])
                    gi_ += 1
                    self.mm(pb[:, :N], woh[:, dc * 128:(dc + 1) * 128], oTh[:, qcol:qcol + N], True, True, [wh_r, oT_r[qg]], [pb_r])
                    self.stt(dst(dc), pb[:, :N], self.gate("m", dc, which), dst(dc), ALU.mult, ALU.add,
                             [pb_r, self.mod_r, dst_r], [dst_r])

    def _modulate_multi(self, src, src_rs, N, which, kind, hT, hT_r, coff):
        A = self.am if kind == "m" else self.af
        shb = 0 if kind == "m" else 24
        ssb, ssb_r = self.bank(self.SSB)
        src_rs = list(src_rs)
        for k in range(KC):
            sq, sq_r = self.sqs[k % len(self.sqs)]
            self.act(sq[:, :N], src(k), AF.Square, src_rs, [sq_r])
            self.mm(ssb[:, :N], self.ones, sq[:, :N], k == 0, k == KC - 1, [sq_r, self.ones_r], [ssb_r])
        rs, rs_r = self.rs_buf
        self.rsqrt_b(ssb, N, 1.0 / D, rs, rs_r, ssb_r)
        for k in range(KC):
            tmp, tmp_r = self.tmps[k % len(self.tmps)]
            self.tt(tmp[:, :N], src(k), rs[:, :N], ALU.mult, src_rs + [rs_r], [tmp_r])
            self.act(hT[:, k, coff:coff + N], tmp[:, :N], AF.Identity, [tmp_r, self.mod_r], [hT_r],
                     scale=A[:, 2 * k + which:2 * k + which + 1],
                     bias=self.mod[:, 2 * (shb + k) + which:2 * (shb + k) + which + 1])

    def _emit(self):
        nc, S = self.nc, self.S
        sem = {}
        for e in ENGS:
            sem[("e", e)] = self.es.enter_context(nc.semaphore(f"s_{e}"))
        for l in S.lanes:
            sem[("l", l)] = self.es.enter_context(nc.semaphore(f"l_{l}"))
        by = {e: [op for op in S.ops if op.eng == e] for e in ENGS}
        block = self.es.enter_context(nc.Block())

        def body(ename):
            def f(eng):
                for op in by[ename]:
                    for key, val in op.waits:
                        eng.wait_ge(sem[key], val)
                    if op.fn is None:
                        continue
                    if op.lane is not None:
                        op.fn(eng, sem[("l", op.lane)])
                    else:
                        inst = op.fn(eng)
                        if op.sigkey is not None:
                            inst.then_inc(sem[("e", ename)], 1)
            return f

        block.tensor(body("pe"))
        block.scalar(body("act"))
        block.vector(body("dve"))
        block.gpsimd(body("pool"))
        block.sync(body("sp"))


def _cols(v, n):
    return np.ascontiguousarray(np.asarray(v, np.float32).reshape(n, 128).T)


def _dup(a):
    return np.repeat(a, 2, axis=1)


_SWAP = np.array([(((d // 16) ^ 1) * 16 + d % 16) for d in range(64)])

VOFF = {}
NV = 0


def _vec_layout():
    global NV
    off = 0

    def add(name, n):
        nonlocal off
        VOFF[name] = off
        off += n
    for i in range(4):
        add(f"adab{i}", 96)
        add(f"gmix{i}", 16)
        add(f"gffn{i}", 16)
    for j in range(2):
        add(f"gqa{j}", 3)
        add(f"gkva{j}", 2)
        for n in ("gqn", "gqp", "gqs", "gkn", "gkp", "gks"):
            add(f"{n}{j}", 1)
        add(f"lng{j}", 16)
        add(f"lnb{j}", 16)
    add("cvec", 16)
    NV = off


_vec_layout()


def _pad64(v):
    o = np.zeros((128, 1), np.float32)
    o[:64, 0] = v
    return o


def _build_vecs(inp, b):
    V = np.zeros((128, NV), np.float32)

    def put(name, a):
        V[:, VOFF[name]:VOFF[name] + a.shape[1]] = a
    for i in range(4):
        put(f"adab{i}", _dup(_cols(inp["ada_b"][i], 48)))
        put(f"gmix{i}", _dup(_cols(inp["norm_mix_g"][i], 8)))
        put(f"gffn{i}", _dup(_cols(inp["norm_ffn_g"][i], 8)))
    for j in range(2):
        put(f"gqa{j}", _cols(inp["mla_q_a_norm"][j], 3))
        put(f"gkva{j}", _cols(inp["mla_kv_a_norm"][j], 2))
        qn = np.asarray(inp["mla_q_norm"][j], np.float32)
        kn = np.asarray(inp["mla_k_norm"][j], np.float32)
        put(f"gqn{j}", qn[:128].reshape(128, 1))
        put(f"gqp{j}", _pad64(qn[128:]))
        put(f"gqs{j}", _pad64(qn[128:][_SWAP]))
        put(f"gkn{j}", kn[:128].reshape(128, 1))
        put(f"gkp{j}", _pad64(kn[128:]))
        put(f"gks{j}", _pad64(kn[128:][_SWAP]))
        put(f"lng{j}", _cols(inp["gm_ln_g"][j], 16))
        put(f"lnb{j}", _cols(inp["gm_ln_b"][j], 16))
    cv = np.stack([_cols(inp["c"][b], 8), _cols(inp["c_ctx"], 8)], axis=2).reshape(128, 16)
    put("cvec", cv)
    return V


def _rope_tables(half):
    pos_own = np.arange(half * NT, (half + 1) * NT)
    pos_par = np.arange((1 - half) * NT, (2 - half) * NT)
    pos = np.concatenate([pos_own, pos_par]).astype(np.float32)
    row = np.floor(pos / 64).astype(np.float32)
    col = (pos - row * 64).astype(np.float32)
    inv = (np.float32(10000.0) ** (-np.arange(0, 32, 2, dtype=np.float32) / np.float32(32))).astype(np.float32)
    ang_r = row[:, None] * inv[None, :]
    ang_c = col[:, None] * inv[None, :]
    ang = np.concatenate([ang_r, ang_r, ang_c, ang_c], axis=-1).astype(np.float32)
    cos = np.cos(ang).astype(np.float32)
    sin = np.sin(ang).astype(np.float32)
    sign = np.concatenate([-np.ones(16), np.ones(16), -np.ones(16), np.ones(16)]).astype(np.float32)
    sinS = sin * sign[None, :]
    cosT = np.concatenate([cos.T, np.ones((64, NCTX), np.float32)], axis=1)
    sinT = np.concatenate([sinS.T, np.zeros((64, NCTX), np.float32)], axis=1)
    return np.ascontiguousarray(np.stack([cosT, sinT], axis=1))


def _shared_weights(inp):
    f = lambda a: np.ascontiguousarray(np.asarray(a, np.float32))
    wkva = np.asarray(inp["mla_wkv_a"], np.float32)
    wkva2 = np.concatenate([wkva, wkva[:, :, 256:][:, :, _SWAP]], axis=2)
    wqb = np.asarray(inp["mla_wq_b"], np.float32).reshape(2, 384, 8, 192)
    wqb2 = np.concatenate([wqb, wqb[:, :, :, 128:][:, :, :, _SWAP]], axis=3).reshape(2, 384, 2048)
    bs = np.asarray(inp["gm_bs"], np.float32)
    bsb = np.broadcast_to(bs.reshape(2, 1, 1024), (2, 128, 1024))
    wsT = np.asarray(inp["gm_ws"], np.float32).transpose(0, 3, 1, 2).reshape(2, 128, 1024)
    return {
        "ident": np.eye(128, dtype=np.float32),
        "bsb": f(bsb), "wsT": f(wsT),
        "ada_w": f(inp["ada_w"]), "wq_a": f(inp["mla_wq_a"]), "wkv_a": f(wkva2), "wq_b": f(wqb2),
        "wkv_b": f(inp["mla_wkv_b"]), "wo": f(inp["mla_wo"]),
        "gm_w_in": f(inp["gm_w_in"]), "gm_w_out": f(inp["gm_w_out"]),
        "ffn_w1": f(inp["ffn_w1"]), "ffn_w2": f(inp["ffn_w2"]),
    }


_PROG_CACHE = {}


def _get_prog(layers, debug_stop=None):
    key = (tuple(layers), debug_stop)
    if key not in _PROG_CACHE:
        _PROG_CACHE[key] = Prog(list(layers), True, True, debug_stop)
    return _PROG_CACHE[key]


def run_layers(inp, x, y, layers, debug_stop=None, ncores=8):
    prog = _get_prog(layers, debug_stop)
    shared = _shared_weights(inp)
    in_maps = []
    for c in range(ncores):
        b, half = c // 2, c % 2
        rope = _rope_tables(half)
        m = dict(shared)
        m["x_own"] = np.ascontiguousarray(x[b, half * NT:(half + 1) * NT])
        m["x_par"] = np.ascontiguousarray(x[b, (1 - half) * NT:(2 - half) * NT])
        m["ctx"] = np.ascontiguousarray(y[b])
        m["vecs"] = _build_vecs(inp, b)
        m["rope"] = rope
        in_maps.append(m)
    res = run_bass_kernel_spmd(prog.nc, in_maps, core_ids=list(range(ncores)))
    xo = np.zeros_like(x)
    yo = np.zeros_like(y)
    if prog.dbg_names:
        run_layers.dbg = [{n: np.asarray(res.results[c][n]) for n in prog.dbg_names} for c in range(ncores)]
    for c in range(ncores):
        b, half = c // 2, c % 2
        xo[b, half * NT:(half + 1) * NT] = res.results[c]["out_x"]
        if half == 0:
            yo[b] = res.results[c]["out_y"]
    return xo, yo


def kernel(**inputs):
    inp = {k: np.asarray(v) for k, v in inputs.items()}
    x = np.ascontiguousarray(inp["x"], dtype=np.float32)
    y = np.ascontiguousarray(inp["ctx"], dtype=np.float32)
    x, y = run_layers(inp, x, y, (0, 1))
    x, y = run_layers(inp, x, y, (2, 3))
    return x.astype(np.float32)
```

```python
import numpy as np
from contextlib import ExitStack
import concourse.bass as bass
import concourse.mybir as mybir
from concourse.bass_utils import run_bass_kernel_spmd

F32 = mybir.dt.float32
BF16 = mybir.dt.bfloat16
AF = mybir.ActivationFunctionType
ALU = mybir.AluOpType

D = 1024
KC = 8
NT = 2048
NCTX = 256
NKEY = 2 * NT + NCTX
HEADS = 8
EPS = 1e-6
SCALE = 192 ** -0.5
ARENA_WORDS = 53000

ENGS = ("pe", "act", "dve", "pool", "sp")
import os as _os
F_LN = _os.environ.get("F_LN", "1") == "1"
F_PAD = _os.environ.get("F_PAD", "1") == "1"
F_SUMS = _os.environ.get("F_SUMS", "1") == "1"


class Res:
    __slots__ = ("name", "w", "r", "rd")

    def __init__(self, name):
        self.name = name
        self.w = None
        self.r = {}
        self.rd = []


class Op:
    __slots__ = ("eng", "fn", "deps", "lane", "ndma", "signal", "sigkey", "sigval", "waits", "known")

    def __init__(self, eng, fn, lane, ndma):
        self.eng = eng
        self.fn = fn
        self.lane = lane
        self.ndma = ndma
        self.signal = False
        self.sigkey = None
        self.sigval = 0
        self.waits = ()
        self.known = None
        self.deps = ()


class Sched:
    def __init__(self):
        self.ops = []
        self.res = []
        self.lanes = {}
        self.pending = {e: None for e in ENGS}
        self.last = {e: None for e in ENGS}
        self.lane_last = {}

    def R(self, name):
        r = Res(name)
        self.res.append(r)
        return r

    def add(self, eng, fn, reads=(), writes=(), lane=None, ndma=0):
        op = Op(eng, fn, lane, ndma)
        deps = set()
        for r in reads:
            if r.w is not None:
                deps.add(r.w)
        for w in writes:
            if w.w is not None:
                deps.add(w.w)
            deps.update(w.r.values())
            deps.update(w.rd)
        if self.pending[eng] is not None:
            deps.update(self.pending[eng])
            self.pending[eng] = None
        dl = []
        for d in deps:
            if d.eng == "pe" and eng == "pe" and d.lane is None and lane is None:
                continue
            d.signal = True
            dl.append(d)
        op.deps = dl
        for r in reads:
            if lane is not None:
                r.rd.append(op)
            else:
                r.r[eng] = op
        for w in writes:
            w.w = op
            w.r = {}
            w.rd = []
        self.ops.append(op)
        if lane is not None:
            if lane in self.lanes:
                assert self.lanes[lane] == eng, "one issuing engine per lane"
            self.lanes[lane] = eng
            self.lane_last[lane] = op
        else:
            self.last[eng] = op
        return op

    def barrier(self):
        outstanding = [o for o in self.last.values() if o is not None]
        outstanding += list(self.lane_last.values())
        for o in outstanding:
            o.signal = True
        for e in ENGS:
            cur = self.pending[e]
            self.pending[e] = list(outstanding) + (cur if cur else [])
        for r in self.res:
            r.w = None
            r.r = {}
            r.rd = []

    def fence(self, eng="sp"):
        self.barrier()
        self.add(eng, None)

    def finalize(self):
        ms = {e: 0 for e in ENGS}
        lc = {l: 0 for l in self.lanes}
        for op in self.ops:
            if op.lane is not None:
                lc[op.lane] += 16 * op.ndma
                op.sigkey = ("l", op.lane)
                op.sigval = lc[op.lane]
            elif op.signal and op.fn is not None:
                ms[op.eng] += 1
                op.sigkey = ("e", op.eng)
                op.sigval = ms[op.eng]
        seen = {e: {} for e in ENGS}
        for op in self.ops:
            s = seen[op.eng]
            waits = {}
            for d in op.deps:
                if d.sigkey is None:
                    continue
                if s.get(d.sigkey, 0) >= d.sigval:
                    continue
                if waits.get(d.sigkey, 0) < d.sigval:
                    waits[d.sigkey] = d.sigval
            for d in op.deps:
                if d.sigkey in waits and d.known is not None:
                    for k, v in d.known.items():
                        if s.get(k, 0) < v:
                            s[k] = v
            for k, v in waits.items():
                if s.get(k, 0) < v:
                    s[k] = v
            op.waits = tuple(waits.items())
            if op.sigkey is not None:
                op.known = dict(s)
        self.stats = (dict(ms), dict(lc))


class Prog:
    def __init__(self, layers, first, last, debug_stop=None):
        self.layers = layers
        self.debug_stop = debug_stop
        self.debug = debug_stop is not None
        self.dbg_names = []
        nc = bass.Bass("TRN2", target_bir_lowering=False)
        self.nc = nc
        self.S = Sched()
        self.es = ExitStack()
        self._declare_dram()
        with self.es:
            self._alloc()
            self._build()
            self.S.finalize()
            self._emit()

    def _declare_dram(self):
        nc = self.nc
        din = lambda n, s: nc.dram_tensor(n, s, F32, kind="ExternalInput").ap()
        self.d_xown = din("x_own", [NT, D])
        self.d_xpar = din("x_par", [NT, D])
        self.d_ctx = din("ctx", [NCTX, D])
        self.d_vecs = din("vecs", [128, NV])
        self.d_ident = din("ident", [128, 128])
        self.d_rope = din("rope", [64, 2, NKEY])
        self.d_bsb = din("bsb", [2, 128, 1024])
        self.d_wsT = din("wsT", [2, 128, 1024])
        self.d_adaw = din("ada_w", [4, D, 6 * D])
        self.d_wqa = din("wq_a", [2, D, 384])
        self.d_wkva = din("wkv_a", [2, D, 384])
        self.d_wqb = din("wq_b", [2, 384, 2048])
        self.d_wkvb = din("wkv_b", [2, 256, 2048])
        self.d_wo = din("wo", [2, D, D])
        self.d_win = din("gm_w_in", [2, D, 4 * D])
        self.d_wout = din("gm_w_out", [2, 2 * D, D])
        self.d_w1 = din("ffn_w1", [4, D, 4 * D])
        self.d_w2 = din("ffn_w2", [4, 4 * D, D])
        self.d_out = nc.dram_tensor("out_x", [NT, D], F32, kind="ExternalOutput").ap()
        self.d_outy = nc.dram_tensor("out_y", [NCTX, D], F32, kind="ExternalOutput").ap()

    def _alloc(self):
        nc, es, S = self.nc, self.es, self.S
        self.arena = es.enter_context(nc.sbuf_tensor("arena", [128, ARENA_WORDS], F32))
        self.arena_b = self.arena.bitcast(BF16)
        self.top = 0
        self.banks = []
        for i in range(8):
            t = es.enter_context(nc.psum_tensor(f"pb{i}", [128, 512], F32))
            self.banks.append((t, S.R(f"pb{i}")))
        self.XT = self.f32(KC * NT).rearrange("p (k n) -> p k n", k=KC)
        self.XTr = [S.R(f"XT{g}") for g in range(4)]
        self.YT = self.f32(KC * NCTX).rearrange("p (k n) -> p k n", k=KC)
        self.YTr = S.R("YT")
        self.vecs = self.f32(NV)
        self.vecs_r = S.R("vecs")
        self.ident = self.f32(128)
        self.ident_r = S.R("ident")
        self.ones = self.bf(128)
        self.ones_r = S.R("ones")
        self.mod = self.f32(96)
        self.am = self.f32(16)
        self.af = self.f32(16)
        self.mod_r = S.R("mod")
        self.scb = self.bf(16)
        self.scb_r = S.R("scb")
        self.persist_top = self.top

    def f32(self, n, parts=None):
        a = self.arena[:, self.top:self.top + n]
        self.top += n
        assert self.top <= ARENA_WORDS, f"SBUF overflow {self.top}"
        return a

    def bf(self, n):
        w = (n + 1) // 2
        a = self.arena_b[:, 2 * self.top:2 * self.top + n]
        self.top += w
        assert self.top <= ARENA_WORDS, f"SBUF overflow {self.top}"
        return a

    def phase(self):
        self.S.barrier()
        self.top = self.persist_top

    def mm(self, out, lhsT, rhs, start, stop, reads, writes):
        self.S.add("pe", lambda e: e.matmul(out, lhsT=lhsT, rhs=rhs, start=start, stop=stop), reads, writes)

    def tr(self, out, in_, reads, writes):
        ident = self.ident
        self.S.add("pe", lambda e: e.transpose(out, in_, ident), list(reads) + [self.ident_r], writes)

    def act(self, out, in_, func, reads, writes, scale=None, bias=None):
        kw = {}
        if scale is not None:
            kw["scale"] = scale
        if bias is not None:
            kw["bias"] = bias
        self.S.add("act", lambda e: e.activation(out=out, in_=in_, func=func, **kw), reads, writes)

    def stt(self, out, in0, scalar, in1, op0, op1, reads, writes, eng="dve"):
        self.S.add(eng, lambda e: e.scalar_tensor_tensor(out=out, in0=in0, scalar=scalar, in1=in1, op0=op0, op1=op1),
                   reads, writes)

    def tt(self, out, in0, in1, op, reads, writes, eng="dve"):
        self.S.add(eng, lambda e: e.tensor_tensor(out=out, in0=in0, in1=in1, op=op), reads, writes)

    def ts(self, out, in0, s1, s2, op0, op1, reads, writes, eng="dve"):
        self.S.add(eng, lambda e: e.tensor_scalar(out=out, in0=in0, scalar1=s1, scalar2=s2, op0=op0, op1=op1),
                   reads, writes)

    def recip(self, out, in_, reads, writes):
        self.S.add("dve", lambda e: e.reciprocal(out=out, in_=in_), reads, writes)

    def copy(self, eng, out, in_, reads, writes):
        if eng == "act":
            self.S.add("act", lambda e: e.copy(out=out, in_=in_), reads, writes)
        else:
            self.S.add(eng, lambda e: e.tensor_copy(out=out, in_=in_), reads, writes)

    def dma(self, eng, out, in_, reads, writes, lane):
        def fn(e, sem):
            e.dma_start(out=out, in_=in_).then_inc(sem, 16)
        self.S.add(eng, fn, reads, writes, lane=lane, ndma=1)

    def dump(self, name, ap, reads, dtype=F32):
        if not getattr(self, "debug", False):
            return
        shape = list(ap.shape)
        d = self.nc.dram_tensor("dbg_" + name, shape, dtype, kind="ExternalOutput").ap()
        self.dma("sp", d, ap, reads, [], lane="dbg_" + name)
        self.dbg_names.append("dbg_" + name)

    def vcol(self, name, j=0):
        o = VOFF[name] + j
        return self.vecs[:, o:o + 1]

    def vslice(self, name, n):
        o = VOFF[name]
        return self.vecs[:, o:o + n]

    def bank(self, i):
        return self.banks[i]

    class WStream:
        def __init__(self, prog, name, slots, specs):
            self.p = prog
            self.slots = slots
            self.specs = specs
            self.issued = 0
            self.name = name

        def get(self, n):
            ns = len(self.slots)
            while self.issued < len(self.specs) and self.issued <= n + ns - 1:
                m = self.issued
                ap, res = self.slots[m % ns]
                src, view = self.specs[m]
                self.p.dma("pool", view(ap), src, [], [res], lane=f"{self.name}{m % ns}")
                self.issued += 1
            ap, res = self.slots[n % ns]
            return self.specs[n][1](ap), res

    def rsqrt_b(self, ps, N, inv_n, rs, rs_r, ps_r, parts=128):
        if F_LN:
            self.act(rs[:parts, :N], ps[:parts, :N], AF.Ln, [ps_r, self.eps_r], [rs_r], scale=inv_n, bias=self.eps_col[:parts, :])
            self.act(rs[:parts, :N], rs[:parts, :N], AF.Exp, [rs_r], [rs_r], scale=-0.5)
        else:
            self.act(rs[:parts, :N], ps[:parts, :N], AF.Sqrt, [ps_r, self.eps_r], [rs_r], scale=inv_n, bias=self.eps_col[:parts, :])
            self.recip(rs[:parts, :N], rs[:parts, :N], [rs_r], [rs_r])

    def modulate(self, src, src_r, N, which, kind, hT, hT_r, coff):
        self._modulate_multi(src, [src_r], N, which, kind, hT, hT_r, coff)

    def linear(self, ps, ps_r, N, w, w_r, hT, hT_r, coff, nk=KC, parts=128):
        for k in range(nk):
            self.mm(ps[:parts, :N], w(k), hT[:, k, coff:coff + N], k == 0, k == nk - 1, [w_r, hT_r], [ps_r])

    def load_tokens(self, dram_rows, ntiles, dest, dest_r):
        for _ in self.load_tokens_iter(dram_rows, ntiles, dest, dest_r):
            pass

    def load_tokens_iter(self, dram_rows, ntiles, dest, dest_r):
        for h in range(ntiles // 2):
            st, st_r = self.stage[h % 2]
            src = dram_rows[h * 256:(h + 1) * 256, :].rearrange("(t p) d -> p t d", p=128)
            self.dma("sp", st, src, [], [st_r], lane=f"stage{h % 2}")
            for k in range(KC):
                pb, pb_r = self.bank(self.TRB[k % 2])
                for t in range(2):
                    self.tr(pb[:, t * 128:(t + 1) * 128], st[:, t, k * 128:(k + 1) * 128], [st_r], [pb_r])
                self.copy("act" if k % 2 == 0 else "dve", dest(k, h * 256, 256), pb[:, 0:256], [pb_r], [dest_r(h)])
            yield h

    def store_tokens(self, src, src_r, ntiles, dram_rows, lane):
        for h in range(ntiles // 2):
            st, st_r = self.stage[h % 2]
            for t in range(2):
                for half in range(2):
                    pb, pb_r = self.bank(self.TRB[(2 * t + half) % 2])
                    for kk in range(4):
                        k = half * 4 + kk
                        self.tr(pb[:, kk * 128:(kk + 1) * 128], src(k, h * 256 + t * 128, 128), [src_r(h)], [pb_r])
                    self.copy("act" if half == 0 else "dve", st[:, t, half * 512:(half + 1) * 512], pb[:, :], [pb_r], [st_r])
            dst = dram_rows[h * 256:(h + 1) * 256, :].rearrange("(t p) d -> p t d", p=128)
            self.dma("sp", dst, st, [st_r], [], lane=f"{lane}{h % 2}")

    def _build(self):
        S = self.S
        self.out_r = S.R("out")
        self.phase()
        self.eps_col = self.f32(1)
        self.eps_r = S.R("eps")
        S.add("dve", lambda e: e.memset(self.eps_col, EPS), [], [self.eps_r])
        self.persist_top = self.top
        S.add("dve", lambda e: e.memset(self.ones, 1.0), [], [self.ones_r])
        self.dma("sp", self.vecs, self.d_vecs, [], [self.vecs_r], lane="vecs")
        self.dma("sp", self.ident, self.d_ident, [], [self.ident_r], lane="ident")
        self.act(self.scb, self.vslice("cvec", 16), AF.Silu, [self.vecs_r], [self.scb_r])
        self.stage = []
        for i in range(2):
            self.stage.append((self.f32(2048).rearrange("p (t d) -> p t d", t=2), S.R(f"stage{i}")))
        self.TRB = [0, 1]
        XT, YT = self.XT, self.YT
        mod_gen = self.build_mod_iter(self.layers[0], first=True)
        for _ in self.load_tokens_iter(self.d_xown, 16, lambda k, off, n: XT[:, k, off:off + n], lambda h: self.XTr[h // 2]):
            next(mod_gen, None)
        for _ in self.load_tokens_iter(self.d_ctx, 2, lambda k, off, n: YT[:, k, off:off + n], lambda h: self.YTr):
            next(mod_gen, None)
        for _ in mod_gen:
            pass

        for li, L in enumerate(self.layers):
            if li > 0:
                self.build_mod(L)
            if self.debug_stop == ("mod", L):
                break
            if L % 2 == 0:
                self.build_mla(L)
            else:
                self.build_gmlp(L)
            self.dump(f"xt_mix{L}", self.XT, self.XTr)
            self.dump(f"yt_mix{L}", self.YT, [self.YTr])
            if self.debug_stop == ("mix", L):
                break
            self.build_ffn(L)
            self.dump(f"xt_ffn{L}", self.XT, self.XTr)
            self.dump(f"yt_ffn{L}", self.YT, [self.YTr])
            if self.debug_stop == ("ffn", L):
                break

        self.phase()
        self.stage = []
        for i in range(2):
            self.stage.append((self.f32(2048).rearrange("p (t d) -> p t d", t=2), S.R(f"ostage{i}")))
        self.TRB = [0, 1]
        self.store_tokens(lambda k, off, n: XT[:, k, off:off + n], lambda h: self.XTr[h // 2], 16, self.d_out, "ox")
        self.store_tokens(lambda k, off, n: YT[:, k, off:off + n], lambda h: self.YTr, 2, self.d_outy, "oy")
        S.fence("sp")

    def build_mod(self, L):
        for _ in self.build_mod_iter(L):
            pass

    def build_mod_iter(self, L, first=False):
        S = self.S
        if not first:
            self.phase()
        slots = [(self.bf(4096), S.R(f"adaslot{i}")) for i in range(2)]
        wsrc = self.d_adaw[L].rearrange("(k p) n -> p k n", p=128)
        view = lambda ap: ap.rearrange("p (k n) -> p k n", k=KC)
        ws = Prog.WStream(self, "ada", slots, [(wsrc[:, :, b * 512:(b + 1) * 512], view) for b in range(12)])
        pb, pb_r = self.bank(2)
        for b in range(12):
            w, w_r = ws.get(b)
            for cc in range(4):
                ch = b * 4 + cc
                for k in range(KC):
                    self.mm(pb[:, 2 * ch:2 * ch + 2], w[:, k, cc * 128:(cc + 1) * 128], self.scb[:, 2 * k:2 * k + 2],
                            k == 0, k == KC - 1, [w_r, self.scb_r], [pb_r])
            yield b
        o = VOFF[f"adab{L}"]
        self.tt(self.mod, pb[:, 0:96], self.vecs[:, o:o + 96], ALU.add, [pb_r, self.vecs_r], [self.mod_r])
        og = VOFF[f"gmix{L}"]
        self.stt(self.am, self.mod[:, 16:32], 1.0, self.vecs[:, og:og + 16], ALU.add, ALU.mult, [self.mod_r, self.vecs_r], [self.mod_r])
        og = VOFF[f"gffn{L}"]
        self.stt(self.af, self.mod[:, 64:80], 1.0, self.vecs[:, og:og + 16], ALU.add, ALU.mult, [self.mod_r, self.vecs_r], [self.mod_r])
        self.dump(f"mod{L}", self.mod, [self.mod_r])

    def gate(self, kind, k, which):
        base = 16 if kind == "m" else 40
        c = 2 * (base + k) + which
        return self.mod[:, c:c + 1]

    def with_ctx(self, L):
        return L < 2

    def build_ffn(self, L):
        S = self.S
        self.phase()
        XT, YT = self.XT, self.YT
        sgs = [[("x", 0), ("x", 1)], [("x", 2), ("x", 3)]]
        if self.with_ctx(L):
            sgs.append([("y", 0)])
        NSG = 1024
        hT = self.bf(KC * NSG).rearrange("p (k n) -> p k n", k=KC)
        hT_r = S.R("hT")
        aT = self.bf(32 * NSG).rearrange("p (j n) -> p j n", j=32)
        aT_r = [S.R(f"aT{j}") for j in range(32)]
        slots = [(self.bf(4096), S.R(f"wslot{i}")) for i in range(4)]
        self.sqs = [(self.bf(512), S.R(f"sq{i}")) for i in range(2)]
        self.tmps = [(self.f32(512), S.R(f"tmp{i}")) for i in range(2)]
        self.rs_buf = (self.f32(512), S.R("rs"))
        rl = [(self.f32(512), S.R(f"relu{i}")) for i in range(2)]
        self.SSB = 7
        MM = [0, 1, 2, 3, 4, 5]
        w1src = self.d_w1[L].rearrange("(k p) n -> p k n", p=128)
        w2src = self.d_w2[L].rearrange("(j p) n -> p j n", p=128)
        v1 = lambda ap: ap.rearrange("p (k n) -> p k n", k=KC)
        v2 = lambda ap: ap.rearrange("p (j n) -> p j n", j=32)
        specs = []
        for sg in sgs:
            specs += [(w1src[:, :, b * 512:(b + 1) * 512], v1) for b in range(8)]
            specs += [(w2src[:, :, dc * 128:(dc + 1) * 128], v2) for dc in range(8)]
        ws = Prog.WStream(self, "ffw", slots, specs)
        bi = 0
        mmi = 0
        for sg in sgs:
            subs = []
            off = 0
            for (kind, g) in sg:
                if kind == "x":
                    N = 512
                    src = (lambda g: (lambda k: XT[:, k, g * 512:(g + 1) * 512]))(g)
                    src_r = self.XTr[g]
                    which = 0
                else:
                    N = NCTX
                    src = lambda k: YT[:, k, :]
                    src_r = self.YTr
                    which = 1
                subs.append((src, src_r, N, which, off))
                off += N
            for (src, src_r, N, which, off) in subs:
                self.modulate(src, src_r, N, which, "f", hT, hT_r, off)
            for b in range(8):
                w, w_r = ws.get(bi)
                bi += 1
                for jj in range(4):
                    j = b * 4 + jj
                    for (src, src_r, N, which, off) in subs:
                        pb, pb_r = self.bank(MM[mmi % len(MM)])
                        mmi += 1
                        self.linear(pb, pb_r, N, (lambda k, w=w, jj=jj: w[:, k, jj * 128:(jj + 1) * 128]), w_r, hT, hT_r, off)
                        r, r_r = rl[mmi % 2]
                        self.act(r[:, :N], pb[:, :N], AF.Relu, [pb_r], [r_r])
                        self.tt(aT[:, j, off:off + N], r[:, :N], r[:, :N], ALU.mult, [r_r], [aT_r[j]])
            for dc in range(8):
                w, w_r = ws.get(bi)
                bi += 1
                for (src, src_r, N, which, off) in subs:
                    pb, pb_r = self.bank(MM[mmi % len(MM)])
                    mmi += 1
                    for j in range(32):
                        self.mm(pb[:, :N], w[:, j, :], aT[:, j, off:off + N], j == 0, j == 31, [w_r, aT_r[j]], [pb_r])
                    self.stt(src(dc), pb[:, :N], self.gate("f", dc, which), src(dc), ALU.mult, ALU.add,
                             [pb_r, self.mod_r, src_r], [src_r])

    def build_gmlp(self, L):
        S = self.S
        j_ = L // 2
        self.phase()
        XT, YT = self.XT, self.YT
        groups = [("x", g) for g in range(4)]
        if self.with_ctx(L):
            groups.append(("y", 0))
        hT = self.bf(KC * 512).rearrange("p (k n) -> p k n", k=KC)
        hT_r = S.R("hT")
        uT = self.bf(16 * 512).rearrange("p (j n) -> p j n", j=16)
        uT_r = [S.R(f"uT{j}") for j in range(16)]
        gT, gT_r = uT, uT_r
        vt = [(self.f32(2048), S.R(f"v{t}")) for t in range(4)]
        vn = [(self.bf(2048), S.R(f"vn{i}")) for i in range(4)]
        slots = [(self.bf(4096), S.R(f"wslot{i}")) for i in range(4)]
        wsT = self.bf(1024).rearrange("p (g n) -> p g n", g=8)
        wsT_r = S.R("wsT")
        bsb = self.f32(1024).rearrange("p (g n) -> p g n", g=8)
        bsb_r = S.R("bsb")
        BIAS = self.f32(2048).rearrange("p (c n) -> p c n", c=16)
        BIAS_r = S.R("BIAS")
        self.sqs = [(self.bf(512), S.R(f"sq{i}")) for i in range(2)]
        self.tmps = [(self.f32(512), S.R(f"tmp{i}")) for i in range(2)]
        self.rs_buf = (self.f32(512), S.R("rs"))
        stats = self.f32(112)
        stats_r = [S.R(f"stats{t}") for t in range(4)]
        sc_r = [S.R(f"sc{t}") for t in range(4)]
        mv = self.f32(8)
        mv_r = [S.R(f"mv{t}") for t in range(4)]
        self.SSB = 7
        MM = [0, 1, 2, 3, 4, 5]
        mmi = 0
        self.dma("pool", wsT, self.d_wsT[j_].rearrange("p (g n) -> p g n", g=8), [], [wsT_r], lane="wsT")
        self.dma("sp", bsb, self.d_bsb[j_].rearrange("p (g n) -> p g n", g=8), [], [bsb_r], lane="bsb")
        pb, pb_r = self.bank(6)
        for g in range(4):
            self.mm(pb[:, g * 128:(g + 1) * 128], self.ones, wsT[:, g, :], True, True, [self.ones_r, wsT_r], [pb_r])
        pb2, pb2_r = self.bank(5)
        for g in range(4):
            self.mm(pb2[:, g * 128:(g + 1) * 128], self.ones, wsT[:, 4 + g, :], True, True, [self.ones_r, wsT_r], [pb2_r])
        for c in range(16):
            g = c // 2
            src = (pb if g < 4 else pb2)[:, (g % 4) * 128:(g % 4 + 1) * 128]
            self.stt(BIAS[:, c, :], src, self.vcol(f"lnb{j_}", c), bsb[:, g, :], ALU.mult, ALU.add,
                     [pb_r, pb2_r, self.vecs_r, bsb_r], [BIAS_r])
        winsrc = self.d_win[j_].rearrange("(k p) n -> p k n", p=128)
        woutsrc = self.d_wout[j_].rearrange("(c p) n -> p c n", p=128)
        v1 = lambda ap: ap.rearrange("p (k n) -> p k n", k=KC)
        v3 = lambda ap: ap.rearrange("p (c n) -> p c n", c=16)
        specs = []
        for _ in groups:
            specs += [(winsrc[:, :, b * 512:(b + 1) * 512], v1) for b in range(8)]
            specs += [(woutsrc[:, :, dp * 256:(dp + 1) * 256], v3) for dp in range(4)]
        ws = Prog.WStream(self, "gmw", slots, specs)
        bi = 0
        for (kind, g) in groups:
            if kind == "x":
                N = 512
                src = (lambda g: (lambda k: XT[:, k, g * 512:(g + 1) * 512]))(g)
                src_r = self.XTr[g]
                which = 0
            else:
                N = NCTX
                src = lambda k: YT[:, k, :]
                src_r = self.YTr
                which = 1
            ntile = N // 128
            self.modulate(src, src_r, N, which, "m", hT, hT_r, 0)
            for b in range(4):
                w, w_r = ws.get(bi)
                bi += 1
                for jj in range(4):
                    j = b * 4 + jj
                    pb, pb_r = self.bank(MM[mmi % len(MM)])
                    mmi += 1
                    self.linear(pb, pb_r, N, (lambda k, w=w, jj=jj: w[:, k, jj * 128:(jj + 1) * 128]), w_r, hT, hT_r, 0)
                    self.act(uT[:, j, :N], pb[:, :N], AF.Gelu, [pb_r], [uT_r[j]])
            for cb in range(4):
                w, w_r = ws.get(bi)
                bi += 1
                for t in range(ntile):
                    pb, pb_r = self.bank(MM[mmi % len(MM)])
                    mmi += 1
                    for k in range(KC):
                        self.mm(pb[:, :], hT[:, k, t * 128:(t + 1) * 128], w[:, k, :], k == 0, k == KC - 1, [w_r, hT_r], [pb_r])
                    self.act(vt[t][0][:, cb * 512:(cb + 1) * 512], pb[:, :], AF.Gelu, [pb_r], [vt[t][1]])
            for t in range(ntile):
                v, v_r = vt[t]
                st = stats[:, t * 24:(t + 1) * 24].rearrange("p (a b) -> p a b", a=4)
                for q in range(4):
                    S.add("dve", (lambda e, st=st, v=v, q=q: e.bn_stats(out=st[:, q, :], in_=v[:, q * 512:(q + 1) * 512])),
                          [v_r], [stats_r[t]])
                m2 = mv[:, 2 * t:2 * t + 2]
                st2 = stats[:, t * 24:(t + 1) * 24]
                S.add("dve", (lambda e, m2=m2, st2=st2: e.bn_aggr(out=m2, in_=st2)), [stats_r[t]], [mv_r[t]])
            for t in range(ntile):
                sc = stats[:, 96 + 2 * t:96 + 2 * t + 1]
                self.act(sc, mv[:, 2 * t + 1:2 * t + 2], AF.Sqrt, [mv_r[t], self.eps_r], [sc_r[t]], scale=1.0, bias=self.eps_col)
            for t in range(ntile):
                sc = stats[:, 96 + 2 * t:96 + 2 * t + 1]
                nb = stats[:, 96 + 2 * t + 1:96 + 2 * t + 2]
                self.recip(sc, sc, [sc_r[t]], [sc_r[t]])
                self.stt(nb, mv[:, 2 * t:2 * t + 1], -1.0, sc, ALU.mult, ALU.mult, [mv_r[t], sc_r[t]], [sc_r[t]])
            for t in range(ntile):
                v, v_r = vt[t]
                sc = stats[:, 96 + 2 * t:96 + 2 * t + 1]
                nb = stats[:, 96 + 2 * t + 1:96 + 2 * t + 2]
                vnb, vnb_r = vn[t]
                self.act(vnb, v, AF.Identity, [v_r, sc_r[t]], [vnb_r], scale=sc, bias=nb)
            for t in range(ntile):
                vnb, vnb_r = vn[t]
                for cq in range(4):
                    pb, pb_r = self.bank(MM[mmi % len(MM)])
                    mmi += 1
                    for cc in range(4):
                        c = cq * 4 + cc
                        self.mm(pb[:, cc * 128:(cc + 1) * 128], vnb[:, c * 128:(c + 1) * 128], wsT[:, c // 2, :], True, True,
                                [vnb_r, wsT_r], [pb_r])
                    for cc in range(4):
                        c = cq * 4 + cc
                        tmp, tmp_r = self.tmps[c % 2]
                        self.stt(tmp[:, :128], pb[:, cc * 128:(cc + 1) * 128], self.vcol(f"lng{j_}", c), BIAS[:, c, :],
                                 ALU.mult, ALU.add, [pb_r, self.vecs_r, BIAS_r], [tmp_r])
                        self.tt(uT[:, c, t * 128:(t + 1) * 128], tmp[:, :128], uT[:, c, t * 128:(t + 1) * 128], ALU.mult,
                                [tmp_r, uT_r[c]], [uT_r[c]])
            for dc in range(8):
                if dc % 2 == 0:
                    w, w_r = ws.get(bi)
                    bi += 1
                pb, pb_r = self.bank(MM[mmi % len(MM)])
                mmi += 1
                for c in range(16):
                    self.mm(pb[:, :N], w[:, c, (dc % 2) * 128:(dc % 2 + 1) * 128], gT[:, c, :N], c == 0, c == 15, [w_r, gT_r[c]], [pb_r])
                self.stt(src(dc), pb[:, :N], self.gate("m", dc, which), src(dc), ALU.mult, ALU.add,
                         [pb_r, self.mod_r, src_r], [src_r])

    def build_mla(self, L):
        S = self.S
        j_ = L // 2
        wctx = self.with_ctx(L)
        XT, YT = self.XT, self.YT
        self.phase()
        NQ = NT + NCTX
        cqn = self.bf(3 * NQ).rearrange("p (m n) -> p m n", m=3)
        cqn_r = [S.R(f"cqn{g}") for g in range(5)]
        ckvn = self.bf(2 * NKEY).rearrange("p (m n) -> p m n", m=2)
        ckvn_r = [S.R(f"ckvn{g}") for g in range(9)]
        Cb = self.bf(NKEY)
        Cb_r = [S.R(f"C{g}") for g in range(9)]
        sqpe = self.bf(NKEY)
        sqpe_r = [S.R(f"sqpe{g}") for g in range(9)]
        if F_PAD:
            S.add("pool", lambda e: e.memset(sqpe[64:128, :], 0.0), [], sqpe_r)
        keep_top = self.top
        hT = self.bf(KC * 512).rearrange("p (k n) -> p k n", k=KC)
        hT_r = S.R("hT")
        xTp = self.f32(KC * 512).rearrange("p (k n) -> p k n", k=KC)
        xTp_r = [S.R("xTp0"), S.R("xTp1")]
        self.stage = [(self.f32(2048).rearrange("p (t d) -> p t d", t=2), S.R(f"stage{i}")) for i in range(2)]
        wkva = self.bf(KC * 384).rearrange("p (k n) -> p k n", k=KC)
        wkva_r = S.R("wkva")
        wqa = self.bf(KC * 384).rearrange("p (k n) -> p k n", k=KC)
        wqa_r = S.R("wqa")
        self.sqs = [(self.bf(512), S.R(f"sq{i}")) for i in range(3)]
        self.tmps = [(self.f32(512), S.R(f"tmp{i}")) for i in range(2)]
        self.rs_buf = (self.f32(512), S.R("rs"))
        rs2 = (self.f32(512), S.R("rs2"))
        tabs = [(self.f32(1024).rearrange("p (a n) -> p a n", a=2), S.R(f"tab{i}")) for i in range(2)]
        self.TRB = [0, 1]
        self.SSB = 2
        GEN = [3, 4, 5, 6, 7]
        self.dma("pool", wkva, self.d_wkva[j_].rearrange("(k p) n -> p k n", p=128), [], [wkva_r], lane="wkva")
        self.dma("pool", wqa, self.d_wqa[j_].rearrange("(k p) n -> p k n", p=128), [], [wqa_r], lane="wqa")
        groups = [("own", g) for g in range(4)] + [("par", g) for g in range(4)] + [("ctx", 0)]
        ti = 0
        for gi, (kind, g) in enumerate(groups):
            N = 512 if kind != "ctx" else NCTX
            kcol = gi * 512
            if kind == "own":
                src = (lambda g: (lambda k: XT[:, k, g * 512:(g + 1) * 512]))(g)
                src_r, which = self.XTr[g], 0
            elif kind == "par":
                self.load_tokens(self.d_xpar[g * 512:(g + 1) * 512, :], 4,
                                 lambda k, off, n: xTp[:, k, off:off + n], lambda h: xTp_r[h])
                src = lambda k: xTp[:, k, :]
                src_r, which = None, 0
            else:
                src = lambda k: YT[:, k, :]
                src_r, which = self.YTr, 1
            src_rs = xTp_r if kind == "par" else [src_r]
            self._modulate_multi(src, src_rs, N, which, "m", hT, hT_r, 0)
            pk = [self.bank(GEN[0]), self.bank(GEN[1])]
            ssb, ssb_r = self.bank(GEN[2])
            for m in range(2):
                self.linear(pk[m][0], pk[m][1], N, (lambda k, m=m: wkva[:, k, m * 128:(m + 1) * 128]), wkva_r, hT, hT_r, 0)
            for m in range(2):
                sq, sq_r = self.sqs[m]
                self.act(sq[:, :N], pk[m][0][:, :N], AF.Square, [pk[m][1]], [sq_r])
                self.mm(ssb[:, :N], self.ones, sq[:, :N], m == 0, m == 1, [sq_r, self.ones_r], [ssb_r])
            rs, rs_r = rs2
            self.rsqrt_b(ssb, N, 1.0 / 256, rs, rs_r, ssb_r)
            for m in range(2):
                self.stt(ckvn[:, m, kcol:kcol + N], pk[m][0][:, :N], self.vcol(f"gkva{j_}", m), rs[:, :N], ALU.mult, ALU.mult,
                         [pk[m][1], self.vecs_r, rs_r], [ckvn_r[gi]])
            pp, pp_r = self.bank(GEN[3])
            pw, pw_r = self.bank(GEN[4])
            self.linear(pp, pp_r, N, lambda k: wkva[:, k, 256:320], wkva_r, hT, hT_r, 0, parts=64)
            self.linear(pw, pw_r, N, lambda k: wkva[:, k, 320:384], wkva_r, hT, hT_r, 0, parts=64)
            self.act(sqpe[0:64, kcol:kcol + N], pp[0:64, :N], AF.Square, [pp_r], [sqpe_r[gi]])
            tb, tab_r = tabs[ti % 2]
            self.dma("sp", tb[0:64, :, :N], self.d_rope[:, :, kcol:kcol + N], [], [tab_r], lane=f"tab{ti % 2}")
            ti += 1
            cosb, sinb = tb[:, 0, :], tb[:, 1, :]
            tA, tA_r = self.tmps[0]
            tB, tB_r = self.tmps[1]
            self.stt(tA[0:64, :N], pp[0:64, :N], self.vcol(f"gkp{j_}")[0:64, :], cosb[0:64, :N], ALU.mult, ALU.mult,
                     [pp_r, self.vecs_r, tab_r], [tA_r])
            self.stt(tB[0:64, :N], pw[0:64, :N], self.vcol(f"gks{j_}")[0:64, :], sinb[0:64, :N], ALU.mult, ALU.mult,
                     [pw_r, self.vecs_r, tab_r], [tB_r])
            self.tt(Cb[0:64, kcol:kcol + N], tA[0:64, :N], tB[0:64, :N], ALU.add, [tA_r, tB_r], [Cb_r[gi]], eng="pool")
            if kind == "own" or (kind == "ctx" and wctx):
                qcol = g * 512 if kind == "own" else NT
                qg = g if kind == "own" else 4
                pq = [self.bank(GEN[0]), self.bank(GEN[1]), self.bank(GEN[3])]
                ssq, ssq_r = self.bank(GEN[2])
                for m in range(3):
                    self.linear(pq[m][0], pq[m][1], N, (lambda k, m=m: wqa[:, k, m * 128:(m + 1) * 128]), wqa_r, hT, hT_r, 0)
                for m in range(3):
                    sq, sq_r = self.sqs[m]
                    self.act(sq[:, :N], pq[m][0][:, :N], AF.Square, [pq[m][1]], [sq_r])
                    self.mm(ssq[:, :N], self.ones, sq[:, :N], m == 0, m == 2, [sq_r, self.ones_r], [ssq_r])
                rs, rs_r = rs2
                self.rsqrt_b(ssq, N, 1.0 / 384, rs, rs_r, ssq_r)
                for m in range(3):
                    self.stt(cqn[:, m, qcol:qcol + N], pq[m][0][:, :N], self.vcol(f"gqa{j_}", m), rs[:, :N], ALU.mult, ALU.mult,
                             [pq[m][1], self.vecs_r, rs_r], [cqn_r[qg]])

        S.barrier()
        self.top = keep_top
        KTn = self.bf(NKEY)
        KTp = self.bf(NKEY)
        Vh = self.bf(34 * 128)
        KV_r = [S.R(f"KV{g}") for g in range(9)]
        KP_r = [S.R(f"KP{g}") for g in range(9)]
        VV_r = [S.R(f"VV{g}") for g in range(9)]
        QTn = self.bf(NQ)
        QTp = self.bf(NQ)
        Q_r = [S.R(f"Q{g}") for g in range(5)]
        oTh = self.bf(NQ)
        oT_r = [S.R(f"oT{g}") for g in range(5)]
        PT = [(self.bf(512), S.R(f"PT{i}")) for i in range(4)]
        rec = (self.f32(512), S.R("rec"))
        rs2 = (self.f32(512), S.R("rs2"))
        accD = (self.f32(512), S.R("accD"))
        accP = (self.f32(512), S.R("accP"))
        sumhi = (self.bf(512), S.R("sumhi"))
        sumlo = (self.bf(512), S.R("sumlo"))
        self.tmps = [(self.f32(512), S.R(f"tmp{i}")) for i in range(3)]
        sqk = [(self.bf(512), S.R(f"sqk{i}")) for i in range(2)]
        sqp = (self.bf(512), S.R("sqp"))
        sqp2 = (self.bf(512), S.R("sqp2"))
        sqps = [sqp, sqp2]
        tabs = [(self.f32(1024).rearrange("p (a n) -> p a n", a=2), S.R(f"tab{i}")) for i in range(2)]
        wh = [(self.bf(2304), S.R(f"wh{i}")) for i in range(2)]
        if F_PAD:
            S.add("pool", lambda e: e.memset(KTp[64:128, :], 0.0), [], KP_r)
            S.add("pool", lambda e: e.memset(QTp[64:128, :], 0.0), [], Q_r)
            S.add("pool", lambda e: e.memset(sqp[0][64:128, :], 0.0), [], [sqp[1]])
            S.add("pool", lambda e: e.memset(sqp2[0][64:128, :], 0.0), [], [sqp2[1]])
        SB = [0, 1, 2, 7]
        GB = [0, 1, 2, 7]
        OB = [3, 4]
        UB = [5, 6]
        qgroups = [(g, g * 512, 512, 0) for g in range(4)]
        if wctx:
            qgroups.append((4, NT, NCTX, 1))
        ti = 0
        oi = 0
        si = 0
        for h in range(HEADS):
            whb, wh_r = wh[h % 2]
            wkvb = whb[:, 0:512].rearrange("p (m n) -> p m n", m=2)
            wqb = whb[:, 512:1280].rearrange("p (m n) -> p m n", m=3)
            woh = whb[:, 1280:2304]
            lane = f"wh{h % 2}"
            self.dma("pool", wkvb, self.d_wkvb[j_].rearrange("(m p) n -> p m n", p=128)[:, :, h * 256:(h + 1) * 256], [], [wh_r], lane=lane)
            self.dma("pool", wqb, self.d_wqb[j_].rearrange("(m p) n -> p m n", p=128)[:, :, h * 256:(h + 1) * 256], [], [wh_r], lane=lane)
            self.dma("pool", woh, self.d_wo[j_][h * 128:(h + 1) * 128, :], [], [wh_r], lane=lane)
            gkn = self.vcol(f"gkn{j_}")
            gqn = self.vcol(f"gqn{j_}")
            rsb = [rs2, accP]

            def k_front(kg):
                N = 512 if kg < 8 else NCTX
                kcol = kg * 512
                pk, pk_r = self.bank([0, 3, 6][kg % 3])
                for m in range(2):
                    self.mm(pk[:, :N], wkvb[:, m, 0:128], ckvn[:, m, kcol:kcol + N], m == 0, m == 1, [wh_r, ckvn_r[kg]], [pk_r])
                sq, sq_r = sqk[kg % 2]
                self.act(sq[:, :N], pk[:, :N], AF.Square, [pk_r], [sq_r])

            k_front(0)
            for kg in range(9):
                N = 512 if kg < 8 else NCTX
                kcol = kg * 512
                pk, pk_r = self.bank([0, 3, 6][kg % 3])
                ssb, ssb_r = self.bank([1, 4, 7][kg % 3])
                pv, pv_r = self.bank([2, 5][kg % 2])
                sq, sq_r = sqk[kg % 2]
                if kg + 1 < 9:
                    k_front(kg + 1)
                for t in range(N // 128):
                    for m in range(2):
                        self.mm(pv[:, t * 128:(t + 1) * 128], ckvn[:, m, kcol + t * 128:kcol + (t + 1) * 128], wkvb[:, m, 128:256],
                                m == 0, m == 1, [wh_r, ckvn_r[kg]], [pv_r])
                self.mm(ssb[:, :N], self.ones, sq[:, :N], True, False, [sq_r, self.ones_r], [ssb_r])
                self.mm(ssb[:, :N], self.ones, sqpe[:, kcol:kcol + N], False, True, [sqpe_r[kg], self.ones_r], [ssb_r])
                rs, rs_r = rsb[kg % 2]
                self.rsqrt_b(ssb, N, 1.0 / 192, rs, rs_r, ssb_r)
                self.stt(KTn[:, kcol:kcol + N], pk[:, :N], gkn, rs[:, :N], ALU.mult, ALU.mult, [pk_r, self.vecs_r, rs_r], [KV_r[kg]])
                self.tt(KTp[0:64, kcol:kcol + N], Cb[0:64, kcol:kcol + N], rs[0:64, :N], ALU.mult, [Cb_r[kg], rs_r], [KP_r[kg]], eng="pool")
                self.copy("dve", Vh[:, kcol:kcol + N], pv[:, :N], [pv_r], [VV_r[kg]])
            qtab = {}

            def q_front(idx):
                nonlocal ti
                (qg, qcol, N, which) = qgroups[idx]
                gset = [[0, 1, 2, 7], [3, 4, 5, 6]][idx % 2]
                pn, pn_r = self.bank(gset[0])
                pp, pp_r = self.bank(gset[1])
                pw, pw_r = self.bank(gset[2])
                for m in range(3):
                    self.mm(pn[:, :N], wqb[:, m, 0:128], cqn[:, m, qcol:qcol + N], m == 0, m == 2, [wh_r, cqn_r[qg]], [pn_r])
                for m in range(3):
                    self.mm(pp[0:64, :N], wqb[:, m, 128:192], cqn[:, m, qcol:qcol + N], m == 0, m == 2, [wh_r, cqn_r[qg]], [pp_r])
                for m in range(3):
                    self.mm(pw[0:64, :N], wqb[:, m, 192:256], cqn[:, m, qcol:qcol + N], m == 0, m == 2, [wh_r, cqn_r[qg]], [pw_r])
                sq, sq_r = sqk[idx % 2]
                self.act(sq[:, :N], pn[:, :N], AF.Square, [pn_r], [sq_r])
                sp_, sp_r = sqps[idx % 2]
                self.act(sp_[0:64, :N], pp[0:64, :N], AF.Square, [pp_r], [sp_r])
                tb, tab_r = tabs[ti % 2]
                tcol = qcol if which == 0 else 2 * NT
                self.dma("sp", tb[0:64, :, :N], self.d_rope[:, :, tcol:tcol + N], [], [tab_r], lane=f"tab{ti % 2}")
                ti += 1
                qtab[idx] = (tb, tab_r)

            q_front(0)
            for idx, (qg, qcol, N, which) in enumerate(qgroups):
                gset = [[0, 1, 2, 7], [3, 4, 5, 6]][idx % 2]
                pn, pn_r = self.bank(gset[0])
                pp, pp_r = self.bank(gset[1])
                pw, pw_r = self.bank(gset[2])
                ssb, ssb_r = self.bank(gset[3])
                sq, sq_r = sqk[idx % 2]
                sp_, sp_r = sqps[idx % 2]
                if idx + 1 < len(qgroups):
                    q_front(idx + 1)
                self.mm(ssb[:, :N], self.ones, sq[:, :N], True, False, [sq_r, self.ones_r], [ssb_r])
                self.mm(ssb[:, :N], self.ones, sp_[:, :N], False, True, [sp_r, self.ones_r], [ssb_r])
                rs, rs_r = rsb[idx % 2]
                self.rsqrt_b(ssb, N, 1.0 / 192, rs, rs_r, ssb_r)
                self.stt(QTn[:, qcol:qcol + N], pn[:, :N], gqn, rs[:, :N], ALU.mult, ALU.mult, [pn_r, self.vecs_r, rs_r], [Q_r[qg]])
                tb, tab_r = qtab.pop(idx)
                cosb, sinb = tb[:, 0, :], tb[:, 1, :]
                tA, tA_r = self.tmps[0]
                tB, tB_r = self.tmps[1]
                tC, tC_r = self.tmps[2]
                self.stt(tA[0:64, :N], pp[0:64, :N], self.vcol(f"gqp{j_}")[0:64, :], cosb[0:64, :N], ALU.mult, ALU.mult,
                         [pp_r, self.vecs_r, tab_r], [tA_r])
                self.stt(tB[0:64, :N], pw[0:64, :N], self.vcol(f"gqs{j_}")[0:64, :], sinb[0:64, :N], ALU.mult, ALU.mult,
                         [pw_r, self.vecs_r, tab_r], [tB_r])
                self.tt(tC[0:64, :N], tA[0:64, :N], tB[0:64, :N], ALU.add, [tA_r, tB_r], [tC_r], eng="pool")
                self.tt(QTp[0:64, qcol:qcol + N], tC[0:64, :N], rs[0:64, :N], ALU.mult, [tC_r, rs_r], [Q_r[qg]], eng="pool")
            for (qg, qcol, N, which) in qgroups:
                tiles = list(range(34)) if which == 0 else [32, 33]
                po, po_r = self.bank(OB[oi % 2])
                pu, pu_r = self.bank(UB[oi % 2])
                oi += 1
                LOOK = 2
                nt_ = len(tiles)
                sbank = {}

                def emit_s(i):
                    nonlocal si
                    kt = tiles[i]
                    ps, ps_r = self.bank(SB[si % len(SB)])
                    si += 1
                    sbank[i] = (ps, ps_r)
                    kg = kt // 4
                    self.mm(ps[:, :N], KTn[:, kt * 128:(kt + 1) * 128], QTn[:, qcol:qcol + N], True, False, [KV_r[kg], Q_r[qg]], [ps_r])
                    if F_PAD:
                        self.mm(ps[:, :N], KTp[:, kt * 128:(kt + 1) * 128], QTp[:, qcol:qcol + N], False, True, [KP_r[kg], Q_r[qg]], [ps_r])
                    else:
                        self.mm(ps[:, :N], KTp[0:64, kt * 128:(kt + 1) * 128], QTp[0:64, qcol:qcol + N], False, True, [KP_r[kg], Q_r[qg]], [ps_r])

                for i in range(min(LOOK, nt_)):
                    emit_s(i)
                for i in range(nt_):
                    kt = tiles[i]
                    kg = kt // 4
                    ps, ps_r = sbank.pop(i)
                    pt, pt_r = PT[i % 4]
                    self.act(pt[:, :N], ps[:, :N], AF.Exp, [ps_r], [pt_r], scale=SCALE)
                    if i + LOOK < nt_:
                        emit_s(i + LOOK)
                    self.mm(po[:, :N], Vh[:, kt * 128:(kt + 1) * 128], pt[:, :N], i == 0, i == nt_ - 1, [VV_r[kg], pt_r], [po_r])
                    if not F_SUMS:
                        self.mm(pu[:, :N], self.ones, pt[:, :N], i == 0, i == nt_ - 1, [self.ones_r, pt_r], [pu_r])
                    else:
                        if i % 3 == 2:
                            self.mm(pu[:, :N], self.ones, pt[:, :N], i == 2, False, [self.ones_r, pt_r], [pu_r])
                        else:
                            acc, acc_r = accD
                            if i == 0:
                                self.copy("dve", acc[:, :N], pt[:, :N], [pt_r], [acc_r])
                            else:
                                self.tt(acc[:, :N], acc[:, :N], pt[:, :N], ALU.add, [acc_r, pt_r], [acc_r])
                if F_SUMS:
                    aD, aD_r = accD
                    hi, hi_r = sumhi
                    lo, lo_r = sumlo
                    self.copy("dve", hi[:, :N], aD[:, :N], [aD_r], [hi_r])
                    self.tt(lo[:, :N], aD[:, :N], hi[:, :N], ALU.subtract, [aD_r, hi_r], [lo_r])
                    self.mm(pu[:, :N], self.ones, hi[:, :N], nt_ <= 2, False, [self.ones_r, hi_r], [pu_r])
                    self.mm(pu[:, :N], self.ones, lo[:, :N], False, True, [self.ones_r, lo_r], [pu_r])
                rc, rc_r = rec
                if F_LN:
                    self.act(rc[:, :N], pu[:, :N], AF.Ln, [pu_r], [rc_r])
                    self.act(rc[:, :N], rc[:, :N], AF.Exp, [rc_r], [rc_r], scale=-1.0)
                else:
                    self.recip(rc[:, :N], pu[:, :N], [pu_r], [rc_r])
                self.tt(oTh[:, qcol:qcol + N], po[:, :N], rc[:, :N], ALU.mult, [po_r, rc_r], [oT_r[qg]])
            gi_ = 0
            for (qg, qcol, N, which) in qgroups:
                if which == 0:
                    dst = (lambda qg: (lambda k: XT[:, k, qg * 512:(qg + 1) * 512]))(qg)
                    dst_r = self.XTr[qg]
                else:
                    dst = lambda k: YT[:, k, :]
                    dst_r = self.YTr
                for dc in range(8):
                    pb, pb_r = self.bank(GB[gi_ % len(GB)])
                    gi_ += 1
                    self.mm(pb[:, :N], woh[:, dc * 128:(dc + 1) * 128], oTh[:, qcol:qcol + N], True, True, [wh_r, oT_r[qg]], [pb_r])
                    self.stt(dst(dc), pb[:, :N], self.gate("m", dc, which), dst(dc), ALU.mult, ALU.add,
                             [pb_r, self.mod_r, dst_r], [dst_r])

    def _modulate_multi(self, src, src_rs, N, which, kind, hT, hT_r, coff):
        A = self.am if kind == "m" else self.af
        shb = 0 if kind == "m" else 24
        ssb, ssb_r = self.bank(self.SSB)
        src_rs = list(src_rs)
        for k in range(KC):
            sq, sq_r = self.sqs[k % len(self.sqs)]
            self.act(sq[:, :N], src(k), AF.Square, src_rs, [sq_r])
            self.mm(ssb[:, :N], self.ones, sq[:, :N], k == 0, k == KC - 1, [sq_r, self.ones_r], [ssb_r])
        rs, rs_r = self.rs_buf
        self.rsqrt_b(ssb, N, 1.0 / D, rs, rs_r, ssb_r)
        for k in range(KC):
            tmp, tmp_r = self.tmps[k % len(self.tmps)]
            self.tt(tmp[:, :N], src(k), rs[:, :N], ALU.mult, src_rs + [rs_r], [tmp_r])
            self.act(hT[:, k, coff:coff + N], tmp[:, :N], AF.Identity, [tmp_r, self.mod_r], [hT_r],
                     scale=A[:, 2 * k + which:2 * k + which + 1],
                     bias=self.mod[:, 2 * (shb + k) + which:2 * (shb + k) + which + 1])

    def _emit(self):
        nc, S = self.nc, self.S
        sem = {}
        for e in ENGS:
            sem[("e", e)] = self.es.enter_context(nc.semaphore(f"s_{e}"))
        for l in S.lanes:
            sem[("l", l)] = self.es.enter_context(nc.semaphore(f"l_{l}"))
        by = {e: [op for op in S.ops if op.eng == e] for e in ENGS}
        block = self.es.enter_context(nc.Block())

        def body(ename):
            def f(eng):
                for op in by[ename]:
                    for key, val in op.waits:
                        eng.wait_ge(sem[key], val)
                    if op.fn is None:
                        continue
                    if op.lane is not None:
                        op.fn(eng, sem[("l", op.lane)])
                    else:
                        inst = op.fn(eng)
                        if op.sigkey is not None:
                            inst.then_inc(sem[("e", ename)], 1)
            return f

        block.tensor(body("pe"))
        block.scalar(body("act"))
        block.vector(body("dve"))
        block.gpsimd(body("pool"))
        block.sync(body("sp"))


def _cols(v, n):
    return np.ascontiguousarray(np.asarray(v, np.float32).reshape(n, 128).T)


def _dup(a):
    return np.repeat(a, 2, axis=1)


_SWAP = np.array([(((d // 16) ^ 1) * 16 + d % 16) for d in range(64)])

VOFF = {}
NV = 0


def _vec_layout():
    global NV
    off = 0

    def add(name, n):
        nonlocal off
        VOFF[name] = off
        off += n
    for i in range(4):
        add(f"adab{i}", 96)
        add(f"gmix{i}", 16)
        add(f"gffn{i}", 16)
    for j in range(2):
        add(f"gqa{j}", 3)
        add(f"gkva{j}", 2)
        for n in ("gqn", "gqp", "gqs", "gkn", "gkp", "gks"):
            add(f"{n}{j}", 1)
        add(f"lng{j}", 16)
        add(f"lnb{j}", 16)
    add("cvec", 16)
    NV = off


_vec_layout()


def _pad64(v):
    o = np.zeros((128, 1), np.float32)
    o[:64, 0] = v
    return o


def _build_vecs(inp, b):
    V = np.zeros((128, NV), np.float32)

    def put(name, a):
        V[:, VOFF[name]:VOFF[name] + a.shape[1]] = a
    for i in range(4):
        put(f"adab{i}", _dup(_cols(inp["ada_b"][i], 48)))
        put(f"gmix{i}", _dup(_cols(inp["norm_mix_g"][i], 8)))
        put(f"gffn{i}", _dup(_cols(inp["norm_ffn_g"][i], 8)))
    for j in range(2):
        put(f"gqa{j}", _cols(inp["mla_q_a_norm"][j], 3))
        put(f"gkva{j}", _cols(inp["mla_kv_a_norm"][j], 2))
        qn = np.asarray(inp["mla_q_norm"][j], np.float32)
        kn = np.asarray(inp["mla_k_norm"][j], np.float32)
        put(f"gqn{j}", qn[:128].reshape(128, 1))
        put(f"gqp{j}", _pad64(qn[128:]))
        put(f"gqs{j}", _pad64(qn[128:][_SWAP]))
        put(f"gkn{j}", kn[:128].reshape(128, 1))
        put(f"gkp{j}", _pad64(kn[128:]))
        put(f"gks{j}", _pad64(kn[128:][_SWAP]))
        put(f"lng{j}", _cols(inp["gm_ln_g"][j], 16))
        put(f"lnb{j}", _cols(inp["gm_ln_b"][j], 16))
    cv = np.stack([_cols(inp["c"][b], 8), _cols(inp["c_ctx"], 8)], axis=2).reshape(128, 16)
    put("cvec", cv)
    return V


def _rope_tables(half):
    pos_own = np.arange(half * NT, (half + 1) * NT)
    pos_par = np.arange((1 - half) * NT, (2 - half) * NT)
    pos = np.concatenate([pos_own, pos_par]).astype(np.float32)
    row = np.floor(pos / 64).astype(np.float32)
    col = (pos - row * 64).astype(np.float32)
    inv = (np.float32(10000.0) ** (-np.arange(0, 32, 2, dtype=np.float32) / np.float32(32))).astype(np.float32)
    ang_r = row[:, None] * inv[None, :]
    ang_c = col[:, None] * inv[None, :]
    ang = np.concatenate([ang_r, ang_r, ang_c, ang_c], axis=-1).astype(np.float32)
    cos = np.cos(ang).astype(np.float32)
    sin = np.sin(ang).astype(np.float32)
    sign = np.concatenate([-np.ones(16), np.ones(16), -np.ones(16), np.ones(16)]).astype(np.float32)
    sinS = sin * sign[None, :]
    cosT = np.concatenate([cos.T, np.ones((64, NCTX), np.float32)], axis=1)
    sinT = np.concatenate([sinS.T, np.zeros((64, NCTX), np.float32)], axis=1)
    return np.ascontiguousarray(np.stack([cosT, sinT], axis=1))


def _shared_weights(inp):
    f = lambda a: np.ascontiguousarray(np.asarray(a, np.float32))
    wkva = np.asarray(inp["mla_wkv_a"], np.float32)
    wkva2 = np.concatenate([wkva, wkva[:, :, 256:][:, :, _SWAP]], axis=2)
    wqb = np.asarray(inp["mla_wq_b"], np.float32).reshape(2, 384, 8, 192)
    wqb2 = np.concatenate([wqb, wqb[:, :, :, 128:][:, :, :, _SWAP]], axis=3).reshape(2, 384, 2048)
    bs = np.asarray(inp["gm_bs"], np.float32)
    bsb = np.broadcast_to(bs.reshape(2, 1, 1024), (2, 128, 1024))
    wsT = np.asarray(inp["gm_ws"], np.float32).transpose(0, 3, 1, 2).reshape(2, 128, 1024)
    return {
        "ident": np.eye(128, dtype=np.float32),
        "bsb": f(bsb), "wsT": f(wsT),
        "ada_w": f(inp["ada_w"]), "wq_a": f(inp["mla_wq_a"]), "wkv_a": f(wkva2), "wq_b": f(wqb2),
        "wkv_b": f(inp["mla_wkv_b"]), "wo": f(inp["mla_wo"]),
        "gm_w_in": f(inp["gm_w_in"]), "gm_w_out": f(inp["gm_w_out"]),
        "ffn_w1": f(inp["ffn_w1"]), "ffn_w2": f(inp["ffn_w2"]),
    }


_PROG_CACHE = {}


def _get_prog(layers, debug_stop=None):
    key = (tuple(layers), debug_stop)
    if key not in _PROG_CACHE:
        _PROG_CACHE[key] = Prog(list(layers), True, True, debug_stop)
    return _PROG_CACHE[key]


def run_layers(inp, x, y, layers, debug_stop=None, ncores=8):
    prog = _get_prog(layers, debug_stop)
    shared = _shared_weights(inp)
    in_maps = []
    for c in range(ncores):
        b, half = c // 2, c % 2
        rope = _rope_tables(half)
        m = dict(shared)
        m["x_own"] = np.ascontiguousarray(x[b, half * NT:(half + 1) * NT])
        m["x_par"] = np.ascontiguousarray(x[b, (1 - half) * NT:(2 - half) * NT])
        m["ctx"] = np.ascontiguousarray(y[b])
        m["vecs"] = _build_vecs(inp, b)
        m["rope"] = rope
        in_maps.append(m)
    res = run_bass_kernel_spmd(prog.nc, in_maps, core_ids=list(range(ncores)))
    xo = np.zeros_like(x)
    yo = np.zeros_like(y)
    if prog.dbg_names:
        run_layers.dbg = [{n: np.asarray(res.results[c][n]) for n in prog.dbg_names} for c in range(ncores)]
    for c in range(ncores):
        b, half = c // 2, c % 2
        xo[b, half * NT:(half + 1) * NT] = res.results[c]["out_x"]
        if half == 0:
            yo[b] = res.results[c]["out_y"]
    return xo, yo


def kernel(**inputs):
    inp = {k: np.asarray(v) for k, v in inputs.items()}
    x = np.ascontiguousarray(inp["x"], dtype=np.float32)
    y = np.ascontiguousarray(inp["ctx"], dtype=np.float32)
    x, y = run_layers(inp, x, y, (0, 1))
    x, y = run_layers(inp, x, y, (2, 3))
    return x.astype(np.float32)
```
